# Optimizing a Trainium2 kernel written in Bass

```python
import jax, jax.numpy as jnp
from jax import lax
import numpy as np

D_MODEL = 1024
BATCH = 2
SEQ = 16384
DEPTH = 2

HEAD_DIM = 64
RWKV_WIDTH = D_MODEL // 2
FOX_WIDTH = D_MODEL - RWKV_WIDTH
MIX_WIDTH = RWKV_WIDTH + FOX_WIDTH
RWKV_HEADS = RWKV_WIDTH // HEAD_DIM
FOX_HEADS = FOX_WIDTH // HEAD_DIM
DECAY_LORA = 64
ICLR_LORA = 64
Q_BLOCK = 128
LN_EPS = 1e-5
GN_EPS = 64e-5
DEEPNORM_ALPHA = (2 * DEPTH) ** 0.25
DEEPNORM_BETA = (8 * DEPTH) ** -0.25

RW_R0 = 0
RW_K0 = RW_R0 + RWKV_WIDTH
RW_V0 = RW_K0 + RWKV_WIDTH
RW_WD0 = RW_V0 + RWKV_WIDTH
RW_AD0 = RW_WD0 + DECAY_LORA
RW_END = RW_AD0 + ICLR_LORA
FX_Q0 = RW_END
FX_K0 = FX_Q0 + FOX_WIDTH
FX_V0 = FX_K0 + FOX_WIDTH
FX_F0 = FX_V0 + FOX_WIDTH
FX_END = FX_F0 + FOX_HEADS
GATE0 = FX_END
P_TOTAL = GATE0 + MIX_WIDTH

kernel_name = "hymba_rwkv7_fox_deepnorm_adaln"


def _layer_norm(x, g, b, eps=LN_EPS):
    x32 = x.astype(jnp.float32)
    mu = jnp.mean(x32, axis=-1, keepdims=True)
    var = jnp.mean(jnp.square(x32 - mu), axis=-1, keepdims=True)
    return ((x32 - mu) * lax.rsqrt(var + eps)).astype(x.dtype) * g + b


def _rwkv7_scan(r, decay, k, v, kk, kka):
    B, T, H, N = r.shape

    def step(S, inp):
        r_t, w_t, k_t, v_t, kk_t, b_t = inp
        sa = jnp.einsum('bhvk,bhk->bhv', S, kk_t)
        S = (S * w_t[:, :, None, :]
             - sa[..., None] * b_t[:, :, None, :]
             + v_t[..., None] * k_t[:, :, None, :])
        y = jnp.einsum('bhvk,bhk->bhv', S, r_t)
        return S, y

    S0 = jnp.zeros((B, H, N, N), jnp.float32)
    xs = (jnp.moveaxis(r, 1, 0), jnp.moveaxis(decay, 1, 0), jnp.moveaxis(k, 1, 0),
          jnp.moveaxis(v, 1, 0), jnp.moveaxis(kk, 1, 0), jnp.moveaxis(kka, 1, 0))
    _, ys = lax.scan(step, S0, xs)
    return jnp.moveaxis(ys, 0, 1)


def _fox_attention(q, k, v, log_f):
    B, T, H, D = q.shape
    qh = q.transpose(0, 2, 1, 3)
    kh = k.transpose(0, 2, 1, 3)
    vh = v.transpose(0, 2, 1, 3)
    cum = jnp.cumsum(log_f, axis=1).transpose(0, 2, 1)
    key_pos = jnp.arange(T)
    scale = D ** -0.5

    def block(i):
        start = i * Q_BLOCK
        qb = lax.dynamic_slice_in_dim(qh, start, Q_BLOCK, axis=2)
        cb = lax.dynamic_slice_in_dim(cum, start, Q_BLOCK, axis=2)
        s = (jnp.einsum('bhqd,bhkd->bhqk', qb, kh).astype(jnp.float32) * scale
             + cb[..., None] - cum[:, :, None, :])
        q_pos = start + jnp.arange(Q_BLOCK)
        s = jnp.where(q_pos[:, None] >= key_pos[None, :], s, -jnp.inf)
        p = jax.nn.softmax(s, axis=-1)
        return jnp.einsum('bhqk,bhkd->bhqd', p.astype(vh.dtype), vh)

    out = lax.map(block, jnp.arange(T // Q_BLOCK))
    return out.transpose(1, 0, 3, 2, 4).reshape(B, T, H * D)


def _hybrid_layer(x, c, w_ada, b_ada, w_in, rwkv_mix, w0, w_up, a0, a_up, k_k, k_a, r_k,
                  gn_g, gn_b, fox_bf, w_out, ln_g, ln_b):
    B, T, _ = x.shape
    mod = c @ w_ada + b_ada
    shift, scale, gate = jnp.split(mod, 3, axis=-1)
    h = x * (1.0 + scale[:, None, :]) + shift[:, None, :]

    proj = h @ w_in

    rw = proj[..., :RW_END]
    rw_prev = jnp.pad(rw, ((0, 0), (1, 0), (0, 0)))[:, :-1]
    rw = rw + (rw_prev - rw) * rwkv_mix
    hs = (B, T, RWKV_HEADS, HEAD_DIM)
    r = rw[..., RW_R0:RW_K0].astype(jnp.float32)
    k = rw[..., RW_K0:RW_V0].astype(jnp.float32)
    v = rw[..., RW_V0:RW_WD0].astype(jnp.float32)
    w_low = rw[..., RW_WD0:RW_AD0].astype(jnp.float32)
    a_low = rw[..., RW_AD0:RW_END].astype(jnp.float32)
    w_log = -jax.nn.softplus(-(w0 + jnp.tanh(w_low) @ w_up)) - 0.5
    decay = jnp.exp(-jnp.exp(w_log))
    a = jax.nn.sigmoid(a0 + a_low @ a_up)
    kk = (k * k_k).reshape(hs)
    kk = kk / jnp.maximum(jnp.sqrt(jnp.sum(kk * kk, axis=-1, keepdims=True)), 1e-12)
    k = k * (1.0 + (a - 1.0) * k_a)
    r_h, k_h, v_h, a_h = r.reshape(hs), k.reshape(hs), v.reshape(hs), a.reshape(hs)
    y_a = _rwkv7_scan(r_h, decay.reshape(hs), k_h, v_h, kk, kk * a_h)
    mu = jnp.mean(y_a, axis=-1, keepdims=True)
    var = jnp.mean(jnp.square(y_a - mu), axis=-1, keepdims=True)
    y_a = ((y_a - mu) * lax.rsqrt(var + GN_EPS)).reshape(B, T, RWKV_WIDTH) * gn_g + gn_b
    bonus = jnp.sum(r_h * k_h * r_k.reshape(RWKV_HEADS, HEAD_DIM), axis=-1, keepdims=True) * v_h
    y_a = (y_a + bonus.reshape(B, T, RWKV_WIDTH)).astype(x.dtype)

    fs = (B, T, FOX_HEADS, HEAD_DIM)
    q_f = proj[..., FX_Q0:FX_K0].reshape(fs)
    k_f = proj[..., FX_K0:FX_V0].reshape(fs)
    v_f = proj[..., FX_V0:FX_F0].reshape(fs)
    log_f = jax.nn.log_sigmoid(proj[..., FX_F0:FX_END].astype(jnp.float32) + fox_bf)
    y_b = _fox_attention(q_f, k_f, v_f, log_f).astype(x.dtype)

    g_path = jax.nn.silu(proj[..., GATE0:P_TOTAL])
    y = jnp.concatenate([y_a, y_b], axis=-1) * g_path
    out = y @ w_out

    return _layer_norm(DEEPNORM_ALPHA * x + (1.0 + gate[:, None, :]) * out, ln_g, ln_b)


def setup_inputs(seed: int = 0) -> dict:
    key = jax.random.key(seed)
    ks = jax.random.split(key, 24)
    f32 = jnp.float32
    nrm = lambda k, shape, s: (jax.random.normal(k, shape, f32) * s)

    x = jax.random.normal(ks[0], (BATCH, SEQ, D_MODEL), f32)
    c = jax.random.normal(ks[1], (BATCH, D_MODEL), f32)
    emb_ln_g = 1.0 + nrm(ks[2], (D_MODEL,), 0.02)
    emb_ln_b = nrm(ks[3], (D_MODEL,), 0.02)

    w_ada = nrm(ks[4], (DEPTH, D_MODEL, 3 * D_MODEL), 0.1 * D_MODEL ** -0.5)
    b_ada = nrm(ks[5], (DEPTH, 3 * D_MODEL), 0.01)
    col_scale = (jnp.ones((P_TOTAL,), f32)
                 .at[RW_V0:RW_WD0].set(DEEPNORM_BETA)
                 .at[FX_V0:FX_F0].set(DEEPNORM_BETA))
    w_in = nrm(ks[6], (DEPTH, D_MODEL, P_TOTAL), D_MODEL ** -0.5) * col_scale
    rwkv_mix = jax.random.uniform(ks[7], (DEPTH, RW_END), f32)
    w0 = jax.random.uniform(ks[8], (DEPTH, RWKV_WIDTH), f32, -6.0, -1.0)
    w_up = nrm(ks[9], (DEPTH, DECAY_LORA, RWKV_WIDTH), 0.1 * DECAY_LORA ** -0.5)
    a0 = nrm(ks[10], (DEPTH, RWKV_WIDTH), 0.1)
    a_up = nrm(ks[11], (DEPTH, ICLR_LORA, RWKV_WIDTH), 0.1 * ICLR_LORA ** -0.5)
    k_k = 0.85 + nrm(ks[12], (DEPTH, RWKV_WIDTH), 0.02)
    k_a = 1.0 + nrm(ks[13], (DEPTH, RWKV_WIDTH), 0.02)
    r_k = nrm(ks[14], (DEPTH, RWKV_WIDTH), 0.1)
    gn_g = 1.0 + nrm(ks[15], (DEPTH, RWKV_WIDTH), 0.02)
    gn_b = nrm(ks[16], (DEPTH, RWKV_WIDTH), 0.02)
    fox_bf = 3.0 + nrm(ks[17], (DEPTH, FOX_HEADS), 0.5)
    w_out = nrm(ks[18], (DEPTH, MIX_WIDTH, D_MODEL), MIX_WIDTH ** -0.5) * DEEPNORM_BETA
    ln_g = 1.0 + nrm(ks[19], (DEPTH, D_MODEL), 0.02)
    ln_b = nrm(ks[20], (DEPTH, D_MODEL), 0.02)
    return {"x": x, "c": c, "emb_ln_g": emb_ln_g, "emb_ln_b": emb_ln_b,
            "w_ada": w_ada, "b_ada": b_ada, "w_in": w_in, "rwkv_mix": rwkv_mix,
            "w0": w0, "w_up": w_up, "a0": a0, "a_up": a_up, "k_k": k_k, "k_a": k_a,
            "r_k": r_k, "gn_g": gn_g, "gn_b": gn_b, "fox_bf": fox_bf, "w_out": w_out,
            "ln_g": ln_g, "ln_b": ln_b}


def reference(x, c, emb_ln_g, emb_ln_b, w_ada, b_ada, w_in, rwkv_mix, w0, w_up, a0, a_up,
              k_k, k_a, r_k, gn_g, gn_b, fox_bf, w_out, ln_g, ln_b):
    h = _layer_norm(x, emb_ln_g, emb_ln_b)
    for l in range(DEPTH):
        h = _hybrid_layer(h, c, w_ada[l], b_ada[l], w_in[l], rwkv_mix[l], w0[l], w_up[l],
                          a0[l], a_up[l], k_k[l], k_a[l], r_k[l], gn_g[l], gn_b[l],
                          fox_bf[l], w_out[l], ln_g[l], ln_b[l])
    return h
```

```python
import contextlib
import numpy as np
import concourse.bass as bass
import concourse.mybir as mybir
from concourse.bass_utils import run_bass_kernel_spmd

F32 = mybir.dt.float32
BF16 = mybir.dt.bfloat16
AF = mybir.ActivationFunctionType
ALU = mybir.AluOpType
AX = mybir.AxisListType

SEM_EPOCH = 20000


class Prog:
    def __init__(self, nc):
        self.nc = nc
        self.es = contextlib.ExitStack()
        self.eng = {"pe": nc.tensor, "act": nc.scalar, "dve": nc.vector,
                    "pool": nc.gpsimd, "sp": nc.sync}
        self.cnt = {e: 0 for e in self.eng}
        self.epoch = {e: 0 for e in self.eng}
        self.esem = {}
        for e in self.eng:
            self.esem[e] = self._newsem(f"s_{e}_0")
        self.seen = {e: {} for e in self.eng}
        self.sems = {}
        self.dcnt = {}
        self.lastw = {}
        self.reads = {}
        self.n_ops = 0
        self.n_waits = 0
        self.tes = self.es
        self.scope_id = 0
        self.ccsem = self._newsem("s_cc")
        self.ccn = 0
        self.kalias = {}

    def _newsem(self, name):
        return self.es.enter_context(self.nc.semaphore(name))

    def sb(self, name, shape, dt):
        return self.tes.enter_context(self.nc.sbuf_tensor(f"s{self.scope_id}_{name}", list(shape), dt))

    def ps(self, name, shape, dt=F32):
        return self.tes.enter_context(self.nc.psum_tensor(f"s{self.scope_id}_{name}", list(shape), dt))

    def begin_scope(self):
        self.scope_id += 1
        self.tes = contextlib.ExitStack()

    def end_scope(self):
        self.barrier()
        self.tes.close()
        self.tes = self.es

    def barrier(self):
        toks = []
        for f in self.eng:
            if self.cnt[f] > 0:
                toks.append((self.esem[f], self.cnt[f], f))
        for name, sem in self.sems.items():
            if self.dcnt[name] > 0:
                toks.append((sem, self.dcnt[name], "dma"))
        for e in self.eng:
            need = {id(sm): (sm, v) for (sm, v, o) in toks if o != e}
            self._emit_waits(e, need)
        keep = {k: t for k, t in self.lastw.items() if t[2] == "cc"}
        self.lastw.clear()
        self.reads.clear()
        self.lastw.update(keep)

    def cc(self, kind, in_ap, out_ap, rg, reads=(), writes=()):
        need = self._need("pool", reads, writes)
        self._emit_waits("pool", need)
        ins = self.nc.gpsimd.collective_compute(kind, ALU.bypass, replica_groups=rg, ins=[in_ap], outs=[out_ap])
        self.ccn += 1
        ins.then_inc(self.ccsem, 1)
        tok = (self.ccsem, self.ccn, "cc")
        self._commit("cc", tok, reads, writes)
        self.n_ops += 1
        return tok

    def close(self):
        if self.ccn > 0:
            self._emit_waits("pool", {id(self.ccsem): (self.ccsem, self.ccn)})
        self.es.close()

    def _need(self, e, reads, writes):
        need = {}

        def add(tok, kind):
            if tok is None:
                return
            sem, val, owner = tok
            if owner == e:
                if e in ("pe", "sp"):
                    return
                if kind == "war":
                    return
            k = id(sem)
            if k not in need or need[k][1] < val:
                need[k] = (sem, val)

        for r in reads:
            add(self.lastw.get(r), "raw")
        for w in writes:
            add(self.lastw.get(w), "waw")
            for tok in self.reads.get(w, {}).values():
                add(tok, "war")
        return need

    def _emit_waits(self, e, need):
        eng = self.eng[e]
        seen = self.seen[e]
        for k, (sem, val) in need.items():
            if seen.get(k, 0) >= val:
                continue
            eng.wait_ge(sem, val)
            seen[k] = val
            self.n_waits += 1

    def _commit(self, e, tok, reads, writes):
        for w in writes:
            self.lastw[w] = tok
            self.reads[w] = {}
        for r in reads:
            self.reads.setdefault(r, {})[(e, id(tok[0]))] = tok

    @staticmethod
    def _is_psum(k):
        return isinstance(k, str) and (k.startswith("ps") or "pp" in k)

    def alias(self, a, b):
        self.kalias[a] = b

    def _ka(self, keys):
        if not self.kalias:
            return keys
        return [self.kalias.get(k, k) for k in keys]

    def op(self, e, fn, reads=(), writes=()):
        reads, writes = self._ka(reads), self._ka(writes)
        px = [k for k in reads if self._is_psum(k)]
        if px:
            reads = [k for k in reads if not self._is_psum(k)]
            writes = list(writes) + px
        need = self._need(e, reads, writes)
        self._emit_waits(e, need)
        if self.cnt[e] >= SEM_EPOCH:
            self.epoch[e] += 1
            self.esem[e] = self._newsem(f"s_{e}_{self.epoch[e]}")
            self.cnt[e] = 0
        ins = fn()
        self.cnt[e] += 1
        ins.then_inc(self.esem[e], 1)
        tok = (self.esem[e], self.cnt[e], e)
        self._commit(e, tok, reads, writes)
        self.n_ops += 1
        return tok

    def dma(self, q, out, in_, dsem, reads=(), writes=(), **kw):
        reads, writes = self._ka(reads), self._ka(writes)
        need = self._need(q, reads, writes)
        self._emit_waits(q, need)
        if dsem not in self.sems:
            self.sems[dsem] = self._newsem("d_" + dsem)
            self.dcnt[dsem] = 0
        sem = self.sems[dsem]
        ins = self.eng[q].dma_start(out=out, in_=in_, **kw)
        self.dcnt[dsem] += 16
        ins.then_inc(sem, 16)
        tok = (sem, self.dcnt[dsem], "dma")
        self._commit("dma", tok, reads, writes)
        self.n_ops += 1
        return tok

    def wait_all(self, e, keys):
        need = {}
        for k in keys:
            tok = self.lastw.get(k)
            if tok is None:
                continue
            kk = id(tok[0])
            if kk not in need or need[kk][1] < tok[1]:
                need[kk] = (tok[0], tok[1])
        self._emit_waits(e, need)

import numpy as np

NCOL = 1154
C_R, C_K, C_V, C_WA, C_GA, C_FQ, C_FK = 0, 128, 256, 384, 512, 640, 768
C_GB0, C_GB1, C_FV, C_F = 896, 960, 1024, 1152
K_ID, K_BO, K_SU, K_UI, K_SL, K_RS, K_ONE = 0, 128, 256, 384, 512, 640, 1152
NCONST = 1280
GN_EPS = 64e-5
RW_DT = BF16
RW_ROUNDS = 22.0
OVERLAP_PROJ = False
NEU_BANKS = 3
FIN_OVERLAP = False
DRAIN_OVERLAP = False
DRAIN_MIN = 100


def make_consts():
    c = np.zeros((128, NCONST), np.float32)
    i = np.arange(128)
    c[:, K_ID:K_ID + 128] = np.eye(128)
    c[:, K_BO:K_BO + 128] = (i[:, None] // 64 == i[None, :] // 64)
    c[:, K_SU:K_SU + 128] = (i[:, None] < i[None, :])
    c[:, K_UI:K_UI + 128] = (i[:, None] <= i[None, :])
    c[:, K_SL:K_SL + 128] = (i[:, None] > i[None, :])
    rs = np.ones(512, np.float32)
    rs[::128] = 0
    c[:, K_RS:K_RS + 512] = rs[None, :]
    c[:, K_ONE:K_ONE + 128] = 1.0
    return c


def interleave(gens, weights):
    gens = list(gens)
    acc = [0.0] * len(gens)
    alive = [True] * len(gens)
    while any(alive):
        for i, g in enumerate(gens):
            if not alive[i]:
                continue
            acc[i] += weights[i]
            while acc[i] >= 1.0 and alive[i]:
                acc[i] -= 1.0
                try:
                    next(g)
                except StopIteration:
                    alive[i] = False


def emit_mix(P, nc, T, hT, wcore, vecs, up, fbf, consts, yT, do_rwkv=True, do_fox=True,
             h_src=None, y_dst=None, after_tile=None, h_keys=None):
    NT = T // 512
    NB = T // 128
    V = nc.vector
    A = nc.scalar
    G = nc.gpsimd
    PE = nc.tensor

    cst = P.sb("cst", [128, NCONST], F32)
    vec = P.sb("vec", [128, 20], F32)
    fbh = P.sb("fbh", [2, 1], F32)
    upt = P.sb("upt", [128, 128], F32)
    fbt = P.sb("fbt", [2, 1], F32)
    P.dma("sp", cst[:], consts[:, :], "ld_cst", writes=["cst"])
    P.dma("sp", vec[:, 0:11], vecs[:, :], "ld_vec", writes=["vec"])
    P.dma("sp", upt[:], up[:, :], "ld_upt", writes=["upt"])
    P.dma("sp", fbt[:], fbf[:, :], "ld_fbt", writes=["fbt"])
    ident = cst[:, K_ID:K_ID + 128]
    bones = cst[:, K_BO:K_BO + 128]
    mask2 = cst[:, K_SU:K_SU + 256]
    msl = cst[:, K_SL:K_SL + 128]
    rsm = cst[:, K_RS:K_RS + 512]
    ones = cst[:, K_ONE:K_ONE + 128]
    P.op("dve", lambda: V.tensor_scalar(out=vec[:, 11:15], in0=vec[:, 0:4], scalar1=-1.0, scalar2=1.0,
                                        op0=ALU.mult, op1=ALU.add), reads=["vec"], writes=["vec"])
    P.op("dve", lambda: V.tensor_scalar(out=vec[:, 15:16], in0=vec[:, 7:8], scalar1=-1.0, scalar2=1.0,
                                        op0=ALU.mult, op1=ALU.add), reads=["vec"], writes=["vec"])
    P.op("dve", lambda: V.tensor_scalar(out=vec[:, 16:18], in0=vec[:, 4:6], scalar1=0.5, scalar2=None,
                                        op0=ALU.mult), reads=["vec"], writes=["vec"])
    P.op("dve", lambda: V.tensor_scalar(out=fbh[:], in0=fbt[:], scalar1=0.5, scalar2=None, op0=ALU.mult),
         reads=["fbt"], writes=["fbh"])
    uib = P.sb("uib", [128, 128], BF16)
    idb = P.sb("idb", [128, 128], BF16)
    P.op("dve", lambda: V.tensor_copy(out=idb[:], in_=cst[:, K_ID:K_ID + 128]), reads=["cst"], writes=["idb"])
    P.op("dve", lambda: V.tensor_copy(out=uib[:], in_=cst[:, K_UI:K_UI + 128]), reads=["cst"], writes=["uib"])

    wsb = P.sb("wsb", [128, 8, NCOL], BF16)
    wst = [P.sb(f"wst{i}", [128, 8, 64], F32) for i in range(2)]
    wv = wcore.rearrange("(c p) n -> p c n", p=128)
    pieces = [(s, min(64, NCOL - s)) for s in range(0, NCOL, 64)]
    for pi, (s, n) in enumerate(pieces):
        b = pi % 2
        P.dma("sp", wst[b][:, :, 0:n], wv[:, :, s:s + n], f"ld_w{b}", writes=[f"wst{b}"])
        eng = "act" if pi % 2 == 0 else "dve"
        if eng == "act":
            P.op("act", lambda: A.copy(out=wsb[:, :, s:s + n], in_=wst[b][:, :, 0:n]),
                 reads=[f"wst{b}"], writes=["wsb"])
        else:
            P.op("dve", lambda: V.tensor_copy(out=wsb[:, :, s:s + n], in_=wst[b][:, :, 0:n]),
                 reads=[f"wst{b}"], writes=["wsb"])

    hTt = [P.sb(f"hT{i}", [128, 8, 512], BF16) for i in range(2)]
    if h_src is None:
        hv = hT.rearrange("(c p) t -> p c t", p=128)
        h_src = lambda j: hv[:, :, j * 512:(j + 1) * 512]
    if y_dst is None:
        y_dst = lambda r0, r1, j: yT[r0:r1, j * 512:(j + 1) * 512]
    NSB = 3 if OVERLAP_PROJ else (6 - NEU_BANKS)
    NPT = 4
    LOOK = 2 if OVERLAP_PROJ else (5 - NEU_BANKS)
    psS = [P.ps(f"psS{i}", [128, 512]) for i in range(NSB)]
    psO = P.ps("psO", [128, 512])
    psA = [P.ps(f"psA{i}", [128, 512]) for i in range(NEU_BANKS)]
    ppb = P.ps("ppb", [128, 512]) if OVERLAP_PROJ else None
    psY = P.ps("psY", [128, 512])
    ppc = [0]

    def nextpp():
        if OVERLAP_PROJ:
            return ppb, "ppb"
        i = ppc[0] % 2
        ppc[0] += 1
        return psA[i], f"psA{i}"

    KT = P.sb("KT", [128, T], BF16)
    Vaug = P.sb("Vaug", [128, NB, 2, 65], BF16)
    ckres = P.sb("ckres", [128, NB, 2], F32)
    P.op("pool", lambda: G.memset(Vaug[:, :, :, 64:65], 1.0), writes=["Vaug_ones"])
    QT = [[P.sb(f"QT{i}_{h}", [128, 512], BF16) for h in range(2)] for i in range(2)]
    for i in range(2):
        for h in range(2):
            P.op("pool", lambda: G.memset(QT[i][h][:], 0.0), writes=[f"QT{i}"])
    sgb = [[P.sb(f"sgb{i}_{h}", [64, 512], F32) for h in range(2)] for i in range(2)]
    rrow = P.sb("rrow", [128, 512], F32)
    lft = P.sb("lft", [2, 512], F32)
    lf = lft[:, :]
    onesrow = rrow[0:2, :]
    cum = [P.sb(f"cum{i}", [2, 512], F32) for i in range(2)]
    P.op("dve", lambda: V.memset(cum[1][:], 0.0), writes=["cum1"])
    i2 = P.sb("i2", [2, 2], F32)
    basebc = [P.sb(f"basebc{i}", [128, 2], F32) for i in range(2)]
    biasj = [P.sb(f"biasj{i}", [128, NB, 2], F32) for i in range(2)]
    PT = [P.sb(f"PT{i}", [128, 512], BF16) for i in range(NPT)]
    t1 = P.sb("t1", [64, 512], F32)
    osb = P.sb("osb", [65, 512], F32)
    ybo = [P.sb(f"ybo{i}", [64, 512], BF16) for i in range(2)]

    raw = {g: P.sb(f"raw_{g}", [128, 513], F32) for g in "rkvw"}
    for g in "rkvw":
        P.op("pool", lambda: G.memset(raw[g][:, 0:1], 0.0), writes=[f"raw_{g}c"])
    tmp = P.sb("tmp", [128, 512], F32)
    sh = {g: P.sb(f"sh_{g}", [128, 512], F32) for g in "rkvw"}
    a_t = P.sb("a_t", [128, 512], F32)
    ld = P.sb("ld", [128, 512], F32)
    cl = P.sb("cl", [128, 512], F32)
    einc = P.sb("einc", [128, 512], F32)
    eneg = P.sb("eneg", [128, 512], F32)
    kk = P.sb("kk", [128, 512], F32)
    kmod = tmp
    P.alias("kmod", "tmp")
    w1 = P.sb("w1", [128, 512], F32)
    w2 = P.sb("w2", [128, 512], F32)
    KRz = [P.sb(f"KRz{i}", [128, 4, 2, 128], RW_DT) for i in range(2)]
    for i in range(2):
        P.op("pool", lambda: G.memset(KRz[i][:], 0.0), writes=["KR"])
    Bt = P.sb("Bt", [128, 512], RW_DT)
    Kt = P.sb("Kt", [128, 512], RW_DT)
    bonus = ld
    P.alias("bonus", "ld")
    sgaL = [P.sb(f"sga{i}", [128, 512], F32) for i in range(2)]
    Vtok = P.sb("Vtok", [128, 4, 128], RW_DT)
    Btok = P.sb("Btok", [128, 4, 128], RW_DT)
    Ktok = P.sb("Ktok", [128, 4, 128], RW_DT)
    ST = P.sb("ST", [128, 64], F32)
    STw = P.sb("STw", [128, 64], F32)
    STb = P.sb("STb", [128, 128], RW_DT)
    P.op("dve", lambda: V.memset(ST[:], 0.0), writes=["ST0", "ST1"])
    P.op("dve", lambda: V.memset(STb[:], 0.0), writes=["STb0", "STb1"])
    AT = [[P.sb(f"AT{i}_{k}", [128, 256], RW_DT) for k in range(4)] for i in range(2)]
    AK = [[P.sb(f"AK{i}_{k}", [128, 256], RW_DT) for k in range(4)] for i in range(2)]
    XL = [[P.sb(f"XL{i}_{k}", [128, 384], RW_DT) for k in range(4)] for i in range(2)]
    Tm = [[XL[i][k][:, 256:384] for k in range(4)] for i in range(2)]
    TmF = Tm
    Zs = P.sb("Zs", [128, 128], RW_DT)
    Us = P.sb("Us", [128, 128], RW_DT)
    P.op("pool", lambda: G.memset(Us[:], 0.0), writes=["Us0", "Us1"])
    P.op("pool", lambda: G.memset(Zs[:], 0.0), writes=["Zs0", "Zs1"])
    ysb = cl
    for k_ in ("ysb", "ysb0", "ysb1"):
        P.alias(k_, "cl")
    yao = [P.sb(f"yao{i}", [128, 512], BF16) for i in range(2)]

    def proj_group(j, col, M, rhs_tile, rhs_key):
        ps, pk = nextpp()
        for c in range(8):
            P.op("pe", lambda: PE.matmul(ps[0:M, :], lhsT=wsb[:, c, col:col + M], rhs=rhs_tile[:, c, :],
                                         start=(c == 0), stop=(c == 7)),
                 reads=["wsb", rhs_key], writes=[pk])
        return ps, pk

    def load_h(j):
        b = j % 2
        P.dma("sp", hTt[b][:], h_src(j), f"ld_h{b}", reads=(h_keys(j) if h_keys else ()), writes=[f"hT{b}"])

    def tile_proj(j):
        b = j % 2
        ht, hk = hTt[b], f"hT{b}"
        sga = sgaL[b]
        sgak = f"sga{b}"
        if do_rwkv:
            for gi, (g, col) in enumerate((("r", C_R), ("k", C_K), ("v", C_V), ("w", C_WA))):
                ps, pk = proj_group(j, col, 128, ht, hk)
                rw = raw[g]
                if j > 0:
                    P.op("act", lambda: A.copy(out=rw[:, 0:1], in_=rw[:, 512:513]), reads=[f"raw_{g}"],
                         writes=[f"raw_{g}c"])
                P.op("dve", lambda: V.tensor_copy(out=rw[:, 1:513], in_=ps[:, :]), reads=[pk], writes=[f"raw_{g}"])
                P.op("act", lambda: A.activation(out=tmp[:], in_=ps[:, :], func=AF.Identity,
                                                 scale=vec[:, 11 + gi:12 + gi]),
                     reads=[pk, "vec"], writes=["tmp"])
                P.op("dve", lambda: V.scalar_tensor_tensor(out=sh[g][:], in0=rw[:, 0:512], scalar=vec[:, gi:gi + 1],
                                                           in1=tmp[:], op0=ALU.mult, op1=ALU.add),
                     reads=[f"raw_{g}", f"raw_{g}c", "tmp", "vec"], writes=[f"sh_{g}"])
                yield
            ps, pk = proj_group(j, C_GA, 128, ht, hk)
            P.op("act", lambda: A.activation(out=sga[:], in_=ps[:, :], func=AF.Tanh, scale=0.5), reads=[pk],
                 writes=[sgak])
            P.op("dve", lambda: V.scalar_tensor_tensor(out=sga[:], in0=sga[:], scalar=1.0, in1=ps[:, :],
                                                       op0=ALU.add, op1=ALU.mult), reads=[pk, "sga"], writes=[sgak])
        if do_fox:
            ps, pk = proj_group(j, C_FQ, 128, ht, hk)
            for h in range(2):
                hp_ = slice(64 * h, 64 * h + 64)
                P.op("act", lambda: A.activation(out=QT[b][h][hp_, :], in_=ps[hp_, :], func=AF.Copy, scale=0.125),
                     reads=[pk], writes=[f"QT{b}"])
            yield
            ps, pk = proj_group(j, C_FK, 128, ht, hk)
            P.op("dve", lambda: V.tensor_copy(out=KT[:, j * 512:(j + 1) * 512], in_=ps[:, :]),
                 reads=[pk], writes=[f"KT{j}"])
            yield
            for h, col in ((0, C_GB0), (1, C_GB1)):
                ps, pk = proj_group(j, col, 64, ht, hk)
                P.op("act", lambda: A.activation(out=sgb[b][h][:], in_=ps[0:64, :], func=AF.Tanh, scale=0.5),
                     reads=[pk], writes=[f"sgb{b}_{h}"])
                P.op("dve", lambda: V.scalar_tensor_tensor(out=sgb[b][h][:], in0=sgb[b][h][:], scalar=1.0,
                                                           in1=ps[0:64, :], op0=ALU.add, op1=ALU.mult),
                     reads=[pk, f"sgb{b}_{h}"], writes=[f"sgb{b}_{h}"])
                yield
            ps, pk = nextpp()
            for q in range(4):
                for c in range(8):
                    P.op("pe", lambda: PE.matmul(ps[:, q * 128:(q + 1) * 128], lhsT=ht[:, c, q * 128:(q + 1) * 128],
                                                 rhs=wsb[:, c, C_FV:C_FV + 128], start=(c == 0), stop=(c == 7)),
                         reads=["wsb", hk], writes=[pk])
            P.op("dve", lambda: V.tensor_copy(
                out=Vaug[:, 4 * j:4 * j + 4, :, 0:64],
                in_=ps[:, :].rearrange("p (q h d) -> p q h d", q=4, h=2)),
                reads=[pk], writes=[f"V{j}"])
            yield
            ps, pk = proj_group(j, C_F, 2, ht, hk)
            P.op("act", lambda: A.activation(out=lf, in_=ps[0:2, :], func=AF.Tanh, bias=fbh[:, 0:1], scale=0.5),
                 reads=[pk, "fbh"], writes=["lf"])
            P.op("dve", lambda: V.tensor_scalar(out=lf, in0=lf, scalar1=0.5, scalar2=0.5, op0=ALU.mult, op1=ALU.add),
                 reads=["lf"], writes=["lf"])
            P.op("act", lambda: A.activation(out=lf, in_=lf, func=AF.Ln), reads=["lf"], writes=["lf"])
            cprev, cb = cum[1 - b], cum[b]
            P.op("dve", lambda: V.tensor_tensor_scan(out=cb[:], data0=onesrow, data1=lf,
                                                     initial=cprev[:, 511:512], op0=ALU.mult, op1=ALU.add),
                 reads=["lf", f"cum{1-b}", "onesrow"], writes=[f"cum{b}"])
            yield
            ps, pk = nextpp()
            for q in range(4):
                P.op("pe", lambda: PE.matmul(ps[:, 2 * q:2 * q + 2], lhsT=cb[0:2, q * 128:(q + 1) * 128],
                                             rhs=cst[0:2, K_ID:K_ID + 2], start=True, stop=True),
                     reads=[f"cum{b}", "cst"], writes=[pk])
            P.op("dve", lambda: V.tensor_scalar(out=i2[:], in0=cst[0:2, K_ID:K_ID + 2], scalar1=cb[:, 255:256],
                                                scalar2=None, op0=ALU.mult),
                 reads=[f"cum{b}", "cst"], writes=["i2"])
            P.op("pe", lambda: PE.matmul(ps[:, 8:10], lhsT=ones[0:2, :], rhs=i2[:, :], start=True, stop=True),
                 reads=["i2", "cst"], writes=[pk])
            P.op("dve", lambda: V.tensor_copy(out=ckres[:, 4 * j:4 * j + 4, :],
                                              in_=ps[:, 0:8].rearrange("p (q h) -> p q h", q=4)),
                 reads=[pk], writes=[f"ck{j}"])
            P.op("dve", lambda: V.tensor_copy(out=basebc[b][:], in_=ps[:, 8:10]), reads=[pk], writes=[f"basebc{b}"])
        yield

    P.op("dve", lambda: V.memset(onesrow, 1.0), writes=["onesrow"])

    def attn_tile(j):
        b = j % 2
        nb = 4 * j + 4
        steps = [(h, kb) for h in range(2) for kb in range(nb)]
        for h in range(2):
            bj = biasj[h]
            P.op("dve", lambda: V.tensor_scalar(out=bj[:, 0:nb, h], in0=ckres[:, 0:nb, h], scalar1=-1.0,
                                                scalar2=basebc[b][:, h:h + 1], op0=ALU.mult, op1=ALU.add),
                 reads=[f"ck{jj}" for jj in range(j + 1)] + [f"basebc{b}"], writes=[f"biasj{h}"])
        yield

        def q0_of(kb):
            m = kb - 4 * j
            return 128 * m if m > 0 else 0

        def emit_S(i):
            h, kb = steps[i]
            hp = slice(64 * h, 64 * h + 64)
            q0 = q0_of(kb)
            si = i % NSB
            P.op("pe", lambda: PE.matmul(psS[si][:, q0:512], lhsT=KT[:, kb * 128:(kb + 1) * 128],
                                         rhs=QT[b][h][:, q0:512], start=True, stop=True),
                 reads=[f"KT{kb // 4}", f"QT{b}"], writes=[f"psS{si}"])

        for i0 in range(min(LOOK, len(steps))):
            emit_S(i0)
        for i, (h, kb) in enumerate(steps):
            if i + LOOK < len(steps):
                emit_S(i + LOOK)
            bj = biasj[h]
            m = kb - 4 * j
            q0 = q0_of(kb)
            si = i % NSB
            pi = i % NPT
            P.op("act", lambda: A.activation(out=PT[pi][:, q0:512], in_=psS[si][:, q0:512], func=AF.Exp,
                                             bias=bj[:, kb, h:h + 1], scale=1.0),
                 reads=[f"psS{si}", f"biasj{h}"], writes=[f"PT{pi}"])
            if m >= 0:
                P.op("pool", lambda: G.tensor_tensor(out=PT[pi][:, q0:q0 + 128], in0=PT[pi][:, q0:q0 + 128],
                                                     in1=uib[:], op=ALU.mult),
                     reads=[f"PT{pi}", "uib"], writes=[f"PT{pi}"])
            P.op("pe", lambda: PE.matmul(psO[0:65, q0:512], lhsT=Vaug[:, kb, h, :], rhs=PT[pi][:, q0:512],
                                         start=(kb == 0), stop=(kb == nb - 1)),
                 reads=[f"PT{pi}", f"V{kb // 4}", "Vaug_ones"], writes=["psO"])
            if kb == nb - 1:
                P.op("dve", lambda: V.tensor_copy(out=osb[0:65, :], in_=psO[0:65, :]), reads=["psO"], writes=["osb"])
                P.op("dve", lambda: V.reciprocal(out=rrow[64:65, :], in_=osb[64:65, :]), reads=["osb"],
                     writes=["rrow"])
                pq, pqk = psS[si], f"psS{si}"
                P.op("pe", lambda: PE.matmul(pq[0:64, :], lhsT=ones[64:65, 0:64], rhs=rrow[64:65, :],
                                             start=True, stop=True),
                     reads=["rrow", "cst"], writes=[pqk])
                P.op("dve", lambda: V.scalar_tensor_tensor(out=t1[:], in0=pq[0:64, :], scalar=0.5, in1=sgb[b][h][:],
                                                           op0=ALU.mult, op1=ALU.mult),
                     reads=[pqk, f"sgb{b}_{h}"], writes=["t1"])
                P.op("pool", lambda: G.tensor_tensor(out=ybo[h][:], in0=osb[0:64, :], in1=t1[:], op=ALU.mult),
                     reads=["osb", "t1"], writes=[f"ybo{h}"])
                P.dma("sp", y_dst(128 + 64 * h, 192 + 64 * h, j), ybo[h][:], f"st_yb{h}",
                      reads=[f"ybo{h}"], writes=[f"yTb{h}", f"ytile{j}_b{h}"])
            yield

    def SLK(h, i0=0, i1=4):
        return [f"psA{h}"]

    psC = psY
    CK = ["psC"]

    progress = [0, 0]

    def rwkv_prep(j):
        for h_ in range(2):
            for c_ in range(4):
                ndone[h_][c_] = False
        pa, pak = psA[0], SLK(0)
        pb, pbk = psA[1], SLK(1)
        th = tmp
        P.op("act", lambda: A.activation(out=th[0:64, :], in_=sh["w"][0:64, :], func=AF.Tanh),
             reads=["sh_w"], writes=["tmp"])
        P.op("pe", lambda: PE.matmul(pa[:, :], lhsT=upt[0:64, :], rhs=th[0:64, :], start=True, stop=True),
             reads=["upt", "tmp"], writes=pak)
        P.op("pe", lambda: PE.matmul(pb[:, :], lhsT=upt[64:128, :], rhs=sh["w"][64:128, :], start=True, stop=True),
             reads=["upt", "sh_w"], writes=pbk)
        P.op("act", lambda: A.activation(out=ld[:], in_=pa[:, :], func=AF.Tanh, bias=vec[:, 16:17], scale=0.5),
             reads=pak + ["vec"], writes=["ld"])
        P.op("act", lambda: A.activation(out=a_t[:], in_=pb[:, :], func=AF.Tanh, bias=vec[:, 17:18], scale=0.5),
             reads=pbk + ["vec"], writes=["a_t"])
        yield
        P.op("dve", lambda: V.tensor_scalar(out=ld[:], in0=ld[:], scalar1=1.0, scalar2=-0.5 * float(np.exp(-0.5)),
                                            op0=ALU.add, op1=ALU.mult), reads=["ld"], writes=["ld"])
        P.op("pool", lambda: G.tensor_scalar(out=a_t[:], in0=a_t[:], scalar1=0.5, scalar2=0.5, op0=ALU.mult,
                                             op1=ALU.add), reads=["a_t"], writes=["a_t"])
        P.op("dve", lambda: V.tensor_tensor_scan(out=cl[:], data0=rsm, data1=ld[:], initial=0.0,
                                                 op0=ALU.mult, op1=ALU.add),
             reads=["ld", "cst"], writes=["cl"])
        P.op("act", lambda: A.activation(out=einc[:], in_=cl[:], func=AF.Exp), reads=["cl"], writes=["einc"])
        P.op("act", lambda: A.activation(out=eneg[:], in_=cl[:], func=AF.Exp, scale=-1.0),
             reads=["cl"], writes=["eneg"])
        P.op("dve", lambda: V.tensor_tensor(out=w1[:], in0=cl[:], in1=ld[:], op=ALU.subtract),
             reads=["cl", "ld"], writes=["w1"])
        P.op("act", lambda: A.activation(out=w1[:], in_=w1[:], func=AF.Exp), reads=["w1"], writes=["w1"])
        yield
        P.op("dve", lambda: V.tensor_scalar(out=kk[:], in0=sh["k"][:], scalar1=vec[:, 6:7], scalar2=None,
                                            op0=ALU.mult), reads=["sh_k", "vec"], writes=["kk"])
        P.op("dve", lambda: V.tensor_tensor(out=w2[:], in0=kk[:], in1=kk[:], op=ALU.mult),
             reads=["kk"], writes=["w2"])
        P.op("pe", lambda: PE.matmul(pa[:, :], lhsT=bones, rhs=w2[:], start=True, stop=True),
             reads=["cst", "w2"], writes=pak)
        P.op("dve", lambda: V.tensor_scalar(out=w2[:], in0=pa[:, :], scalar1=1e-24, scalar2=None, op0=ALU.max),
             reads=pak, writes=["w2"])
        P.op("act", lambda: A.activation(out=w2[:], in_=w2[:], func=AF.Ln), reads=["w2"], writes=["w2"])
        P.op("act", lambda: A.activation(out=w2[:], in_=w2[:], func=AF.Exp, scale=-0.5), reads=["w2"], writes=["w2"])
        P.op("dve", lambda: V.tensor_tensor(out=kk[:], in0=kk[:], in1=w2[:], op=ALU.mult),
             reads=["kk", "w2"], writes=["kk"])
        yield
        P.op("dve", lambda: V.tensor_scalar(out=w2[:], in0=a_t[:], scalar1=vec[:, 7:8], scalar2=vec[:, 15:16],
                                            op0=ALU.mult, op1=ALU.add), reads=["a_t", "vec"], writes=["w2"])
        P.op("dve", lambda: V.tensor_tensor(out=kmod[:], in0=sh["k"][:], in1=w2[:], op=ALU.mult),
             reads=["sh_k", "w2"], writes=["kmod"])
        P.op("dve", lambda: V.tensor_tensor(out=w2[:], in0=kk[:], in1=a_t[:], op=ALU.mult),
             reads=["kk", "a_t"], writes=["w2"])
        P.op("dve", lambda: V.tensor_tensor(out=Bt[:], in0=w2[:], in1=eneg[:], op=ALU.mult),
             reads=["w2", "eneg"], writes=["Bt"])
        P.op("pool", lambda: G.tensor_tensor(out=Kt[:], in0=kmod[:], in1=eneg[:], op=ALU.mult),
             reads=["kmod", "eneg"], writes=["Kt"])
        for hh in range(2):
            hq = slice(64 * hh, 64 * hh + 64)
            P.op("dve", lambda: V.tensor_tensor(out=KRz[hh][hq, :, 0, :],
                                                in0=kk[hq, :].rearrange("p (c t) -> p c t", c=4),
                                                in1=w1[hq, :].rearrange("p (c t) -> p c t", c=4), op=ALU.mult),
                 reads=["kk", "w1"], writes=["KR"])
            P.op("pool", lambda: G.tensor_tensor(out=KRz[hh][hq, :, 1, :],
                                                 in0=sh["r"][hq, :].rearrange("p (c t) -> p c t", c=4),
                                                 in1=einc[hq, :].rearrange("p (c t) -> p c t", c=4), op=ALU.mult),
                 reads=["sh_r", "einc"], writes=["KR"])
        yield
        P.op("dve", lambda: V.scalar_tensor_tensor(out=w2[:], in0=sh["r"][:], scalar=vec[:, 8:9], in1=kmod[:],
                                                   op0=ALU.mult, op1=ALU.mult),
             reads=["sh_r", "vec", "kmod"], writes=["w2"])
        P.op("pe", lambda: PE.matmul(pb[:, :], lhsT=bones, rhs=w2[:], start=True, stop=True),
             reads=["cst", "w2"], writes=pbk)
        P.op("dve", lambda: V.tensor_tensor(out=bonus[:], in0=pb[:, :], in1=sh["v"][:], op=ALU.mult),
             reads=pbk + ["sh_v"], writes=["bonus"])
        yield
        for (src, skey, dst, dkey, pq, pqk) in ((sh["v"], "sh_v", Vtok, "Vtok", pa, pak),
                                                (Bt, "Bt", Btok, "Btok", pb, pbk),
                                                (Kt, "Kt", Ktok, "Ktok", pa, pak)):
            isb = (src.dtype == BF16)
            pqv = pq[:, :].bitcast(BF16) if isb else pq[:, :]
            for c in range(4):
                P.op("pe", lambda: PE.transpose(pqv[:, c * 128:(c + 1) * 128], src[:, c * 128:(c + 1) * 128],
                                                idb[:] if isb else ident),
                     reads=[skey, "cst", "idb"], writes=pqk)
            P.op("dve", lambda: V.tensor_copy(out=dst[:].rearrange("p c t -> p (c t)"), in_=pqv[:, 0:512]),
                 reads=pqk, writes=[dkey])
            yield

    ndone = [[False] * 4, [False] * 4]

    def neumann(j, h, c):
        bi = h if NEU_BANKS == 2 else (2 * c + h) % NEU_BANKS
        pa = psA[bi]
        pk = SLK(bi)
        cs = slice(c * 128, (c + 1) * 128)
        at, atk = AT[h][c], f"AT{h}_{c}"
        ak, akk = AK[h][c], f"AK{h}_{c}"
        X, Xk = XL[h][c], f"TmF{h}_{c}"
        Lc, Ltc, Tc = X[:, 0:128], X[:, 128:256], X[:, 256:384]
        krc = KRz[h][:, c, :, :].rearrange("p a t -> p (a t)")
        P.op("pe", lambda: PE.matmul(pa[:, 0:256], lhsT=Bt[:, cs], rhs=krc, start=True, stop=True),
             reads=["Bt", "KR"], writes=pk)
        P.op("pe", lambda: PE.matmul(pa[:, 256:384], lhsT=KRz[h][:, c, 0, :], rhs=Bt[:, cs], start=True, stop=True),
             reads=["Bt", "KR"], writes=pk)
        P.op("dve", lambda: V.tensor_tensor(out=at[:], in0=pa[:, 0:256], in1=mask2, op=ALU.mult),
             reads=pk + ["cst"], writes=[atk])
        P.op("dve", lambda: V.tensor_tensor(out=Lc, in0=pa[:, 256:384], in1=msl, op=ALU.mult),
             reads=pk + ["cst"], writes=[Xk])
        P.op("pool", lambda: G.tensor_tensor(out=Tc, in0=ident, in1=at[:, 0:128], op=ALU.subtract),
             reads=[atk, "cst", Xk], writes=[Xk])
        yield
        P.op("pe", lambda: PE.matmul(pa[:, 0:128], lhsT=at[:, 0:128], rhs=Lc, start=True, stop=True),
             reads=[atk, Xk], writes=pk)
        P.op("pe", lambda: PE.matmul(pa[:, 128:256], lhsT=Lc, rhs=at[:, 0:128], start=True, stop=True),
             reads=[atk, Xk], writes=pk)
        P.op("pe", lambda: PE.matmul(pa[:, 256:512], lhsT=Kt[:, cs], rhs=krc, start=True, stop=True),
             reads=["Kt", "KR"], writes=pk)
        P.op("dve", lambda: V.tensor_copy(out=X[:, 0:256], in_=pa[:, 0:256]), reads=pk + [Xk], writes=[Xk])
        P.op("dve", lambda: V.tensor_tensor(out=ak[:], in0=pa[:, 256:512], in1=mask2, op=ALU.mult),
             reads=pk + ["cst"], writes=[akk])
        yield
        for k in range(1, 7):
            P.op("pe", lambda: PE.matmul(pa[:, 256:384], lhsT=Lc, rhs=Tc, start=True, stop=False),
                 reads=[Xk], writes=pk)
            P.op("pe", lambda: PE.matmul(pa[:, 256:384], lhsT=idb[:], rhs=Tc, start=False, stop=True),
                 reads=[Xk, "idb"], writes=pk)
            if k < 6:
                P.op("pe", lambda: PE.matmul(pa[:, 0:128], lhsT=Ltc, rhs=Lc, start=True, stop=True),
                     reads=[Xk], writes=pk)
            if k < 5:
                P.op("pe", lambda: PE.matmul(pa[:, 128:256], lhsT=Lc, rhs=Ltc, start=True, stop=True),
                     reads=[Xk], writes=pk)
            lo = 0 if k < 6 else 256
            P.op("dve", lambda: V.tensor_copy(out=X[:, lo:384], in_=pa[:, lo:384]), reads=pk + [Xk], writes=[Xk])
            yield
        ndone[h][c] = True

    def chain(j, h):
        pa = psC
        hp = slice(64 * h, 64 * h + 64)
        hc = slice(64 * h, 64 * h + 64)
        o0 = 256 * h
        stk, stbk, stwk, zk, uk = f"ST{h}", f"STb{h}", f"STw{h}", f"Zs{h}", f"Us{h}"
        for c in range(4):
            while not ndone[h][c]:
                yield
            par = c
            cs = slice(c * 128, (c + 1) * 128)
            wc = einc[hp, c * 128 + 127:c * 128 + 128]
            P.op("pe", lambda: PE.matmul(pa[:, o0:o0 + 64], lhsT=KRz[h][:, c, 0, :], rhs=STb[:, 0:64], start=True, stop=False),
                 reads=["KR", stbk], writes=CK)
            P.op("pe", lambda: PE.matmul(pa[:, o0:o0 + 64], lhsT=AK[h][par][:, 0:128], rhs=Vtok[:, c, hc],
                                         start=False, stop=True),
                 reads=[f"AK{h}_{par}", "Vtok"], writes=CK)
            P.op("dve", lambda: V.tensor_copy(out=Zs[:, hc], in_=pa[:, o0:o0 + 64]), reads=CK, writes=[zk])
            P.op("pool", lambda: G.tensor_scalar(out=STw[hp, :], in0=ST[hp, :], scalar1=wc, scalar2=None,
                                                 op0=ALU.mult), reads=[stk, "einc"], writes=[stwk])
            yield
            P.op("pe", lambda: PE.matmul(pa[:, o0 + 64:o0 + 128], lhsT=TmF[h][par], rhs=Zs[:, hc], start=True, stop=True),
                 reads=[f"TmF{h}_{par}", zk], writes=CK)
            P.op("dve", lambda: V.tensor_scalar(out=Us[:, hc], in0=pa[:, o0 + 64:o0 + 128], scalar1=-1.0,
                                                scalar2=None, op0=ALU.mult), reads=CK, writes=[uk])
            yield
            yo = pa[:, o0 + 128:o0 + 256]
            P.op("pe", lambda: PE.matmul(yo, lhsT=STb[:, :], rhs=KRz[h][:, c, 1, :], start=True, stop=False),
                 reads=[stbk, "KR"], writes=CK)
            P.op("pe", lambda: PE.matmul(yo, lhsT=Us[:, :], rhs=AT[h][par][:, 128:256], start=False, stop=False),
                 reads=[uk, f"AT{h}_{par}"], writes=CK)
            P.op("pe", lambda: PE.matmul(yo, lhsT=Vtok[:, c, :], rhs=AK[h][par][:, 128:256], start=False,
                                         stop=True), reads=["Vtok", f"AK{h}_{par}"], writes=CK)
            P.op("dve", lambda: V.tensor_copy(out=ysb[hp, cs], in_=pa[hp, o0 + 128:o0 + 256]), reads=CK,
                 writes=[f"ysb{h}"])
            P.op("pe", lambda: PE.matmul(pa[:, o0:o0 + 64], lhsT=Btok[:, c, :], rhs=Us[:, hc], start=True, stop=False),
                 reads=["Btok", uk], writes=CK)
            P.op("pe", lambda: PE.matmul(pa[:, o0:o0 + 64], lhsT=Ktok[:, c, :], rhs=Vtok[:, c, hc], start=False, stop=True),
                 reads=["Ktok", "Vtok"], writes=CK)
            P.op("dve", lambda: V.scalar_tensor_tensor(out=ST[hp, :], in0=pa[hp, o0:o0 + 64], scalar=wc, in1=STw[hp, :],
                                                       op0=ALU.mult, op1=ALU.add),
                 reads=CK + ["einc", stwk], writes=[stk])
            P.op("pool", lambda: G.tensor_copy(out=STb[hp, 0:64], in_=ST[hp, :]), reads=[stk], writes=[stbk])
            P.op("dve", lambda: V.tensor_copy(out=STb[hp, 64:128], in_=ST[hp, :]), reads=[stk], writes=[stbk])
            yield

    def rwkv_fin(j):
        b = j % 2
        pa, pak = psA[0], SLK(0)
        pb, pbk = psA[1], SLK(1)
        P.op("pool", lambda: G.tensor_tensor(out=w2[:], in0=ysb[:], in1=ysb[:], op=ALU.mult),
             reads=["ysb0", "ysb1", "ysb"], writes=["w2"])
        yield
        P.op("pe", lambda: PE.matmul(pa[:, :], lhsT=bones, rhs=ysb[:], start=True, stop=True),
             reads=["cst", "ysb0", "ysb1", "ysb"], writes=pak)
        P.op("pe", lambda: PE.matmul(pb[:, :], lhsT=bones, rhs=w2[:], start=True, stop=True),
             reads=["cst", "w2"], writes=pbk)
        P.op("act", lambda: A.activation(out=w1[:], in_=pa[:, :], func=AF.Square, scale=1.0 / 64), reads=pak,
             writes=["w1"])
        P.op("dve", lambda: V.scalar_tensor_tensor(out=ysb[:], in0=pa[:, :], scalar=-1.0 / 64, in1=ysb[:],
                                                   op0=ALU.mult, op1=ALU.add), reads=pak + ["ysb", "ysb0", "ysb1"],
             writes=["ysb", "ysb0", "ysb1"])
        P.op("dve", lambda: V.scalar_tensor_tensor(out=w1[:], in0=pb[:, :], scalar=1.0 / 64, in1=w1[:],
                                                   op0=ALU.mult, op1=ALU.subtract), reads=pbk + ["w1"], writes=["w1"])
        yield
        P.op("dve", lambda: V.tensor_scalar(out=w1[:], in0=w1[:], scalar1=GN_EPS, scalar2=None, op0=ALU.add),
             reads=["w1"], writes=["w1"])
        P.op("act", lambda: A.activation(out=w1[:], in_=w1[:], func=AF.Ln), reads=["w1"], writes=["w1"])
        P.op("act", lambda: A.activation(out=w1[:], in_=w1[:], func=AF.Exp, scale=-0.5), reads=["w1"], writes=["w1"])
        yield
        P.op("pool", lambda: G.tensor_tensor(out=ysb[:], in0=ysb[:], in1=w1[:], op=ALU.mult),
             reads=["ysb", "w1"], writes=["ysb"])
        P.op("dve", lambda: V.tensor_scalar(out=ysb[:], in0=ysb[:], scalar1=vec[:, 9:10], scalar2=vec[:, 10:11],
                                            op0=ALU.mult, op1=ALU.add), reads=["ysb", "vec"], writes=["ysb"])
        yield
        P.op("pool", lambda: G.tensor_tensor(out=ysb[:], in0=ysb[:], in1=bonus[:], op=ALU.add),
             reads=["ysb", "bonus"], writes=["ysb"])
        P.op("dve", lambda: V.scalar_tensor_tensor(out=yao[b][:], in0=ysb[:], scalar=0.5, in1=sgaL[b][:], op0=ALU.mult,
                                                   op1=ALU.mult), reads=["ysb", f"sga{b}"], writes=[f"yao{b}"])
        P.dma("sp", y_dst(0, 128, j), yao[b][:], f"st_ya{b}", reads=[f"yao{b}"], writes=[f"yTa{b}", f"ytile{j}_a"])
        yield

    def run_stage(prims, bg, ratio):
        alive = list(prims)
        acc = 0.0
        while alive:
            for g in list(alive):
                try:
                    next(g)
                except StopIteration:
                    alive.remove(g)
            if bg[0] is not None:
                acc += ratio
                while acc >= 1.0 and bg[0] is not None:
                    acc -= 1.0
                    try:
                        next(bg[0])
                    except StopIteration:
                        bg[0] = None

    def run_tile(j, nxt):
        bg = [attn_tile(j) if do_fox else None]
        n_attn = 2 * (4 * j + 4) + 1
        drain = bool(nxt) and DRAIN_OVERLAP and n_attn >= DRAIN_MIN
        r = n_attn / (RW_ROUNDS + (14.0 if drain else 0.0))
        if do_rwkv:
            run_stage([rwkv_prep(j)], bg, r)
            run_stage([neumann(j, h_, c_) for c_ in range(4) for h_ in range(2)] + [chain(j, 0), chain(j, 1)] + ([nxt] if (nxt and OVERLAP_PROJ) else []), bg, r)
            run_stage([rwkv_fin(j)] + ([nxt] if (nxt and FIN_OVERLAP) else []), bg, r)
            if drain:
                run_stage([nxt], bg, r)
        elif nxt:
            run_stage([nxt], bg, r)
        while bg[0] is not None:
            try:
                next(bg[0])
            except StopIteration:
                bg[0] = None

    load_h(0)
    if NT > 1:
        load_h(1)
    for _ in tile_proj(0):
        pass
    for j in range(NT):
        nxt = tile_proj(j + 1) if j + 1 < NT else None
        if j + 2 < NT:
            load_h(j + 2)
        if OVERLAP_PROJ:
            run_tile(j, nxt)
        else:
            run_tile(j, nxt if (FIN_OVERLAP or DRAIN_OVERLAP) else None)
            if nxt:
                for _ in nxt:
                    pass
        if after_tile is not None:
            after_tile(j)
    P.wait_all("sp", ["yTa0", "yTa1", "yTb0", "yTb1"])


def build_mix(T, **kw):
    nc = bass.Bass("TRN2", target_bir_lowering=False)
    hT = nc.dram_tensor("hT", [1024, T], BF16, kind="ExternalInput").ap()
    wcore = nc.dram_tensor("wcore", [1024, NCOL], F32, kind="ExternalInput").ap()
    vecs = nc.dram_tensor("vecs", [128, 11], F32, kind="ExternalInput").ap()
    up = nc.dram_tensor("up", [128, 128], F32, kind="ExternalInput").ap()
    fbf = nc.dram_tensor("fbf", [2, 1], F32, kind="ExternalInput").ap()
    consts = nc.dram_tensor("consts", [128, NCONST], F32, kind="ExternalInput").ap()
    yT = nc.dram_tensor("yT", [256, T], BF16, kind="ExternalOutput").ap()
    P = Prog(nc)
    emit_mix(P, nc, T, hT, wcore, vecs, up, fbf, consts, yT, **kw)
    print("mix ops", P.n_ops, "waits", P.n_waits)
    P.close()
    return nc

import numpy as np

LN_EPS = 1e-5
ALPHA = float(4 ** 0.25)
K_ID, K_ONE = 0, 1152


def emit_tok(P, nc, NTOK, consts, c_in, *, x_in=None, embg=None, embb=None,
             yT=None, xprev=None, wout=None, wada_g=None, bada_g=None, lng=None, lnb=None,
             wada_f=None, bada_f=None, x_out=None, hT_out=None, pfx="t",
             y_src=None, y_keys=(), h_dst=None, after_h=None):
    do_back = (yT is not None) or (y_src is not None)
    do_front = (hT_out is not None) or (h_dst is not None)
    if do_back and y_src is None:
        y_src = lambda s: yT.rearrange("(c p) t -> p c t", p=128)[:, :, s * 512:(s + 1) * 512]
    if do_front and h_dst is None:
        h_dst = lambda s: hT_out.rearrange("(c p) t -> p c t", p=128)[:, :, s * 512:(s + 1) * 512]
    NS = NTOK // 512
    V, A, G, PE = nc.vector, nc.scalar, nc.gpsimd, nc.tensor
    k = lambda s: pfx + s

    cst = P.sb(k("cst"), [128, 1280], F32)
    P.dma("sp", cst[:], consts[:, :], k("ld_cst"), writes=[k("cst")])
    ident = cst[:, K_ID:K_ID + 128]
    ones = cst[:, K_ONE:K_ONE + 128]
    c_sb = P.sb(k("c_sb"), [128, 8], F32)
    P.dma("sp", c_sb[:], c_in.rearrange("(c p) -> p c", p=128), k("ld_c"), writes=[k("c_sb")],
          allow_slow_non_contiguous=True)
    cbc = P.sb(k("cbc"), [128, 8, 128], F32)
    for c in range(8):
        P.op("dve", lambda: V.tensor_scalar(out=cbc[:, c, :], in0=ones, scalar1=c_sb[:, c:c + 1], scalar2=None,
                                            op0=ALU.mult), reads=[k("cst"), k("c_sb")], writes=[k("cbc")])
    pp = [P.ps(k(f"pp{i}"), [128, 512]) for i in range(4)]
    ppc = [0]

    def nextpp():
        i = ppc[0] % 4
        ppc[0] += 1
        return pp[i], k(f"pp{i}")

    wad = [P.sb(k(f"wad{i}"), [128, 8, 512], F32) for i in range(2)]
    wadc = [0]

    def mod_bc(wada, bada, N, name):
        mt = P.sb(k(name), [128, N], F32)
        P.dma("sp", mt[:], bada.partition_broadcast(128), k("ld_" + name), writes=[k(name)])
        wv = wada.rearrange("(c p) n -> p c n", p=128)
        for n0 in range(0, N, 512):
            b = wadc[0] % 2
            wadc[0] += 1
            P.dma("sp", wad[b][:], wv[:, :, n0:n0 + 512], k(f"ld_wad{b}"), writes=[k(f"wad{b}")])
            ps, pk = nextpp()
            for c in range(8):
                P.op("pe", lambda: PE.matmul(ps[:, :], lhsT=cbc[:, c, :], rhs=wad[b][:, c, :], start=(c == 0),
                                             stop=(c == 7)), reads=[k("cbc"), k(f"wad{b}")], writes=[pk])
            P.op("dve", lambda: V.tensor_tensor(out=mt[:, n0:n0 + 512], in0=ps[:, :], in1=mt[:, n0:n0 + 512],
                                                op=ALU.add), reads=[pk, k(name)], writes=[k(name)])
        return mt

    def bc_load(vec_ap, name):
        t = P.sb(k(name), [128, 1024], F32)
        P.dma("sp", t[:], vec_ap.partition_broadcast(128), k("ld_" + name), writes=[k(name)])
        return t

    if do_back:
        mg = mod_bc(wada_g, bada_g, 1024, "mg")
        P.op("dve", lambda: V.tensor_scalar(out=mg[:], in0=mg[:], scalar1=1.0, scalar2=None, op0=ALU.add),
             reads=[k("mg")], writes=[k("mg")])
        wo = P.sb(k("wo"), [128, 8, 1024], BF16)
        wov = wout.rearrange("(c p) n -> p c n", p=128)
        for pi, n0 in enumerate(range(0, 1024, 512)):
            b = wadc[0] % 2
            wadc[0] += 1
            P.dma("sp", wad[b][:], wov[:, :, n0:n0 + 512], k(f"ld_wad{b}"), writes=[k(f"wad{b}")])
            for c in range(8):
                e = "dve" if c % 2 == 0 else "pool"
                eng = V if c % 2 == 0 else G
                P.op(e, lambda: eng.tensor_tensor(out=wo[:, c, n0:n0 + 512], in0=wad[b][:, c, :],
                                                  in1=mg[:, n0:n0 + 512], op=ALU.mult),
                     reads=[k(f"wad{b}"), k("mg")], writes=[k("wo")])
        g_bc = bc_load(lng, "lng")
        b_bc = bc_load(lnb, "lnb")
    else:
        g_bc = bc_load(embg, "embg")
        b_bc = bc_load(embb, "embb")
    if do_front:
        mf = mod_bc(wada_f, bada_f, 2048, "mf")
        P.op("dve", lambda: V.tensor_scalar(out=mf[:, 1024:2048], in0=mf[:, 1024:2048], scalar1=1.0, scalar2=None,
                                            op0=ALU.add), reads=[k("mf")], writes=[k("mf")])
        fm = P.sb(k("fm"), [128, 16], F32)
        for q in range(4):
            ps, pk = nextpp()
            for u in range(4):
                cc = q * 4 + u
                P.op("pe", lambda: PE.transpose(ps[:, u * 128:(u + 1) * 128], mf[:, cc * 128:(cc + 1) * 128], ident),
                     reads=[k("mf"), k("cst")], writes=[pk])
            P.op("dve", lambda: V.tensor_copy(out=fm[:, q * 4:q * 4 + 4],
                                              in_=ps[:, :].rearrange("p (c t) -> p c t", t=128)[:, :, 0]),
                 reads=[pk], writes=[k("fm")])

    xt = [P.sb(k(f"xt{i}"), [128, 4, 1024], F32) for i in range(2)]
    yt = [P.sb(k(f"yt{i}"), [128, 8, 512], BF16) for i in range(2)] if do_back else None
    ht = [P.sb(k(f"ht{i}"), [128, 8, 512], BF16) for i in range(2)] if do_front else None
    xsrc = xprev if do_back else x_in

    def load(s):
        b = s % 2
        P.dma("sp", xt[b][:], xsrc[s * 512:(s + 1) * 512, :].rearrange("(u p) d -> p u d", p=128),
              k(f"ld_x{b}"), writes=[k(f"xt{b}")] + [k(f"xt{b}_{u}") for u in range(4)])
        if do_back:
            P.dma("sp", yt[b][:], y_src(s), k(f"ld_y{b}"), reads=y_keys, writes=[k(f"yt{b}")])

    stats4 = [P.sb(k(f"stats4_{i}"), [128, 4, 2, 6], F32) for i in range(2)]
    mv4 = [P.sb(k(f"mv4_{i}"), [128, 4, 2], F32) for i in range(2)]
    rs4 = [P.sb(k(f"rs4_{i}"), [128, 4], F32) for i in range(2)]
    nb4 = [P.sb(k(f"nb4_{i}"), [128, 4], F32) for i in range(2)]
    load(0)
    for s in range(NS):
        b = s % 2
        if s + 1 < NS:
            load(s + 1)
        xk = k(f"xt{b}")
        xku = [k(f"xt{b}_{u}") for u in range(4)]
        for u in range(4):
            xs = xt[b][:, u, :]
            if do_back:
                for n in range(2):
                    ps, pk = nextpp()
                    for c in range(8):
                        P.op("pe", lambda: PE.matmul(ps[:, :], lhsT=yt[b][:, c, u * 128:(u + 1) * 128],
                                                     rhs=wo[:, c, n * 512:(n + 1) * 512], start=(c == 0),
                                                     stop=(c == 7)), reads=[k(f"yt{b}"), k("wo")], writes=[pk])
                    P.op("dve", lambda: V.scalar_tensor_tensor(out=xs[:, n * 512:(n + 1) * 512],
                                                               in0=xs[:, n * 512:(n + 1) * 512], scalar=ALPHA,
                                                               in1=ps[:, :], op0=ALU.mult, op1=ALU.add),
                         reads=[xk, xku[u], pk], writes=[xku[u]])
            for n in range(2):
                P.op("dve", lambda: V.bn_stats(out=stats4[b][:, u, n, :], in_=xs[:, n * 512:(n + 1) * 512]),
                     reads=[xk, xku[u]], writes=[k(f"st4_{b}_{u}")])
            P.op("dve", lambda: V.bn_aggr(out=mv4[b][:, u, :], in_=stats4[b][:, u, :, :].rearrange("p a b -> p (a b)")),
                 reads=[k(f"st4_{b}_{u}")], writes=[k(f"mv4_{b}")])
        P.op("dve", lambda: V.tensor_scalar(out=rs4[b][:], in0=mv4[b][:, :, 1], scalar1=LN_EPS, scalar2=None,
                                            op0=ALU.add), reads=[k(f"mv4_{b}")], writes=[k(f"rs4_{b}")])
        P.op("act", lambda: A.activation(out=rs4[b][:], in_=rs4[b][:], func=AF.Sqrt), reads=[k(f"rs4_{b}")],
             writes=[k(f"rs4_{b}")])
        P.op("dve", lambda: V.reciprocal(out=rs4[b][:], in_=rs4[b][:]), reads=[k(f"rs4_{b}")], writes=[k(f"rs4_{b}")])
        P.op("dve", lambda: V.scalar_tensor_tensor(out=nb4[b][:], in0=mv4[b][:, :, 0], scalar=-1.0, in1=rs4[b][:],
                                                   op0=ALU.mult, op1=ALU.mult),
             reads=[k(f"mv4_{b}"), k(f"rs4_{b}")], writes=[k(f"nb4_{b}")])
        for u in range(4):
            xs = xt[b][:, u, :]
            P.op("act", lambda: A.activation(out=xs, in_=xs, func=AF.Identity, scale=rs4[b][:, u:u + 1],
                                             bias=nb4[b][:, u:u + 1]),
                 reads=[xk, xku[u], k(f"rs4_{b}"), k(f"nb4_{b}")], writes=[xku[u]])
            P.op("dve", lambda: V.tensor_tensor(out=xs, in0=xs, in1=g_bc[:], op=ALU.mult),
                 reads=[xku[u], k("lng"), k("embg")], writes=[xku[u]])
            P.op("pool", lambda: G.tensor_tensor(out=xs, in0=xs, in1=b_bc[:], op=ALU.add),
                 reads=[xku[u], k("lnb"), k("embb")], writes=[xku[u]])
        if x_out is not None:
            P.dma("sp", x_out[s * 512:(s + 1) * 512, :].rearrange("(u p) d -> p u d", p=128), xt[b][:],
                  k(f"st_x{b}"), reads=[xk] + xku, writes=[k(f"xo{b}")])
        if do_front:
            hk = k(f"ht{b}")
            for c in range(8):
                ps, pk = nextpp()
                for u in range(4):
                    P.op("pe", lambda: PE.transpose(ps[:, u * 128:(u + 1) * 128], xt[b][:, u, c * 128:(c + 1) * 128],
                                                    ident), reads=[xk, xku[u], k("cst")], writes=[pk])
                P.op("act", lambda: A.activation(out=ht[b][:, c, :], in_=ps[:, :], func=AF.Identity,
                                                 scale=fm[:, 8 + c:9 + c], bias=fm[:, c:c + 1]),
                     reads=[pk, k("fm")], writes=[hk])
            P.dma("sp", h_dst(s), ht[b][:], k(f"st_h{b}"), reads=[hk], writes=[k(f"ho{b}"), f"htile{s}"])
            if after_h is not None:
                after_h(s)
    P.wait_all("sp", [k("xo0"), k("xo1"), k("ho0"), k("ho1")])


def build_tok(NTOK, mode):
    nc = bass.Bass("TRN2", target_bir_lowering=False)
    dt = lambda n, s, d=F32, kind="ExternalInput": nc.dram_tensor(n, s, d, kind=kind).ap()
    consts = dt("consts", [128, 1280])
    c_in = dt("c", [1024])
    kw = {}
    if mode == "pre":
        kw.update(x_in=dt("x", [NTOK, 1024]), embg=dt("embg", [1024]), embb=dt("embb", [1024]))
    else:
        kw.update(yT=dt("yT", [1024, NTOK], BF16), xprev=dt("xprev", [NTOK, 1024]), wout=dt("wout", [1024, 1024]),
                  wada_g=dt("wada_g", [1024, 1024]), bada_g=dt("bada_g", [1024]), lng=dt("lng", [1024]),
                  lnb=dt("lnb", [1024]))
    if mode != "post":
        kw.update(wada_f=dt("wada_f", [1024, 2048]), bada_f=dt("bada_f", [2048]),
                  hT_out=dt("hT", [1024, NTOK], BF16, kind="ExternalOutput"))
    kw.update(x_out=dt("xo", [NTOK, 1024], kind="ExternalOutput"))
    P = Prog(nc)
    emit_tok(P, nc, NTOK, consts, c_in, **kw)
    print("tok", mode, "ops", P.n_ops, "waits", P.n_waits)
    P.close()
    return nc

import numpy as np

D = 1024
RWW = 512
RW_R0, RW_K0, RW_V0, RW_WD0, RW_AD0, RW_END = 0, 512, 1024, 1536, 1600, 1664
FX_Q0, FX_K0, FX_V0, FX_F0, FX_END = 1664, 2176, 2688, 3200, 3208
GATE0 = 3208


def core_cols(g):
    ch = np.arange(128 * g, 128 * g + 128)
    l64 = np.arange(64)
    return np.concatenate([
        RW_R0 + ch, RW_K0 + ch, RW_V0 + ch, RW_WD0 + l64, RW_AD0 + l64, GATE0 + ch,
        FX_Q0 + ch, FX_K0 + ch,
        GATE0 + 512 + 128 * g + l64, GATE0 + 512 + 128 * g + 64 + l64,
        FX_V0 + ch, FX_F0 + np.array([2 * g, 2 * g + 1])])


def pack_core(inp, l, g):
    ch = np.arange(128 * g, 128 * g + 128)
    l64 = np.arange(64)
    wcore = np.ascontiguousarray(inp["w_in"][l][:, core_cols(g)])
    mix = inp["rwkv_mix"][l]
    vecs = np.stack([
        mix[RW_R0 + ch], mix[RW_K0 + ch], mix[RW_V0 + ch],
        np.concatenate([mix[RW_WD0 + l64], mix[RW_AD0 + l64]]),
        inp["w0"][l][ch], inp["a0"][l][ch], inp["k_k"][l][ch], inp["k_a"][l][ch],
        inp["r_k"][l][ch], inp["gn_g"][l][ch], inp["gn_b"][l][ch]], axis=1).astype(np.float32)
    up = np.concatenate([inp["w_up"][l][:, ch], inp["a_up"][l][:, ch]], axis=0).astype(np.float32)
    fbf = inp["fox_bf"][l][[2 * g, 2 * g + 1]][:, None].astype(np.float32)
    return dict(wcore=wcore, vecs=np.ascontiguousarray(vecs), up=np.ascontiguousarray(up),
                fbf=np.ascontiguousarray(fbf))


T_SEQ = 16384
NTOK = 4096
RG = [[0, 1, 2, 3], [4, 5, 6, 7]]
_NC_CACHE = {}


def build_fused(T=T_SEQ, NT=NTOK):
    nc = bass.Bass("TRN2", target_bir_lowering=False)
    dt = lambda n, s, d=F32, kind="ExternalInput": nc.dram_tensor(n, s, d, kind=kind).ap()
    NS = NT // 512
    NK = T // 2048
    consts = dt("consts", [128, 1280])
    c_in = dt("c", [1024])
    x_in = dt("x", [NT, 1024])
    embg = dt("embg", [1024])
    embb = dt("embb", [1024])
    L = []
    for l in range(2):
        L.append(dict(
            wada_f=dt(f"wada_f{l}", [1024, 2048]), bada_f=dt(f"bada_f{l}", [2048]),
            wada_g=dt(f"wada_g{l}", [1024, 1024]), bada_g=dt(f"bada_g{l}", [1024]),
            wout=dt(f"wout{l}", [1024, 1024]), lng=dt(f"lng{l}", [1024]), lnb=dt(f"lnb{l}", [1024]),
            wcore=dt(f"wcore{l}", [1024, NCOL]), vecs=dt(f"vecs{l}", [128, 11]), up=dt(f"up{l}", [128, 128]),
            fbf=dt(f"fbf{l}", [2, 1])))
    xo = dt("xo", [NT, 1024], kind="ExternalOutput")
    xd = [nc.dram_tensor(f"xd{l}", [NT, 1024], F32).ap() for l in range(2)]
    hloc = [nc.dram_tensor(f"hloc{l}", [NS, 1024, 512], BF16).ap() for l in range(2)]
    hg = [nc.dram_tensor(f"hg{l}", [NS, 4, 1024, 512], BF16).ap() for l in range(2)]
    yc = [nc.dram_tensor(f"yc{l}", [NK, 256, 2048], BF16).ap() for l in range(2)]
    yg = [nc.dram_tensor(f"yg{l}", [NK, 4, 256, 2048], BF16).ap() for l in range(2)]
    q = nc.partition_id() % 4
    P = Prog(nc)

    def front_hooks(l):
        h_dst = lambda s: hloc[l][s].rearrange("(c p) t -> p c t", p=128)
        after_h = lambda s: P.cc("AllGather", hloc[l][s].opt(), hg[l][s].opt(), RG,
                                 reads=[f"htile{s}"], writes=[f"hg{s}"])
        return dict(h_dst=h_dst, after_h=after_h, wada_f=L[l]["wada_f"], bada_f=L[l]["bada_f"])

    P.begin_scope()
    emit_tok(P, nc, NT, consts, c_in, x_in=x_in, embg=embg, embb=embb, x_out=xd[0], pfx="a", **front_hooks(0))
    P.end_scope()
    for l in range(2):
        P.begin_scope()

        def after_tile(j, l=l):
            if j % 4 == 3:
                kk = j // 4
                keys = []
                for jj in range(j - 3, j + 1):
                    keys += [f"ytile{jj}_a", f"ytile{jj}_b0", f"ytile{jj}_b1"]
                P.cc("AllGather", yc[l][kk].opt(), yg[l][kk].opt(), RG, reads=keys, writes=[f"yg{kk}"])

        emit_mix(P, nc, T, None, L[l]["wcore"], L[l]["vecs"], L[l]["up"], L[l]["fbf"], consts, None,
                 h_src=lambda j, l=l: hg[l][j % NS, j // NS].rearrange("(c p) t -> p c t", p=128),
                 y_dst=lambda r0, r1, j, l=l: yc[l][j // 4, r0:r1, (j % 4) * 512:(j % 4 + 1) * 512],
                 after_tile=after_tile, h_keys=lambda j: [f"hg{j % NS}"])
        P.end_scope()
        P.begin_scope()

        def y_src(s, l=l):
            v = yg[l][bass.ds(2 * q + s // 4, 1)]
            return v.rearrange("o r (h p) t -> p (o r h) t", p=128)[:, :, (s % 4) * 512:(s % 4 + 1) * 512]

        kw = dict(y_src=y_src, y_keys=[f"yg{k_}" for k_ in range(NK)], xprev=xd[l], wout=L[l]["wout"], wada_g=L[l]["wada_g"], bada_g=L[l]["bada_g"],
                  lng=L[l]["lng"], lnb=L[l]["lnb"], pfx=f"b{l}")
        if l == 0:
            kw.update(front_hooks(1))
            kw.update(x_out=xd[1])
        else:
            kw.update(x_out=xo)
        emit_tok(P, nc, NT, consts, c_in, **kw)
        P.end_scope()
    print("fused ops", P.n_ops, "waits", P.n_waits)
    P.close()
    return nc


def wout_perm():
    idx = []
    for r in range(4):
        idx += list(range(128 * r, 128 * r + 128))
        idx += list(range(512 + 128 * r, 512 + 128 * r + 128))
    return np.array(idx)


def kernel(**inp):
    inp = {k: np.ascontiguousarray(np.asarray(v)) for k, v in inp.items()}
    consts = make_consts()
    ca = np.ascontiguousarray
    if "nc" not in _NC_CACHE:
        _NC_CACHE["nc"] = build_fused()
    nc = _NC_CACHE["nc"]
    perm = wout_perm()
    maps = []
    for core in range(8):
        b, g = core // 4, core % 4
        m = dict(consts=consts, c=ca(inp["c"][b]), x=ca(inp["x"][b][g * NTOK:(g + 1) * NTOK]),
                 embg=inp["emb_ln_g"], embb=inp["emb_ln_b"])
        for l in range(2):
            pk = pack_core(inp, l, g)
            m.update({f"wada_f{l}": ca(inp["w_ada"][l][:, 0:2048]), f"bada_f{l}": ca(inp["b_ada"][l][0:2048]),
                      f"wada_g{l}": ca(inp["w_ada"][l][:, 2048:3072]), f"bada_g{l}": ca(inp["b_ada"][l][2048:3072]),
                      f"wout{l}": ca(inp["w_out"][l][perm]), f"lng{l}": inp["ln_g"][l], f"lnb{l}": inp["ln_b"][l],
                      f"wcore{l}": pk["wcore"], f"vecs{l}": pk["vecs"], f"up{l}": pk["up"], f"fbf{l}": pk["fbf"]})
        maps.append(m)
    res = run_bass_kernel_spmd(nc, maps, core_ids=list(range(8))).results
    out = np.stack([np.concatenate([res[b * 4 + g]["xo"] for g in range(4)], axis=0) for b in range(2)], axis=0)
    return np.asarray(out, dtype=np.float32)
```

```python
import contextlib
import numpy as np
import concourse.bass as bass
import concourse.mybir as mybir
from concourse.bass_utils import run_bass_kernel_spmd

F32 = mybir.dt.float32
BF16 = mybir.dt.bfloat16
AF = mybir.ActivationFunctionType
ALU = mybir.AluOpType
AX = mybir.AxisListType

SEM_EPOCH = 20000


class Prog:
    def __init__(self, nc):
        self.nc = nc
        self.es = contextlib.ExitStack()
        self.eng = {"pe": nc.tensor, "act": nc.scalar, "dve": nc.vector,
                    "pool": nc.gpsimd, "sp": nc.sync}
        self.cnt = {e: 0 for e in self.eng}
        self.epoch = {e: 0 for e in self.eng}
        self.esem = {}
        for e in self.eng:
            self.esem[e] = self._newsem(f"s_{e}_0")
        self.seen = {e: {} for e in self.eng}
        self.sems = {}
        self.dcnt = {}
        self.lastw = {}
        self.reads = {}
        self.n_ops = 0
        self.n_waits = 0
        self.tes = self.es
        self.scope_id = 0
        self.ccsem = self._newsem("s_cc")
        self.ccn = 0
        self.kalias = {}

    def _newsem(self, name):
        return self.es.enter_context(self.nc.semaphore(name))

    def sb(self, name, shape, dt):
        return self.tes.enter_context(self.nc.sbuf_tensor(f"s{self.scope_id}_{name}", list(shape), dt))

    def ps(self, name, shape, dt=F32):
        return self.tes.enter_context(self.nc.psum_tensor(f"s{self.scope_id}_{name}", list(shape), dt))

    def begin_scope(self):
        self.scope_id += 1
        self.tes = contextlib.ExitStack()

    def end_scope(self):
        self.barrier()
        self.tes.close()
        self.tes = self.es

    def barrier(self):
        toks = []
        for f in self.eng:
            if self.cnt[f] > 0:
                toks.append((self.esem[f], self.cnt[f], f))
        for name, sem in self.sems.items():
            if self.dcnt[name] > 0:
                toks.append((sem, self.dcnt[name], "dma"))
        for e in self.eng:
            need = {id(sm): (sm, v) for (sm, v, o) in toks if o != e}
            self._emit_waits(e, need)
        keep = {k: t for k, t in self.lastw.items() if t[2] == "cc"}
        self.lastw.clear()
        self.reads.clear()
        self.lastw.update(keep)

    def cc(self, kind, in_ap, out_ap, rg, reads=(), writes=()):
        need = self._need("pool", reads, writes)
        self._emit_waits("pool", need)
        ins = self.nc.gpsimd.collective_compute(kind, ALU.bypass, replica_groups=rg, ins=[in_ap], outs=[out_ap])
        self.ccn += 1
        ins.then_inc(self.ccsem, 1)
        tok = (self.ccsem, self.ccn, "cc")
        self._commit("cc", tok, reads, writes)
        self.n_ops += 1
        return tok

    def close(self):
        if self.ccn > 0:
            self._emit_waits("pool", {id(self.ccsem): (self.ccsem, self.ccn)})
        self.es.close()

    def _need(self, e, reads, writes):
        need = {}

        def add(tok, kind):
            if tok is None:
                return
            sem, val, owner = tok
            if owner == e:
                if e in ("pe", "sp"):
                    return
                if kind == "war":
                    return
            k = id(sem)
            if k not in need or need[k][1] < val:
                need[k] = (sem, val)

        for r in reads:
            add(self.lastw.get(r), "raw")
        for w in writes:
            add(self.lastw.get(w), "waw")
            for tok in self.reads.get(w, {}).values():
                add(tok, "war")
        return need

    def _emit_waits(self, e, need):
        eng = self.eng[e]
        seen = self.seen[e]
        for k, (sem, val) in need.items():
            if seen.get(k, 0) >= val:
                continue
            eng.wait_ge(sem, val)
            seen[k] = val
            self.n_waits += 1

    def _commit(self, e, tok, reads, writes):
        for w in writes:
            self.lastw[w] = tok
            self.reads[w] = {}
        for r in reads:
            self.reads.setdefault(r, {})[(e, id(tok[0]))] = tok

    @staticmethod
    def _is_psum(k):
        return isinstance(k, str) and (k.startswith("ps") or "pp" in k)

    def alias(self, a, b):
        self.kalias[a] = b

    def _ka(self, keys):
        if not self.kalias:
            return keys
        return [self.kalias.get(k, k) for k in keys]

    def op(self, e, fn, reads=(), writes=()):
        reads, writes = self._ka(reads), self._ka(writes)
        px = [k for k in reads if self._is_psum(k)]
        if px:
            reads = [k for k in reads if not self._is_psum(k)]
            writes = list(writes) + px
        need = self._need(e, reads, writes)
        self._emit_waits(e, need)
        if self.cnt[e] >= SEM_EPOCH:
            self.epoch[e] += 1
            self.esem[e] = self._newsem(f"s_{e}_{self.epoch[e]}")
            self.cnt[e] = 0
        ins = fn()
        self.cnt[e] += 1
        ins.then_inc(self.esem[e], 1)
        tok = (self.esem[e], self.cnt[e], e)
        self._commit(e, tok, reads, writes)
        self.n_ops += 1
        return tok

    def dma(self, q, out, in_, dsem, reads=(), writes=(), **kw):
        reads, writes = self._ka(reads), self._ka(writes)
        need = self._need(q, reads, writes)
        self._emit_waits(q, need)
        if dsem not in self.sems:
            self.sems[dsem] = self._newsem("d_" + dsem)
            self.dcnt[dsem] = 0
        sem = self.sems[dsem]
        ins = self.eng[q].dma_start(out=out, in_=in_, **kw)
        self.dcnt[dsem] += 16
        ins.then_inc(sem, 16)
        tok = (sem, self.dcnt[dsem], "dma")
        self._commit("dma", tok, reads, writes)
        self.n_ops += 1
        return tok

    def wait_all(self, e, keys):
        need = {}
        for k in keys:
            tok = self.lastw.get(k)
            if tok is None:
                continue
            kk = id(tok[0])
            if kk not in need or need[kk][1] < tok[1]:
                need[kk] = (tok[0], tok[1])
        self._emit_waits(e, need)

import numpy as np

NCOL = 1154
C_R, C_K, C_V, C_WA, C_GA, C_FQ, C_FK = 0, 128, 256, 384, 512, 640, 768
C_GB0, C_GB1, C_FV, C_F = 896, 960, 1024, 1152
K_ID, K_BO, K_SU, K_UI, K_SL, K_RS, K_ONE = 0, 128, 256, 384, 512, 640, 1152
NCONST = 1280
GN_EPS = 64e-5
RW_DT = BF16
RW_ROUNDS = 48.0
OVERLAP_PROJ = False
NEU_BANKS = 3
FIN_OVERLAP = False
DRAIN_OVERLAP = False
DRAIN_MIN = 100


def make_consts():
    c = np.zeros((128, NCONST), np.float32)
    i = np.arange(128)
    c[:, K_ID:K_ID + 128] = np.eye(128)
    c[:, K_BO:K_BO + 128] = (i[:, None] // 64 == i[None, :] // 64)
    c[:, K_SU:K_SU + 128] = (i[:, None] < i[None, :])
    c[:, K_UI:K_UI + 128] = (i[:, None] <= i[None, :])
    c[:, K_SL:K_SL + 128] = (i[:, None] > i[None, :])
    rs = np.ones(512, np.float32)
    rs[::128] = 0
    c[:, K_RS:K_RS + 512] = rs[None, :]
    c[:, K_ONE:K_ONE + 128] = 1.0
    return c


def interleave(gens, weights):
    gens = list(gens)
    acc = [0.0] * len(gens)
    alive = [True] * len(gens)
    while any(alive):
        for i, g in enumerate(gens):
            if not alive[i]:
                continue
            acc[i] += weights[i]
            while acc[i] >= 1.0 and alive[i]:
                acc[i] -= 1.0
                try:
                    next(g)
                except StopIteration:
                    alive[i] = False


def emit_mix(P, nc, T, hT, wcore, vecs, up, fbf, consts, yT, do_rwkv=True, do_fox=True,
             h_src=None, y_dst=None, after_tile=None, h_keys=None):
    NT = T // 512
    NB = T // 128
    V = nc.vector
    A = nc.scalar
    G = nc.gpsimd
    PE = nc.tensor

    cst = P.sb("cst", [128, NCONST], F32)
    vec = P.sb("vec", [128, 20], F32)
    fbh = P.sb("fbh", [2, 1], F32)
    upt = P.sb("upt", [128, 128], F32)
    fbt = P.sb("fbt", [2, 1], F32)
    P.dma("sp", cst[:], consts[:, :], "ld_cst", writes=["cst"])
    P.dma("sp", vec[:, 0:11], vecs[:, :], "ld_vec", writes=["vec"])
    P.dma("sp", upt[:], up[:, :], "ld_upt", writes=["upt"])
    P.dma("sp", fbt[:], fbf[:, :], "ld_fbt", writes=["fbt"])
    ident = cst[:, K_ID:K_ID + 128]
    bones = cst[:, K_BO:K_BO + 128]
    mask2 = cst[:, K_SU:K_SU + 256]
    msl = cst[:, K_SL:K_SL + 128]
    rsm = cst[:, K_RS:K_RS + 512]
    ones = cst[:, K_ONE:K_ONE + 128]
    P.op("dve", lambda: V.tensor_scalar(out=vec[:, 11:15], in0=vec[:, 0:4], scalar1=-1.0, scalar2=1.0,
                                        op0=ALU.mult, op1=ALU.add), reads=["vec"], writes=["vec"])
    P.op("dve", lambda: V.tensor_scalar(out=vec[:, 15:16], in0=vec[:, 7:8], scalar1=-1.0, scalar2=1.0,
                                        op0=ALU.mult, op1=ALU.add), reads=["vec"], writes=["vec"])
    P.op("dve", lambda: V.tensor_scalar(out=vec[:, 16:18], in0=vec[:, 4:6], scalar1=0.5, scalar2=None,
                                        op0=ALU.mult), reads=["vec"], writes=["vec"])
    P.op("dve", lambda: V.tensor_scalar(out=fbh[:], in0=fbt[:], scalar1=0.5, scalar2=None, op0=ALU.mult),
         reads=["fbt"], writes=["fbh"])
    uib = P.sb("uib", [128, 128], BF16)
    idb = P.sb("idb", [128, 128], BF16)
    P.op("dve", lambda: V.tensor_copy(out=idb[:], in_=cst[:, K_ID:K_ID + 128]), reads=["cst"], writes=["idb"])
    P.op("dve", lambda: V.tensor_copy(out=uib[:], in_=cst[:, K_UI:K_UI + 128]), reads=["cst"], writes=["uib"])

    wsb = P.sb("wsb", [128, 8, NCOL], BF16)
    wst = [P.sb(f"wst{i}", [128, 8, 64], F32) for i in range(2)]
    wv = wcore.rearrange("(c p) n -> p c n", p=128)
    pieces = [(s, min(64, NCOL - s)) for s in range(0, NCOL, 64)]
    for pi, (s, n) in enumerate(pieces):
        b = pi % 2
        P.dma("sp", wst[b][:, :, 0:n], wv[:, :, s:s + n], f"ld_w{b}", writes=[f"wst{b}"])
        eng = "act" if pi % 2 == 0 else "dve"
        if eng == "act":
            P.op("act", lambda: A.copy(out=wsb[:, :, s:s + n], in_=wst[b][:, :, 0:n]),
                 reads=[f"wst{b}"], writes=["wsb"])
        else:
            P.op("dve", lambda: V.tensor_copy(out=wsb[:, :, s:s + n], in_=wst[b][:, :, 0:n]),
                 reads=[f"wst{b}"], writes=["wsb"])

    hTt = [P.sb(f"hT{i}", [128, 8, 512], BF16) for i in range(2)]
    if h_src is None:
        hv = hT.rearrange("(c p) t -> p c t", p=128)
        h_src = lambda j: hv[:, :, j * 512:(j + 1) * 512]
    if y_dst is None:
        y_dst = lambda r0, r1, j: yT[r0:r1, j * 512:(j + 1) * 512]
    NSB = 3 if OVERLAP_PROJ else (6 - NEU_BANKS)
    NPT = 4
    LOOK = 2 if OVERLAP_PROJ else (5 - NEU_BANKS)
    psS = [P.ps(f"psS{i}", [128, 512]) for i in range(NSB)]
    psO = P.ps("psO", [128, 512])
    psA = [P.ps(f"psA{i}", [128, 512]) for i in range(NEU_BANKS)]
    ppb = P.ps("ppb", [128, 512]) if OVERLAP_PROJ else None
    psY = P.ps("psY", [128, 512])
    ppc = [0]

    def nextpp():
        if OVERLAP_PROJ:
            return ppb, "ppb"
        i = ppc[0] % 2
        ppc[0] += 1
        return psA[i], f"psA{i}"

    KT = P.sb("KT", [128, T], BF16)
    Vaug = P.sb("Vaug", [128, NB, 2, 65], BF16)
    ckres = P.sb("ckres", [128, NB, 2], F32)
    P.op("pool", lambda: G.memset(Vaug[:, :, :, 64:65], 1.0), writes=["Vaug_ones"])
    QT = [[P.sb(f"QT{i}_{h}", [128, 512], BF16) for h in range(2)] for i in range(2)]
    for i in range(2):
        for h in range(2):
            P.op("pool", lambda: G.memset(QT[i][h][:], 0.0), writes=[f"QT{i}"])
    sgb = [[P.sb(f"sgb{i}_{h}", [64, 512], F32) for h in range(2)] for i in range(2)]
    rrow = P.sb("rrow", [128, 512], F32)
    lft = P.sb("lft", [2, 512], F32)
    lf = lft[:, :]
    onesrow = rrow[0:2, :]
    cum = [P.sb(f"cum{i}", [2, 512], F32) for i in range(2)]
    P.op("dve", lambda: V.memset(cum[1][:], 0.0), writes=["cum1"])
    i2 = P.sb("i2", [2, 2], F32)
    basebc = [P.sb(f"basebc{i}", [128, 2], F32) for i in range(2)]
    biasj = [P.sb(f"biasj{i}", [128, NB, 2], F32) for i in range(2)]
    PT = [P.sb(f"PT{i}", [128, 512], BF16) for i in range(NPT)]
    t1 = P.sb("t1", [64, 512], F32)
    osb = P.sb("osb", [65, 512], F32)
    ybo = [P.sb(f"ybo{i}", [64, 512], BF16) for i in range(2)]

    raw = {g: P.sb(f"raw_{g}", [128, 513], F32) for g in "rkvw"}
    for g in "rkvw":
        P.op("pool", lambda: G.memset(raw[g][:, 0:1], 0.0), writes=[f"raw_{g}c"])
    tmp = P.sb("tmp", [128, 512], F32)
    sh = {g: P.sb(f"sh_{g}", [128, 512], F32) for g in "rkvw"}
    a_t = P.sb("a_t", [128, 512], F32)
    ld = P.sb("ld", [128, 512], F32)
    cl = P.sb("cl", [128, 512], F32)
    einc = P.sb("einc", [128, 512], F32)
    eneg = P.sb("eneg", [128, 512], F32)
    kk = P.sb("kk", [128, 512], F32)
    kmod = tmp
    P.alias("kmod", "tmp")
    w1 = P.sb("w1", [128, 512], F32)
    w2 = P.sb("w2", [128, 512], F32)
    KRz = [P.sb(f"KRz{i}", [128, 4, 2, 128], RW_DT) for i in range(2)]
    for i in range(2):
        P.op("pool", lambda: G.memset(KRz[i][:], 0.0), writes=["KR"])
    Bt = P.sb("Bt", [128, 512], RW_DT)
    Kt = P.sb("Kt", [128, 512], RW_DT)
    bonus = ld
    P.alias("bonus", "ld")
    sgaL = [P.sb(f"sga{i}", [128, 512], F32) for i in range(2)]
    Vtok = P.sb("Vtok", [128, 4, 128], RW_DT)
    Btok = P.sb("Btok", [128, 4, 128], RW_DT)
    Ktok = P.sb("Ktok", [128, 4, 128], RW_DT)
    ST = P.sb("ST", [128, 64], F32)
    STw = P.sb("STw", [128, 64], F32)
    STb = P.sb("STb", [128, 128], RW_DT)
    P.op("dve", lambda: V.memset(ST[:], 0.0), writes=["ST0", "ST1"])
    P.op("dve", lambda: V.memset(STb[:], 0.0), writes=["STb0", "STb1"])
    AT = [[P.sb(f"AT{i}_{k}", [128, 256], RW_DT) for k in range(4)] for i in range(2)]
    AK = [[P.sb(f"AK{i}_{k}", [128, 256], RW_DT) for k in range(4)] for i in range(2)]
    XL = [[P.sb(f"XL{i}_{k}", [128, 384], RW_DT) for k in range(4)] for i in range(2)]
    Tm = [[XL[i][k][:, 256:384] for k in range(4)] for i in range(2)]
    TmF = Tm
    Zs = P.sb("Zs", [128, 128], RW_DT)
    Us = P.sb("Us", [128, 128], RW_DT)
    P.op("pool", lambda: G.memset(Us[:], 0.0), writes=["Us0", "Us1"])
    P.op("pool", lambda: G.memset(Zs[:], 0.0), writes=["Zs0", "Zs1"])
    ysb = cl
    for k_ in ("ysb", "ysb0", "ysb1"):
        P.alias(k_, "cl")
    yao = [P.sb(f"yao{i}", [128, 512], BF16) for i in range(2)]

    def proj_group(j, col, M, rhs_tile, rhs_key):
        ps, pk = nextpp()
        for c in range(8):
            P.op("pe", lambda: PE.matmul(ps[0:M, :], lhsT=wsb[:, c, col:col + M], rhs=rhs_tile[:, c, :],
                                         start=(c == 0), stop=(c == 7)),
                 reads=["wsb", rhs_key], writes=[pk])
        return ps, pk

    def load_h(j):
        b = j % 2
        P.dma("sp", hTt[b][:], h_src(j), f"ld_h{b}", reads=(h_keys(j) if h_keys else ()), writes=[f"hT{b}"])

    def tile_proj(j):
        b = j % 2
        ht, hk = hTt[b], f"hT{b}"
        sga = sgaL[b]
        sgak = f"sga{b}"
        if do_rwkv:
            for gi, (g, col) in enumerate((("r", C_R), ("k", C_K), ("v", C_V), ("w", C_WA))):
                ps, pk = proj_group(j, col, 128, ht, hk)
                rw = raw[g]
                if j > 0:
                    P.op("act", lambda: A.copy(out=rw[:, 0:1], in_=rw[:, 512:513]), reads=[f"raw_{g}"],
                         writes=[f"raw_{g}c"])
                P.op("dve", lambda: V.tensor_copy(out=rw[:, 1:513], in_=ps[:, :]), reads=[pk], writes=[f"raw_{g}"])
                P.op("act", lambda: A.activation(out=tmp[:], in_=ps[:, :], func=AF.Identity,
                                                 scale=vec[:, 11 + gi:12 + gi]),
                     reads=[pk, "vec"], writes=["tmp"])
                P.op("dve", lambda: V.scalar_tensor_tensor(out=sh[g][:], in0=rw[:, 0:512], scalar=vec[:, gi:gi + 1],
                                                           in1=tmp[:], op0=ALU.mult, op1=ALU.add),
                     reads=[f"raw_{g}", f"raw_{g}c", "tmp", "vec"], writes=[f"sh_{g}"])
                yield
            ps, pk = proj_group(j, C_GA, 128, ht, hk)
            P.op("act", lambda: A.activation(out=sga[:], in_=ps[:, :], func=AF.Tanh, scale=0.5), reads=[pk],
                 writes=[sgak])
            P.op("dve", lambda: V.scalar_tensor_tensor(out=sga[:], in0=sga[:], scalar=1.0, in1=ps[:, :],
                                                       op0=ALU.add, op1=ALU.mult), reads=[pk, "sga"], writes=[sgak])
        if do_fox:
            ps, pk = proj_group(j, C_FQ, 128, ht, hk)
            for h in range(2):
                hp_ = slice(64 * h, 64 * h + 64)
                P.op("act", lambda: A.activation(out=QT[b][h][hp_, :], in_=ps[hp_, :], func=AF.Copy, scale=0.125),
                     reads=[pk], writes=[f"QT{b}"])
            yield
            ps, pk = proj_group(j, C_FK, 128, ht, hk)
            P.op("dve", lambda: V.tensor_copy(out=KT[:, j * 512:(j + 1) * 512], in_=ps[:, :]),
                 reads=[pk], writes=[f"KT{j}"])
            yield
            for h, col in ((0, C_GB0), (1, C_GB1)):
                ps, pk = proj_group(j, col, 64, ht, hk)
                P.op("act", lambda: A.activation(out=sgb[b][h][:], in_=ps[0:64, :], func=AF.Tanh, scale=0.5),
                     reads=[pk], writes=[f"sgb{b}_{h}"])
                P.op("dve", lambda: V.scalar_tensor_tensor(out=sgb[b][h][:], in0=sgb[b][h][:], scalar=1.0,
                                                           in1=ps[0:64, :], op0=ALU.add, op1=ALU.mult),
                     reads=[pk, f"sgb{b}_{h}"], writes=[f"sgb{b}_{h}"])
                yield
            ps, pk = nextpp()
            for q in range(4):
                for c in range(8):
                    P.op("pe", lambda: PE.matmul(ps[:, q * 128:(q + 1) * 128], lhsT=ht[:, c, q * 128:(q + 1) * 128],
                                                 rhs=wsb[:, c, C_FV:C_FV + 128], start=(c == 0), stop=(c == 7)),
                         reads=["wsb", hk], writes=[pk])
            P.op("dve", lambda: V.tensor_copy(
                out=Vaug[:, 4 * j:4 * j + 4, :, 0:64],
                in_=ps[:, :].rearrange("p (q h d) -> p q h d", q=4, h=2)),
                reads=[pk], writes=[f"V{j}"])
            yield
            ps, pk = proj_group(j, C_F, 2, ht, hk)
            P.op("act", lambda: A.activation(out=lf, in_=ps[0:2, :], func=AF.Tanh, bias=fbh[:, 0:1], scale=0.5),
                 reads=[pk, "fbh"], writes=["lf"])
            P.op("dve", lambda: V.tensor_scalar(out=lf, in0=lf, scalar1=0.5, scalar2=0.5, op0=ALU.mult, op1=ALU.add),
                 reads=["lf"], writes=["lf"])
            P.op("act", lambda: A.activation(out=lf, in_=lf, func=AF.Ln), reads=["lf"], writes=["lf"])
            cprev, cb = cum[1 - b], cum[b]
            P.op("dve", lambda: V.tensor_tensor_scan(out=cb[:], data0=onesrow, data1=lf,
                                                     initial=cprev[:, 511:512], op0=ALU.mult, op1=ALU.add),
                 reads=["lf", f"cum{1-b}", "onesrow"], writes=[f"cum{b}"])
            yield
            ps, pk = nextpp()
            for q in range(4):
                P.op("pe", lambda: PE.matmul(ps[:, 2 * q:2 * q + 2], lhsT=cb[0:2, q * 128:(q + 1) * 128],
                                             rhs=cst[0:2, K_ID:K_ID + 2], start=True, stop=True),
                     reads=[f"cum{b}", "cst"], writes=[pk])
            P.op("dve", lambda: V.tensor_scalar(out=i2[:], in0=cst[0:2, K_ID:K_ID + 2], scalar1=cb[:, 255:256],
                                                scalar2=None, op0=ALU.mult),
                 reads=[f"cum{b}", "cst"], writes=["i2"])
            P.op("pe", lambda: PE.matmul(ps[:, 8:10], lhsT=ones[0:2, :], rhs=i2[:, :], start=True, stop=True),
                 reads=["i2", "cst"], writes=[pk])
            P.op("dve", lambda: V.tensor_copy(out=ckres[:, 4 * j:4 * j + 4, :],
                                              in_=ps[:, 0:8].rearrange("p (q h) -> p q h", q=4)),
                 reads=[pk], writes=[f"ck{j}"])
            P.op("dve", lambda: V.tensor_copy(out=basebc[b][:], in_=ps[:, 8:10]), reads=[pk], writes=[f"basebc{b}"])
        yield

    P.op("dve", lambda: V.memset(onesrow, 1.0), writes=["onesrow"])

    def attn_tile(j):
        b = j % 2
        nb = 4 * j + 4
        steps = [(h, kb) for h in range(2) for kb in range(nb)]
        for h in range(2):
            bj = biasj[h]
            P.op("dve", lambda: V.tensor_scalar(out=bj[:, 0:nb, h], in0=ckres[:, 0:nb, h], scalar1=-1.0,
                                                scalar2=basebc[b][:, h:h + 1], op0=ALU.mult, op1=ALU.add),
                 reads=[f"ck{jj}" for jj in range(j + 1)] + [f"basebc{b}"], writes=[f"biasj{h}"])
        yield

        def q0_of(kb):
            m = kb - 4 * j
            return 128 * m if m > 0 else 0

        def emit_S(i):
            h, kb = steps[i]
            hp = slice(64 * h, 64 * h + 64)
            q0 = q0_of(kb)
            si = i % NSB
            P.op("pe", lambda: PE.matmul(psS[si][:, q0:512], lhsT=KT[:, kb * 128:(kb + 1) * 128],
                                         rhs=QT[b][h][:, q0:512], start=True, stop=True),
                 reads=[f"KT{kb // 4}", f"QT{b}"], writes=[f"psS{si}"])

        for i0 in range(min(LOOK, len(steps))):
            emit_S(i0)
        for i, (h, kb) in enumerate(steps):
            if i + LOOK < len(steps):
                emit_S(i + LOOK)
            bj = biasj[h]
            m = kb - 4 * j
            q0 = q0_of(kb)
            si = i % NSB
            pi = i % NPT
            P.op("act", lambda: A.activation(out=PT[pi][:, q0:512], in_=psS[si][:, q0:512], func=AF.Exp,
                                             bias=bj[:, kb, h:h + 1], scale=1.0),
                 reads=[f"psS{si}", f"biasj{h}"], writes=[f"PT{pi}"])
            if m >= 0:
                P.op("pool", lambda: G.tensor_tensor(out=PT[pi][:, q0:q0 + 128], in0=PT[pi][:, q0:q0 + 128],
                                                     in1=uib[:], op=ALU.mult),
                     reads=[f"PT{pi}", "uib"], writes=[f"PT{pi}"])
            P.op("pe", lambda: PE.matmul(psO[0:65, q0:512], lhsT=Vaug[:, kb, h, :], rhs=PT[pi][:, q0:512],
                                         start=(kb == 0), stop=(kb == nb - 1)),
                 reads=[f"PT{pi}", f"V{kb // 4}", "Vaug_ones"], writes=["psO"])
            if kb == nb - 1:
                P.op("dve", lambda: V.tensor_copy(out=osb[0:65, :], in_=psO[0:65, :]), reads=["psO"], writes=["osb"])
                P.op("dve", lambda: V.reciprocal(out=rrow[64:65, :], in_=osb[64:65, :]), reads=["osb"],
                     writes=["rrow"])
                pq, pqk = psS[si], f"psS{si}"
                P.op("pe", lambda: PE.matmul(pq[0:64, :], lhsT=ones[64:65, 0:64], rhs=rrow[64:65, :],
                                             start=True, stop=True),
                     reads=["rrow", "cst"], writes=[pqk])
                P.op("dve", lambda: V.scalar_tensor_tensor(out=t1[:], in0=pq[0:64, :], scalar=0.5, in1=sgb[b][h][:],
                                                           op0=ALU.mult, op1=ALU.mult),
                     reads=[pqk, f"sgb{b}_{h}"], writes=["t1"])
                P.op("pool", lambda: G.tensor_tensor(out=ybo[h][:], in0=osb[0:64, :], in1=t1[:], op=ALU.mult),
                     reads=["osb", "t1"], writes=[f"ybo{h}"])
                P.dma("sp", y_dst(128 + 64 * h, 192 + 64 * h, j), ybo[h][:], f"st_yb{h}",
                      reads=[f"ybo{h}"], writes=[f"yTb{h}", f"ytile{j}_b{h}"])
            yield

    def SLK(h, i0=0, i1=4):
        return [f"psA{h}"]

    psC = psY
    CK = ["psC"]

    progress = [0, 0]

    def rwkv_prep(j):
        for h_ in range(2):
            for c_ in range(4):
                ndone[h_][c_] = False
        pa, pak = psA[0], SLK(0)
        pb, pbk = psA[1], SLK(1)
        th = tmp
        P.op("act", lambda: A.activation(out=th[0:64, :], in_=sh["w"][0:64, :], func=AF.Tanh),
             reads=["sh_w"], writes=["tmp"])
        P.op("pe", lambda: PE.matmul(pa[:, :], lhsT=upt[0:64, :], rhs=th[0:64, :], start=True, stop=True),
             reads=["upt", "tmp"], writes=pak)
        P.op("pe", lambda: PE.matmul(pb[:, :], lhsT=upt[64:128, :], rhs=sh["w"][64:128, :], start=True, stop=True),
             reads=["upt", "sh_w"], writes=pbk)
        P.op("act", lambda: A.activation(out=ld[:], in_=pa[:, :], func=AF.Tanh, bias=vec[:, 16:17], scale=0.5),
             reads=pak + ["vec"], writes=["ld"])
        P.op("act", lambda: A.activation(out=a_t[:], in_=pb[:, :], func=AF.Tanh, bias=vec[:, 17:18], scale=0.5),
             reads=pbk + ["vec"], writes=["a_t"])
        yield
        P.op("dve", lambda: V.tensor_scalar(out=ld[:], in0=ld[:], scalar1=1.0, scalar2=-0.5 * float(np.exp(-0.5)),
                                            op0=ALU.add, op1=ALU.mult), reads=["ld"], writes=["ld"])
        P.op("pool", lambda: G.tensor_scalar(out=a_t[:], in0=a_t[:], scalar1=0.5, scalar2=0.5, op0=ALU.mult,
                                             op1=ALU.add), reads=["a_t"], writes=["a_t"])
        P.op("dve", lambda: V.tensor_tensor_scan(out=cl[:], data0=rsm, data1=ld[:], initial=0.0,
                                                 op0=ALU.mult, op1=ALU.add),
             reads=["ld", "cst"], writes=["cl"])
        P.op("act", lambda: A.activation(out=einc[:], in_=cl[:], func=AF.Exp), reads=["cl"], writes=["einc"])
        P.op("act", lambda: A.activation(out=eneg[:], in_=cl[:], func=AF.Exp, scale=-1.0),
             reads=["cl"], writes=["eneg"])
        P.op("dve", lambda: V.tensor_tensor(out=w1[:], in0=cl[:], in1=ld[:], op=ALU.subtract),
             reads=["cl", "ld"], writes=["w1"])
        P.op("act", lambda: A.activation(out=w1[:], in_=w1[:], func=AF.Exp), reads=["w1"], writes=["w1"])
        yield
        P.op("dve", lambda: V.tensor_scalar(out=kk[:], in0=sh["k"][:], scalar1=vec[:, 6:7], scalar2=None,
                                            op0=ALU.mult), reads=["sh_k", "vec"], writes=["kk"])
        P.op("dve", lambda: V.tensor_tensor(out=w2[:], in0=kk[:], in1=kk[:], op=ALU.mult),
             reads=["kk"], writes=["w2"])
        P.op("pe", lambda: PE.matmul(pa[:, :], lhsT=bones, rhs=w2[:], start=True, stop=True),
             reads=["cst", "w2"], writes=pak)
        P.op("dve", lambda: V.tensor_scalar(out=w2[:], in0=pa[:, :], scalar1=1e-24, scalar2=None, op0=ALU.max),
             reads=pak, writes=["w2"])
        P.op("act", lambda: A.activation(out=w2[:], in_=w2[:], func=AF.Ln), reads=["w2"], writes=["w2"])
        P.op("act", lambda: A.activation(out=w2[:], in_=w2[:], func=AF.Exp, scale=-0.5), reads=["w2"], writes=["w2"])
        P.op("dve", lambda: V.tensor_tensor(out=kk[:], in0=kk[:], in1=w2[:], op=ALU.mult),
             reads=["kk", "w2"], writes=["kk"])
        yield
        P.op("dve", lambda: V.tensor_scalar(out=w2[:], in0=a_t[:], scalar1=vec[:, 7:8], scalar2=vec[:, 15:16],
                                            op0=ALU.mult, op1=ALU.add), reads=["a_t", "vec"], writes=["w2"])
        P.op("dve", lambda: V.tensor_tensor(out=kmod[:], in0=sh["k"][:], in1=w2[:], op=ALU.mult),
             reads=["sh_k", "w2"], writes=["kmod"])
        P.op("dve", lambda: V.tensor_tensor(out=w2[:], in0=kk[:], in1=a_t[:], op=ALU.mult),
             reads=["kk", "a_t"], writes=["w2"])
        P.op("dve", lambda: V.tensor_tensor(out=Bt[:], in0=w2[:], in1=eneg[:], op=ALU.mult),
             reads=["w2", "eneg"], writes=["Bt"])
        P.op("pool", lambda: G.tensor_tensor(out=Kt[:], in0=kmod[:], in1=eneg[:], op=ALU.mult),
             reads=["kmod", "eneg"], writes=["Kt"])
        for hh in range(2):
            hq = slice(64 * hh, 64 * hh + 64)
            P.op("dve", lambda: V.tensor_tensor(out=KRz[hh][hq, :, 0, :],
                                                in0=kk[hq, :].rearrange("p (c t) -> p c t", c=4),
                                                in1=w1[hq, :].rearrange("p (c t) -> p c t", c=4), op=ALU.mult),
                 reads=["kk", "w1"], writes=["KR"])
            P.op("pool", lambda: G.tensor_tensor(out=KRz[hh][hq, :, 1, :],
                                                 in0=sh["r"][hq, :].rearrange("p (c t) -> p c t", c=4),
                                                 in1=einc[hq, :].rearrange("p (c t) -> p c t", c=4), op=ALU.mult),
                 reads=["sh_r", "einc"], writes=["KR"])
        yield
        P.op("dve", lambda: V.scalar_tensor_tensor(out=w2[:], in0=sh["r"][:], scalar=vec[:, 8:9], in1=kmod[:],
                                                   op0=ALU.mult, op1=ALU.mult),
             reads=["sh_r", "vec", "kmod"], writes=["w2"])
        P.op("pe", lambda: PE.matmul(pb[:, :], lhsT=bones, rhs=w2[:], start=True, stop=True),
             reads=["cst", "w2"], writes=pbk)
        P.op("dve", lambda: V.tensor_tensor(out=bonus[:], in0=pb[:, :], in1=sh["v"][:], op=ALU.mult),
             reads=pbk + ["sh_v"], writes=["bonus"])
        yield
        for (src, skey, dst, dkey, pq, pqk) in ((sh["v"], "sh_v", Vtok, "Vtok", pa, pak),
                                                (Bt, "Bt", Btok, "Btok", pb, pbk),
                                                (Kt, "Kt", Ktok, "Ktok", pa, pak)):
            isb = (src.dtype == BF16)
            pqv = pq[:, :].bitcast(BF16) if isb else pq[:, :]
            for c in range(4):
                P.op("pe", lambda: PE.transpose(pqv[:, c * 128:(c + 1) * 128], src[:, c * 128:(c + 1) * 128],
                                                idb[:] if isb else ident),
                     reads=[skey, "cst", "idb"], writes=pqk)
            P.op("dve", lambda: V.tensor_copy(out=dst[:].rearrange("p c t -> p (c t)"), in_=pqv[:, 0:512]),
                 reads=pqk, writes=[dkey])
            yield

    ndone = [[False] * 4, [False] * 4]

    def neumann(j, h, c):
        bi = h if NEU_BANKS == 2 else (2 * c + h) % NEU_BANKS
        pa = psA[bi]
        pk = SLK(bi)
        cs = slice(c * 128, (c + 1) * 128)
        at, atk = AT[h][c], f"AT{h}_{c}"
        ak, akk = AK[h][c], f"AK{h}_{c}"
        X, Xk = XL[h][c], f"TmF{h}_{c}"
        Lc, Ltc, Tc = X[:, 0:128], X[:, 128:256], X[:, 256:384]
        krc = KRz[h][:, c, :, :].rearrange("p a t -> p (a t)")
        P.op("pe", lambda: PE.matmul(pa[:, 0:256], lhsT=Bt[:, cs], rhs=krc, start=True, stop=True),
             reads=["Bt", "KR"], writes=pk)
        P.op("pe", lambda: PE.matmul(pa[:, 256:384], lhsT=KRz[h][:, c, 0, :], rhs=Bt[:, cs], start=True, stop=True),
             reads=["Bt", "KR"], writes=pk)
        P.op("dve", lambda: V.tensor_tensor(out=at[:], in0=pa[:, 0:256], in1=mask2, op=ALU.mult),
             reads=pk + ["cst"], writes=[atk])
        P.op("dve", lambda: V.tensor_tensor(out=Lc, in0=pa[:, 256:384], in1=msl, op=ALU.mult),
             reads=pk + ["cst"], writes=[Xk])
        P.op("pool", lambda: G.tensor_tensor(out=Tc, in0=ident, in1=at[:, 0:128], op=ALU.subtract),
             reads=[atk, "cst", Xk], writes=[Xk])
        yield
        P.op("pe", lambda: PE.matmul(pa[:, 0:128], lhsT=at[:, 0:128], rhs=Lc, start=True, stop=True),
             reads=[atk, Xk], writes=pk)
        P.op("pe", lambda: PE.matmul(pa[:, 128:256], lhsT=Lc, rhs=at[:, 0:128], start=True, stop=True),
             reads=[atk, Xk], writes=pk)
        P.op("pe", lambda: PE.matmul(pa[:, 256:512], lhsT=Kt[:, cs], rhs=krc, start=True, stop=True),
             reads=["Kt", "KR"], writes=pk)
        P.op("dve", lambda: V.tensor_copy(out=X[:, 0:256], in_=pa[:, 0:256]), reads=pk + [Xk], writes=[Xk])
        P.op("dve", lambda: V.tensor_tensor(out=ak[:], in0=pa[:, 256:512], in1=mask2, op=ALU.mult),
             reads=pk + ["cst"], writes=[akk])
        yield
        for k in range(1, 7):
            P.op("pe", lambda: PE.matmul(pa[:, 256:384], lhsT=Lc, rhs=Tc, start=True, stop=False),
                 reads=[Xk], writes=pk)
            P.op("pe", lambda: PE.matmul(pa[:, 256:384], lhsT=idb[:], rhs=Tc, start=False, stop=True),
                 reads=[Xk, "idb"], writes=pk)
            if k < 6:
                P.op("pe", lambda: PE.matmul(pa[:, 0:128], lhsT=Ltc, rhs=Lc, start=True, stop=True),
                     reads=[Xk], writes=pk)
            if k < 5:
                P.op("pe", lambda: PE.matmul(pa[:, 128:256], lhsT=Lc, rhs=Ltc, start=True, stop=True),
                     reads=[Xk], writes=pk)
            lo = 0 if k < 6 else 256
            P.op("dve", lambda: V.tensor_copy(out=X[:, lo:384], in_=pa[:, lo:384]), reads=pk + [Xk], writes=[Xk])
            yield
        ndone[h][c] = True

    def chain(j, h):
        pa = psC
        hp = slice(64 * h, 64 * h + 64)
        hc = slice(64 * h, 64 * h + 64)
        o0 = 256 * h
        stk, stbk, stwk, zk, uk = f"ST{h}", f"STb{h}", f"STw{h}", f"Zs{h}", f"Us{h}"
        for c in range(4):
            while not ndone[h][c]:
                yield
            par = c
            cs = slice(c * 128, (c + 1) * 128)
            wc = einc[hp, c * 128 + 127:c * 128 + 128]
            P.op("pe", lambda: PE.matmul(pa[:, o0:o0 + 64], lhsT=KRz[h][:, c, 0, :], rhs=STb[:, 0:64], start=True, stop=False),
                 reads=["KR", stbk], writes=CK)
            P.op("pe", lambda: PE.matmul(pa[:, o0:o0 + 64], lhsT=AK[h][par][:, 0:128], rhs=Vtok[:, c, hc],
                                         start=False, stop=True),
                 reads=[f"AK{h}_{par}", "Vtok"], writes=CK)
            P.op("dve", lambda: V.tensor_copy(out=Zs[:, hc], in_=pa[:, o0:o0 + 64]), reads=CK, writes=[zk])
            P.op("pool", lambda: G.tensor_scalar(out=STw[hp, :], in0=ST[hp, :], scalar1=wc, scalar2=None,
                                                 op0=ALU.mult), reads=[stk, "einc"], writes=[stwk])
            yield
            P.op("pe", lambda: PE.matmul(pa[:, o0 + 64:o0 + 128], lhsT=TmF[h][par], rhs=Zs[:, hc], start=True, stop=True),
                 reads=[f"TmF{h}_{par}", zk], writes=CK)
            P.op("dve", lambda: V.tensor_scalar(out=Us[:, hc], in0=pa[:, o0 + 64:o0 + 128], scalar1=-1.0,
                                                scalar2=None, op0=ALU.mult), reads=CK, writes=[uk])
            yield
            yo = pa[:, o0 + 128:o0 + 256]
            P.op("pe", lambda: PE.matmul(yo, lhsT=STb[:, :], rhs=KRz[h][:, c, 1, :], start=True, stop=False),
                 reads=[stbk, "KR"], writes=CK)
            P.op("pe", lambda: PE.matmul(yo, lhsT=Us[:, :], rhs=AT[h][par][:, 128:256], start=False, stop=False),
                 reads=[uk, f"AT{h}_{par}"], writes=CK)
            P.op("pe", lambda: PE.matmul(yo, lhsT=Vtok[:, c, :], rhs=AK[h][par][:, 128:256], start=False,
                                         stop=True), reads=["Vtok", f"AK{h}_{par}"], writes=CK)
            P.op("dve", lambda: V.tensor_copy(out=ysb[hp, cs], in_=pa[hp, o0 + 128:o0 + 256]), reads=CK,
                 writes=[f"ysb{h}"])
            P.op("pe", lambda: PE.matmul(pa[:, o0:o0 + 64], lhsT=Btok[:, c, :], rhs=Us[:, hc], start=True, stop=False),
                 reads=["Btok", uk], writes=CK)
            P.op("pe", lambda: PE.matmul(pa[:, o0:o0 + 64], lhsT=Ktok[:, c, :], rhs=Vtok[:, c, hc], start=False, stop=True),
                 reads=["Ktok", "Vtok"], writes=CK)
            P.op("dve", lambda: V.scalar_tensor_tensor(out=ST[hp, :], in0=pa[hp, o0:o0 + 64], scalar=wc, in1=STw[hp, :],
                                                       op0=ALU.mult, op1=ALU.add),
                 reads=CK + ["einc", stwk], writes=[stk])
            P.op("pool", lambda: G.tensor_copy(out=STb[hp, 0:64], in_=ST[hp, :]), reads=[stk], writes=[stbk])
            P.op("dve", lambda: V.tensor_copy(out=STb[hp, 64:128], in_=ST[hp, :]), reads=[stk], writes=[stbk])
            yield

    def rwkv_fin(j):
        b = j % 2
        pa, pak = psA[0], SLK(0)
        pb, pbk = psA[1], SLK(1)
        P.op("pool", lambda: G.tensor_tensor(out=w2[:], in0=ysb[:], in1=ysb[:], op=ALU.mult),
             reads=["ysb0", "ysb1", "ysb"], writes=["w2"])
        yield
        P.op("pe", lambda: PE.matmul(pa[:, :], lhsT=bones, rhs=ysb[:], start=True, stop=True),
             reads=["cst", "ysb0", "ysb1", "ysb"], writes=pak)
        P.op("pe", lambda: PE.matmul(pb[:, :], lhsT=bones, rhs=w2[:], start=True, stop=True),
             reads=["cst", "w2"], writes=pbk)
        P.op("act", lambda: A.activation(out=w1[:], in_=pa[:, :], func=AF.Square, scale=1.0 / 64), reads=pak,
             writes=["w1"])
        P.op("dve", lambda: V.scalar_tensor_tensor(out=ysb[:], in0=pa[:, :], scalar=-1.0 / 64, in1=ysb[:],
                                                   op0=ALU.mult, op1=ALU.add), reads=pak + ["ysb", "ysb0", "ysb1"],
             writes=["ysb", "ysb0", "ysb1"])
        P.op("dve", lambda: V.scalar_tensor_tensor(out=w1[:], in0=pb[:, :], scalar=1.0 / 64, in1=w1[:],
                                                   op0=ALU.mult, op1=ALU.subtract), reads=pbk + ["w1"], writes=["w1"])
        yield
        P.op("dve", lambda: V.tensor_scalar(out=w1[:], in0=w1[:], scalar1=GN_EPS, scalar2=None, op0=ALU.add),
             reads=["w1"], writes=["w1"])
        P.op("act", lambda: A.activation(out=w1[:], in_=w1[:], func=AF.Ln), reads=["w1"], writes=["w1"])
        P.op("act", lambda: A.activation(out=w1[:], in_=w1[:], func=AF.Exp, scale=-0.5), reads=["w1"], writes=["w1"])
        yield
        P.op("pool", lambda: G.tensor_tensor(out=ysb[:], in0=ysb[:], in1=w1[:], op=ALU.mult),
             reads=["ysb", "w1"], writes=["ysb"])
        P.op("dve", lambda: V.tensor_scalar(out=ysb[:], in0=ysb[:], scalar1=vec[:, 9:10], scalar2=vec[:, 10:11],
                                            op0=ALU.mult, op1=ALU.add), reads=["ysb", "vec"], writes=["ysb"])
        yield
        P.op("pool", lambda: G.tensor_tensor(out=ysb[:], in0=ysb[:], in1=bonus[:], op=ALU.add),
             reads=["ysb", "bonus"], writes=["ysb"])
        P.op("dve", lambda: V.scalar_tensor_tensor(out=yao[b][:], in0=ysb[:], scalar=0.5, in1=sgaL[b][:], op0=ALU.mult,
                                                   op1=ALU.mult), reads=["ysb", f"sga{b}"], writes=[f"yao{b}"])
        P.dma("sp", y_dst(0, 128, j), yao[b][:], f"st_ya{b}", reads=[f"yao{b}"], writes=[f"yTa{b}", f"ytile{j}_a"])
        yield

    def run_stage(prims, bg, ratio):
        alive = list(prims)
        acc = 0.0
        while alive:
            for g in list(alive):
                try:
                    next(g)
                except StopIteration:
                    alive.remove(g)
            if bg[0] is not None:
                acc += ratio
                while acc >= 1.0 and bg[0] is not None:
                    acc -= 1.0
                    try:
                        next(bg[0])
                    except StopIteration:
                        bg[0] = None

    def run_tile(j, nxt):
        bg = [attn_tile(j) if do_fox else None]
        n_attn = 2 * (4 * j + 4) + 1
        drain = bool(nxt) and DRAIN_OVERLAP and n_attn >= DRAIN_MIN
        r = n_attn / (RW_ROUNDS + (14.0 if drain else 0.0))
        if do_rwkv:
            run_stage([rwkv_prep(j)], bg, r)
            run_stage([neumann(j, h_, c_) for c_ in range(4) for h_ in range(2)] + [chain(j, 0), chain(j, 1)] + ([nxt] if (nxt and OVERLAP_PROJ) else []), bg, r)
            run_stage([rwkv_fin(j)] + ([nxt] if (nxt and FIN_OVERLAP) else []), bg, r)
            if drain:
                run_stage([nxt], bg, r)
        elif nxt:
            run_stage([nxt], bg, r)
        while bg[0] is not None:
            try:
                next(bg[0])
            except StopIteration:
                bg[0] = None

    load_h(0)
    if NT > 1:
        load_h(1)
    for _ in tile_proj(0):
        pass
    for j in range(NT):
        nxt = tile_proj(j + 1) if j + 1 < NT else None
        if j + 2 < NT:
            load_h(j + 2)
        if OVERLAP_PROJ:
            run_tile(j, nxt)
        else:
            run_tile(j, nxt if (FIN_OVERLAP or DRAIN_OVERLAP) else None)
            if nxt:
                for _ in nxt:
                    pass
        if after_tile is not None:
            after_tile(j)
    P.wait_all("sp", ["yTa0", "yTa1", "yTb0", "yTb1"])


def build_mix(T, **kw):
    nc = bass.Bass("TRN2", target_bir_lowering=False)
    hT = nc.dram_tensor("hT", [1024, T], BF16, kind="ExternalInput").ap()
    wcore = nc.dram_tensor("wcore", [1024, NCOL], F32, kind="ExternalInput").ap()
    vecs = nc.dram_tensor("vecs", [128, 11], F32, kind="ExternalInput").ap()
    up = nc.dram_tensor("up", [128, 128], F32, kind="ExternalInput").ap()
    fbf = nc.dram_tensor("fbf", [2, 1], F32, kind="ExternalInput").ap()
    consts = nc.dram_tensor("consts", [128, NCONST], F32, kind="ExternalInput").ap()
    yT = nc.dram_tensor("yT", [256, T], BF16, kind="ExternalOutput").ap()
    P = Prog(nc)
    emit_mix(P, nc, T, hT, wcore, vecs, up, fbf, consts, yT, **kw)
    print("mix ops", P.n_ops, "waits", P.n_waits)
    P.close()
    return nc

import numpy as np

LN_EPS = 1e-5
ALPHA = float(4 ** 0.25)
K_ID, K_ONE = 0, 1152


def emit_tok(P, nc, NTOK, consts, c_in, *, x_in=None, embg=None, embb=None,
             yT=None, xprev=None, wout=None, wada_g=None, bada_g=None, lng=None, lnb=None,
             wada_f=None, bada_f=None, x_out=None, hT_out=None, pfx="t",
             y_src=None, y_keys=(), h_dst=None, after_h=None):
    do_back = (yT is not None) or (y_src is not None)
    do_front = (hT_out is not None) or (h_dst is not None)
    if do_back and y_src is None:
        y_src = lambda s: yT.rearrange("(c p) t -> p c t", p=128)[:, :, s * 512:(s + 1) * 512]
    if do_front and h_dst is None:
        h_dst = lambda s: hT_out.rearrange("(c p) t -> p c t", p=128)[:, :, s * 512:(s + 1) * 512]
    NS = NTOK // 512
    V, A, G, PE = nc.vector, nc.scalar, nc.gpsimd, nc.tensor
    k = lambda s: pfx + s

    cst = P.sb(k("cst"), [128, 1280], F32)
    P.dma("sp", cst[:], consts[:, :], k("ld_cst"), writes=[k("cst")])
    ident = cst[:, K_ID:K_ID + 128]
    ones = cst[:, K_ONE:K_ONE + 128]
    c_sb = P.sb(k("c_sb"), [128, 8], F32)
    P.dma("sp", c_sb[:], c_in.rearrange("(c p) -> p c", p=128), k("ld_c"), writes=[k("c_sb")],
          allow_slow_non_contiguous=True)
    cbc = P.sb(k("cbc"), [128, 8, 128], F32)
    for c in range(8):
        P.op("dve", lambda: V.tensor_scalar(out=cbc[:, c, :], in0=ones, scalar1=c_sb[:, c:c + 1], scalar2=None,
                                            op0=ALU.mult), reads=[k("cst"), k("c_sb")], writes=[k("cbc")])
    pp = [P.ps(k(f"pp{i}"), [128, 512]) for i in range(4)]
    ppc = [0]

    def nextpp():
        i = ppc[0] % 4
        ppc[0] += 1
        return pp[i], k(f"pp{i}")

    wad = [P.sb(k(f"wad{i}"), [128, 8, 512], F32) for i in range(2)]
    wadc = [0]

    def mod_bc(wada, bada, N, name):
        mt = P.sb(k(name), [128, N], F32)
        P.dma("sp", mt[:], bada.partition_broadcast(128), k("ld_" + name), writes=[k(name)])
        wv = wada.rearrange("(c p) n -> p c n", p=128)
        for n0 in range(0, N, 512):
            b = wadc[0] % 2
            wadc[0] += 1
            P.dma("sp", wad[b][:], wv[:, :, n0:n0 + 512], k(f"ld_wad{b}"), writes=[k(f"wad{b}")])
            ps, pk = nextpp()
            for c in range(8):
                P.op("pe", lambda: PE.matmul(ps[:, :], lhsT=cbc[:, c, :], rhs=wad[b][:, c, :], start=(c == 0),
                                             stop=(c == 7)), reads=[k("cbc"), k(f"wad{b}")], writes=[pk])
            P.op("dve", lambda: V.tensor_tensor(out=mt[:, n0:n0 + 512], in0=ps[:, :], in1=mt[:, n0:n0 + 512],
                                                op=ALU.add), reads=[pk, k(name)], writes=[k(name)])
        return mt

    def bc_load(vec_ap, name):
        t = P.sb(k(name), [128, 1024], F32)
        P.dma("sp", t[:], vec_ap.partition_broadcast(128), k("ld_" + name), writes=[k(name)])
        return t

    if do_back:
        mg = mod_bc(wada_g, bada_g, 1024, "mg")
        P.op("dve", lambda: V.tensor_scalar(out=mg[:], in0=mg[:], scalar1=1.0, scalar2=None, op0=ALU.add),
             reads=[k("mg")], writes=[k("mg")])
        wo = P.sb(k("wo"), [128, 8, 1024], BF16)
        wov = wout.rearrange("(c p) n -> p c n", p=128)
        for pi, n0 in enumerate(range(0, 1024, 512)):
            b = wadc[0] % 2
            wadc[0] += 1
            P.dma("sp", wad[b][:], wov[:, :, n0:n0 + 512], k(f"ld_wad{b}"), writes=[k(f"wad{b}")])
            for c in range(8):
                e = "dve" if c % 2 == 0 else "pool"
                eng = V if c % 2 == 0 else G
                P.op(e, lambda: eng.tensor_tensor(out=wo[:, c, n0:n0 + 512], in0=wad[b][:, c, :],
                                                  in1=mg[:, n0:n0 + 512], op=ALU.mult),
                     reads=[k(f"wad{b}"), k("mg")], writes=[k("wo")])
        g_bc = bc_load(lng, "lng")
        b_bc = bc_load(lnb, "lnb")
    else:
        g_bc = bc_load(embg, "embg")
        b_bc = bc_load(embb, "embb")
    if do_front:
        mf = mod_bc(wada_f, bada_f, 2048, "mf")
        P.op("dve", lambda: V.tensor_scalar(out=mf[:, 1024:2048], in0=mf[:, 1024:2048], scalar1=1.0, scalar2=None,
                                            op0=ALU.add), reads=[k("mf")], writes=[k("mf")])
        fm = P.sb(k("fm"), [128, 16], F32)
        for q in range(4):
            ps, pk = nextpp()
            for u in range(4):
                cc = q * 4 + u
                P.op("pe", lambda: PE.transpose(ps[:, u * 128:(u + 1) * 128], mf[:, cc * 128:(cc + 1) * 128], ident),
                     reads=[k("mf"), k("cst")], writes=[pk])
            P.op("dve", lambda: V.tensor_copy(out=fm[:, q * 4:q * 4 + 4],
                                              in_=ps[:, :].rearrange("p (c t) -> p c t", t=128)[:, :, 0]),
                 reads=[pk], writes=[k("fm")])

    xt = [P.sb(k(f"xt{i}"), [128, 4, 1024], F32) for i in range(2)]
    yt = [P.sb(k(f"yt{i}"), [128, 8, 512], BF16) for i in range(2)] if do_back else None
    ht = [P.sb(k(f"ht{i}"), [128, 8, 512], BF16) for i in range(2)] if do_front else None
    xsrc = xprev if do_back else x_in

    def load(s):
        b = s % 2
        P.dma("sp", xt[b][:], xsrc[s * 512:(s + 1) * 512, :].rearrange("(u p) d -> p u d", p=128),
              k(f"ld_x{b}"), writes=[k(f"xt{b}")] + [k(f"xt{b}_{u}") for u in range(4)])
        if do_back:
            P.dma("sp", yt[b][:], y_src(s), k(f"ld_y{b}"), reads=y_keys, writes=[k(f"yt{b}")])

    stats4 = [P.sb(k(f"stats4_{i}"), [128, 4, 2, 6], F32) for i in range(2)]
    mv4 = [P.sb(k(f"mv4_{i}"), [128, 4, 2], F32) for i in range(2)]
    rs4 = [P.sb(k(f"rs4_{i}"), [128, 4], F32) for i in range(2)]
    nb4 = [P.sb(k(f"nb4_{i}"), [128, 4], F32) for i in range(2)]
    load(0)
    for s in range(NS):
        b = s % 2
        if s + 1 < NS:
            load(s + 1)
        xk = k(f"xt{b}")
        xku = [k(f"xt{b}_{u}") for u in range(4)]
        for u in range(4):
            xs = xt[b][:, u, :]
            if do_back:
                for n in range(2):
                    ps, pk = nextpp()
                    for c in range(8):
                        P.op("pe", lambda: PE.matmul(ps[:, :], lhsT=yt[b][:, c, u * 128:(u + 1) * 128],
                                                     rhs=wo[:, c, n * 512:(n + 1) * 512], start=(c == 0),
                                                     stop=(c == 7)), reads=[k(f"yt{b}"), k("wo")], writes=[pk])
                    P.op("dve", lambda: V.scalar_tensor_tensor(out=xs[:, n * 512:(n + 1) * 512],
                                                               in0=xs[:, n * 512:(n + 1) * 512], scalar=ALPHA,
                                                               in1=ps[:, :], op0=ALU.mult, op1=ALU.add),
                         reads=[xk, xku[u], pk], writes=[xku[u]])
            for n in range(2):
                P.op("dve", lambda: V.bn_stats(out=stats4[b][:, u, n, :], in_=xs[:, n * 512:(n + 1) * 512]),
                     reads=[xk, xku[u]], writes=[k(f"st4_{b}_{u}")])
            P.op("dve", lambda: V.bn_aggr(out=mv4[b][:, u, :], in_=stats4[b][:, u, :, :].rearrange("p a b -> p (a b)")),
                 reads=[k(f"st4_{b}_{u}")], writes=[k(f"mv4_{b}")])
        P.op("dve", lambda: V.tensor_scalar(out=rs4[b][:], in0=mv4[b][:, :, 1], scalar1=LN_EPS, scalar2=None,
                                            op0=ALU.add), reads=[k(f"mv4_{b}")], writes=[k(f"rs4_{b}")])
        P.op("act", lambda: A.activation(out=rs4[b][:], in_=rs4[b][:], func=AF.Sqrt), reads=[k(f"rs4_{b}")],
             writes=[k(f"rs4_{b}")])
        P.op("dve", lambda: V.reciprocal(out=rs4[b][:], in_=rs4[b][:]), reads=[k(f"rs4_{b}")], writes=[k(f"rs4_{b}")])
        P.op("dve", lambda: V.scalar_tensor_tensor(out=nb4[b][:], in0=mv4[b][:, :, 0], scalar=-1.0, in1=rs4[b][:],
                                                   op0=ALU.mult, op1=ALU.mult),
             reads=[k(f"mv4_{b}"), k(f"rs4_{b}")], writes=[k(f"nb4_{b}")])
        for u in range(4):
            xs = xt[b][:, u, :]
            P.op("act", lambda: A.activation(out=xs, in_=xs, func=AF.Identity, scale=rs4[b][:, u:u + 1],
                                             bias=nb4[b][:, u:u + 1]),
                 reads=[xk, xku[u], k(f"rs4_{b}"), k(f"nb4_{b}")], writes=[xku[u]])
            P.op("dve", lambda: V.tensor_tensor(out=xs, in0=xs, in1=g_bc[:], op=ALU.mult),
                 reads=[xku[u], k("lng"), k("embg")], writes=[xku[u]])
            P.op("pool", lambda: G.tensor_tensor(out=xs, in0=xs, in1=b_bc[:], op=ALU.add),
                 reads=[xku[u], k("lnb"), k("embb")], writes=[xku[u]])
        if x_out is not None:
            P.dma("sp", x_out[s * 512:(s + 1) * 512, :].rearrange("(u p) d -> p u d", p=128), xt[b][:],
                  k(f"st_x{b}"), reads=[xk] + xku, writes=[k(f"xo{b}")])
        if do_front:
            hk = k(f"ht{b}")
            for c in range(8):
                ps, pk = nextpp()
                for u in range(4):
                    P.op("pe", lambda: PE.transpose(ps[:, u * 128:(u + 1) * 128], xt[b][:, u, c * 128:(c + 1) * 128],
                                                    ident), reads=[xk, xku[u], k("cst")], writes=[pk])
                P.op("act", lambda: A.activation(out=ht[b][:, c, :], in_=ps[:, :], func=AF.Identity,
                                                 scale=fm[:, 8 + c:9 + c], bias=fm[:, c:c + 1]),
                     reads=[pk, k("fm")], writes=[hk])
            P.dma("sp", h_dst(s), ht[b][:], k(f"st_h{b}"), reads=[hk], writes=[k(f"ho{b}"), f"htile{s}"])
            if after_h is not None:
                after_h(s)
    P.wait_all("sp", [k("xo0"), k("xo1"), k("ho0"), k("ho1")])


def build_tok(NTOK, mode):
    nc = bass.Bass("TRN2", target_bir_lowering=False)
    dt = lambda n, s, d=F32, kind="ExternalInput": nc.dram_tensor(n, s, d, kind=kind).ap()
    consts = dt("consts", [128, 1280])
    c_in = dt("c", [1024])
    kw = {}
    if mode == "pre":
        kw.update(x_in=dt("x", [NTOK, 1024]), embg=dt("embg", [1024]), embb=dt("embb", [1024]))
    else:
        kw.update(yT=dt("yT", [1024, NTOK], BF16), xprev=dt("xprev", [NTOK, 1024]), wout=dt("wout", [1024, 1024]),
                  wada_g=dt("wada_g", [1024, 1024]), bada_g=dt("bada_g", [1024]), lng=dt("lng", [1024]),
                  lnb=dt("lnb", [1024]))
    if mode != "post":
        kw.update(wada_f=dt("wada_f", [1024, 2048]), bada_f=dt("bada_f", [2048]),
                  hT_out=dt("hT", [1024, NTOK], BF16, kind="ExternalOutput"))
    kw.update(x_out=dt("xo", [NTOK, 1024], kind="ExternalOutput"))
    P = Prog(nc)
    emit_tok(P, nc, NTOK, consts, c_in, **kw)
    print("tok", mode, "ops", P.n_ops, "waits", P.n_waits)
    P.close()
    return nc

import numpy as np

D = 1024
RWW = 512
RW_R0, RW_K0, RW_V0, RW_WD0, RW_AD0, RW_END = 0, 512, 1024, 1536, 1600, 1664
FX_Q0, FX_K0, FX_V0, FX_F0, FX_END = 1664, 2176, 2688, 3200, 3208
GATE0 = 3208


def core_cols(g):
    ch = np.arange(128 * g, 128 * g + 128)
    l64 = np.arange(64)
    return np.concatenate([
        RW_R0 + ch, RW_K0 + ch, RW_V0 + ch, RW_WD0 + l64, RW_AD0 + l64, GATE0 + ch,
        FX_Q0 + ch, FX_K0 + ch,
        GATE0 + 512 + 128 * g + l64, GATE0 + 512 + 128 * g + 64 + l64,
        FX_V0 + ch, FX_F0 + np.array([2 * g, 2 * g + 1])])


def pack_core(inp, l, g):
    ch = np.arange(128 * g, 128 * g + 128)
    l64 = np.arange(64)
    wcore = np.ascontiguousarray(inp["w_in"][l][:, core_cols(g)])
    mix = inp["rwkv_mix"][l]
    vecs = np.stack([
        mix[RW_R0 + ch], mix[RW_K0 + ch], mix[RW_V0 + ch],
        np.concatenate([mix[RW_WD0 + l64], mix[RW_AD0 + l64]]),
        inp["w0"][l][ch], inp["a0"][l][ch], inp["k_k"][l][ch], inp["k_a"][l][ch],
        inp["r_k"][l][ch], inp["gn_g"][l][ch], inp["gn_b"][l][ch]], axis=1).astype(np.float32)
    up = np.concatenate([inp["w_up"][l][:, ch], inp["a_up"][l][:, ch]], axis=0).astype(np.float32)
    fbf = inp["fox_bf"][l][[2 * g, 2 * g + 1]][:, None].astype(np.float32)
    return dict(wcore=wcore, vecs=np.ascontiguousarray(vecs), up=np.ascontiguousarray(up),
                fbf=np.ascontiguousarray(fbf))


T_SEQ = 16384
NTOK = 4096
RG = [[0, 1, 2, 3], [4, 5, 6, 7]]
_NC_CACHE = {}


def build_fused(T=T_SEQ, NT=NTOK):
    nc = bass.Bass("TRN2", target_bir_lowering=False)
    dt = lambda n, s, d=F32, kind="ExternalInput": nc.dram_tensor(n, s, d, kind=kind).ap()
    NS = NT // 512
    NK = T // 2048
    consts = dt("consts", [128, 1280])
    c_in = dt("c", [1024])
    x_in = dt("x", [NT, 1024])
    embg = dt("embg", [1024])
    embb = dt("embb", [1024])
    L = []
    for l in range(2):
        L.append(dict(
            wada_f=dt(f"wada_f{l}", [1024, 2048]), bada_f=dt(f"bada_f{l}", [2048]),
            wada_g=dt(f"wada_g{l}", [1024, 1024]), bada_g=dt(f"bada_g{l}", [1024]),
            wout=dt(f"wout{l}", [1024, 1024]), lng=dt(f"lng{l}", [1024]), lnb=dt(f"lnb{l}", [1024]),
            wcore=dt(f"wcore{l}", [1024, NCOL]), vecs=dt(f"vecs{l}", [128, 11]), up=dt(f"up{l}", [128, 128]),
            fbf=dt(f"fbf{l}", [2, 1])))
    xo = dt("xo", [NT, 1024], kind="ExternalOutput")
    xd = [nc.dram_tensor(f"xd{l}", [NT, 1024], F32).ap() for l in range(2)]
    hloc = [nc.dram_tensor(f"hloc{l}", [NS, 1024, 512], BF16).ap() for l in range(2)]
    hg = [nc.dram_tensor(f"hg{l}", [NS, 4, 1024, 512], BF16).ap() for l in range(2)]
    yc = [nc.dram_tensor(f"yc{l}", [NK, 256, 2048], BF16).ap() for l in range(2)]
    yg = [nc.dram_tensor(f"yg{l}", [NK, 4, 256, 2048], BF16).ap() for l in range(2)]
    q = nc.partition_id() % 4
    P = Prog(nc)

    def front_hooks(l):
        h_dst = lambda s: hloc[l][s].rearrange("(c p) t -> p c t", p=128)
        after_h = lambda s: P.cc("AllGather", hloc[l][s].opt(), hg[l][s].opt(), RG,
                                 reads=[f"htile{s}"], writes=[f"hg{s}"])
        return dict(h_dst=h_dst, after_h=after_h, wada_f=L[l]["wada_f"], bada_f=L[l]["bada_f"])

    P.begin_scope()
    emit_tok(P, nc, NT, consts, c_in, x_in=x_in, embg=embg, embb=embb, x_out=xd[0], pfx="a", **front_hooks(0))
    P.end_scope()
    for l in range(2):
        P.begin_scope()

        def after_tile(j, l=l):
            if j % 4 == 3:
                kk = j // 4
                keys = []
                for jj in range(j - 3, j + 1):
                    keys += [f"ytile{jj}_a", f"ytile{jj}_b0", f"ytile{jj}_b1"]
                P.cc("AllGather", yc[l][kk].opt(), yg[l][kk].opt(), RG, reads=keys, writes=[f"yg{kk}"])

        emit_mix(P, nc, T, None, L[l]["wcore"], L[l]["vecs"], L[l]["up"], L[l]["fbf"], consts, None,
                 h_src=lambda j, l=l: hg[l][j % NS, j // NS].rearrange("(c p) t -> p c t", p=128),
                 y_dst=lambda r0, r1, j, l=l: yc[l][j // 4, r0:r1, (j % 4) * 512:(j % 4 + 1) * 512],
                 after_tile=after_tile, h_keys=lambda j: [f"hg{j % NS}"])
        P.end_scope()
        P.begin_scope()

        def y_src(s, l=l):
            v = yg[l][bass.ds(2 * q + s // 4, 1)]
            return v.rearrange("o r (h p) t -> p (o r h) t", p=128)[:, :, (s % 4) * 512:(s % 4 + 1) * 512]

        kw = dict(y_src=y_src, y_keys=[f"yg{k_}" for k_ in range(NK)], xprev=xd[l], wout=L[l]["wout"], wada_g=L[l]["wada_g"], bada_g=L[l]["bada_g"],
                  lng=L[l]["lng"], lnb=L[l]["lnb"], pfx=f"b{l}")
        if l == 0:
            kw.update(front_hooks(1))
            kw.update(x_out=xd[1])
        else:
            kw.update(x_out=xo)
        emit_tok(P, nc, NT, consts, c_in, **kw)
        P.end_scope()
    print("fused ops", P.n_ops, "waits", P.n_waits)
    P.close()
    return nc


def wout_perm():
    idx = []
    for r in range(4):
        idx += list(range(128 * r, 128 * r + 128))
        idx += list(range(512 + 128 * r, 512 + 128 * r + 128))
    return np.array(idx)


def kernel(**inp):
    inp = {k: np.ascontiguousarray(np.asarray(v)) for k, v in inp.items()}
    consts = make_consts()
    ca = np.ascontiguousarray
    if "nc" not in _NC_CACHE:
        _NC_CACHE["nc"] = build_fused()
    nc = _NC_CACHE["nc"]
    perm = wout_perm()
    maps = []
    for core in range(8):
        b, g = core // 4, core % 4
        m = dict(consts=consts, c=ca(inp["c"][b]), x=ca(inp["x"][b][g * NTOK:(g + 1) * NTOK]),
                 embg=inp["emb_ln_g"], embb=inp["emb_ln_b"])
        for l in range(2):
            pk = pack_core(inp, l, g)
            m.update({f"wada_f{l}": ca(inp["w_ada"][l][:, 0:2048]), f"bada_f{l}": ca(inp["b_ada"][l][0:2048]),
                      f"wada_g{l}": ca(inp["w_ada"][l][:, 2048:3072]), f"bada_g{l}": ca(inp["b_ada"][l][2048:3072]),
                      f"wout{l}": ca(inp["w_out"][l][perm]), f"lng{l}": inp["ln_g"][l], f"lnb{l}": inp["ln_b"][l],
                      f"wcore{l}": pk["wcore"], f"vecs{l}": pk["vecs"], f"up{l}": pk["up"], f"fbf{l}": pk["fbf"]})
        maps.append(m)
    res = run_bass_kernel_spmd(nc, maps, core_ids=list(range(8))).results
    out = np.stack([np.concatenate([res[b * 4 + g]["xo"] for g in range(4)], axis=0) for b in range(2)], axis=0)
    return np.asarray(out, dtype=np.float32)
```

```python
import contextlib
import numpy as np
import concourse.bass as bass
import concourse.mybir as mybir
from concourse.bass_utils import run_bass_kernel_spmd

F32 = mybir.dt.float32
BF16 = mybir.dt.bfloat16
AF = mybir.ActivationFunctionType
ALU = mybir.AluOpType
AX = mybir.AxisListType

SEM_EPOCH = 20000


class Prog:
    def __init__(self, nc):
        self.nc = nc
        self.es = contextlib.ExitStack()
        self.eng = {"pe": nc.tensor, "act": nc.scalar, "dve": nc.vector,
                    "pool": nc.gpsimd, "sp": nc.sync}
        self.cnt = {e: 0 for e in self.eng}
        self.epoch = {e: 0 for e in self.eng}
        self.esem = {}
        for e in self.eng:
            self.esem[e] = self._newsem(f"s_{e}_0")
        self.seen = {e: {} for e in self.eng}
        self.sems = {}
        self.dcnt = {}
        self.lastw = {}
        self.reads = {}
        self.n_ops = 0
        self.n_waits = 0
        self.tes = self.es
        self.scope_id = 0
        self.ccsem = self._newsem("s_cc")
        self.ccn = 0
        self.kalias = {}

    def _newsem(self, name):
        return self.es.enter_context(self.nc.semaphore(name))

    def sb(self, name, shape, dt):
        return self.tes.enter_context(self.nc.sbuf_tensor(f"s{self.scope_id}_{name}", list(shape), dt))

    def ps(self, name, shape, dt=F32):
        return self.tes.enter_context(self.nc.psum_tensor(f"s{self.scope_id}_{name}", list(shape), dt))

    def begin_scope(self):
        self.scope_id += 1
        self.tes = contextlib.ExitStack()

    def end_scope(self):
        self.barrier()
        self.tes.close()
        self.tes = self.es

    def barrier(self):
        toks = []
        for f in self.eng:
            if self.cnt[f] > 0:
                toks.append((self.esem[f], self.cnt[f], f))
        for name, sem in self.sems.items():
            if self.dcnt[name] > 0:
                toks.append((sem, self.dcnt[name], "dma"))
        for e in self.eng:
            need = {id(sm): (sm, v) for (sm, v, o) in toks if o != e}
            self._emit_waits(e, need)
        keep = {k: t for k, t in self.lastw.items() if t[2] == "cc"}
        self.lastw.clear()
        self.reads.clear()
        self.lastw.update(keep)

    def cc(self, kind, in_ap, out_ap, rg, reads=(), writes=()):
        need = self._need("pool", reads, writes)
        self._emit_waits("pool", need)
        ins = self.nc.gpsimd.collective_compute(kind, ALU.bypass, replica_groups=rg, ins=[in_ap], outs=[out_ap])
        self.ccn += 1
        ins.then_inc(self.ccsem, 1)
        tok = (self.ccsem, self.ccn, "cc")
        self._commit("cc", tok, reads, writes)
        self.n_ops += 1
        return tok

    def close(self):
        if self.ccn > 0:
            self._emit_waits("pool", {id(self.ccsem): (self.ccsem, self.ccn)})
        self.es.close()

    def _need(self, e, reads, writes):
        need = {}

        def add(tok, kind):
            if tok is None:
                return
            sem, val, owner = tok
            if owner == e:
                if e in ("pe", "sp"):
                    return
                if kind == "war":
                    return
            k = id(sem)
            if k not in need or need[k][1] < val:
                need[k] = (sem, val)

        for r in reads:
            add(self.lastw.get(r), "raw")
        for w in writes:
            add(self.lastw.get(w), "waw")
            for tok in self.reads.get(w, {}).values():
                add(tok, "war")
        return need

    def _emit_waits(self, e, need):
        eng = self.eng[e]
        seen = self.seen[e]
        for k, (sem, val) in need.items():
            if seen.get(k, 0) >= val:
                continue
            eng.wait_ge(sem, val)
            seen[k] = val
            self.n_waits += 1

    def _commit(self, e, tok, reads, writes):
        for w in writes:
            self.lastw[w] = tok
            self.reads[w] = {}
        for r in reads:
            self.reads.setdefault(r, {})[(e, id(tok[0]))] = tok

    @staticmethod
    def _is_psum(k):
        return isinstance(k, str) and (k.startswith("ps") or "pp" in k)

    def alias(self, a, b):
        self.kalias[a] = b

    def _ka(self, keys):
        if not self.kalias:
            return keys
        return [self.kalias.get(k, k) for k in keys]

    def op(self, e, fn, reads=(), writes=()):
        reads, writes = self._ka(reads), self._ka(writes)
        px = [k for k in reads if self._is_psum(k)]
        if px:
            reads = [k for k in reads if not self._is_psum(k)]
            writes = list(writes) + px
        need = self._need(e, reads, writes)
        self._emit_waits(e, need)
        if self.cnt[e] >= SEM_EPOCH:
            self.epoch[e] += 1
            self.esem[e] = self._newsem(f"s_{e}_{self.epoch[e]}")
            self.cnt[e] = 0
        ins = fn()
        self.cnt[e] += 1
        ins.then_inc(self.esem[e], 1)
        tok = (self.esem[e], self.cnt[e], e)
        self._commit(e, tok, reads, writes)
        self.n_ops += 1
        return tok

    def dma(self, q, out, in_, dsem, reads=(), writes=(), **kw):
        reads, writes = self._ka(reads), self._ka(writes)
        need = self._need(q, reads, writes)
        self._emit_waits(q, need)
        if dsem not in self.sems:
            self.sems[dsem] = self._newsem("d_" + dsem)
            self.dcnt[dsem] = 0
        sem = self.sems[dsem]
        ins = self.eng[q].dma_start(out=out, in_=in_, **kw)
        self.dcnt[dsem] += 16
        ins.then_inc(sem, 16)
        tok = (sem, self.dcnt[dsem], "dma")
        self._commit("dma", tok, reads, writes)
        self.n_ops += 1
        return tok

    def wait_all(self, e, keys):
        need = {}
        for k in keys:
            tok = self.lastw.get(k)
            if tok is None:
                continue
            kk = id(tok[0])
            if kk not in need or need[kk][1] < tok[1]:
                need[kk] = (tok[0], tok[1])
        self._emit_waits(e, need)

import numpy as np

NCOL = 1154
C_R, C_K, C_V, C_WA, C_GA, C_FQ, C_FK = 0, 128, 256, 384, 512, 640, 768
C_GB0, C_GB1, C_FV, C_F = 896, 960, 1024, 1152
K_ID, K_BO, K_SU, K_UI, K_SL, K_RS, K_ONE = 0, 128, 256, 384, 512, 640, 1152
NCONST = 1280
GN_EPS = 64e-5
RW_DT = BF16
RW_ROUNDS = 41.0
OVERLAP_PROJ = False
NEU_BANKS = 3
FIN_OVERLAP = False
DRAIN_OVERLAP = False
DRAIN_MIN = 100


def make_consts():
    c = np.zeros((128, NCONST), np.float32)
    i = np.arange(128)
    c[:, K_ID:K_ID + 128] = np.eye(128)
    c[:, K_BO:K_BO + 128] = (i[:, None] // 64 == i[None, :] // 64)
    c[:, K_SU:K_SU + 128] = (i[:, None] < i[None, :])
    c[:, K_UI:K_UI + 128] = (i[:, None] <= i[None, :])
    c[:, K_SL:K_SL + 128] = (i[:, None] > i[None, :])
    rs = np.ones(512, np.float32)
    rs[::128] = 0
    c[:, K_RS:K_RS + 512] = rs[None, :]
    c[:, K_ONE:K_ONE + 128] = 1.0
    return c


def interleave(gens, weights):
    gens = list(gens)
    acc = [0.0] * len(gens)
    alive = [True] * len(gens)
    while any(alive):
        for i, g in enumerate(gens):
            if not alive[i]:
                continue
            acc[i] += weights[i]
            while acc[i] >= 1.0 and alive[i]:
                acc[i] -= 1.0
                try:
                    next(g)
                except StopIteration:
                    alive[i] = False


def emit_mix(P, nc, T, hT, wcore, vecs, up, fbf, consts, yT, do_rwkv=True, do_fox=True,
             h_src=None, y_dst=None, after_tile=None, h_keys=None):
    NT = T // 512
    NB = T // 128
    V = nc.vector
    A = nc.scalar
    G = nc.gpsimd
    PE = nc.tensor

    cst = P.sb("cst", [128, NCONST], F32)
    vec = P.sb("vec", [128, 20], F32)
    fbh = P.sb("fbh", [2, 1], F32)
    upt = P.sb("upt", [128, 128], F32)
    fbt = P.sb("fbt", [2, 1], F32)
    P.dma("sp", cst[:], consts[:, :], "ld_cst", writes=["cst"])
    P.dma("sp", vec[:, 0:11], vecs[:, :], "ld_vec", writes=["vec"])
    P.dma("sp", upt[:], up[:, :], "ld_upt", writes=["upt"])
    P.dma("sp", fbt[:], fbf[:, :], "ld_fbt", writes=["fbt"])
    ident = cst[:, K_ID:K_ID + 128]
    bones = cst[:, K_BO:K_BO + 128]
    mask2 = cst[:, K_SU:K_SU + 256]
    msl = cst[:, K_SL:K_SL + 128]
    rsm = cst[:, K_RS:K_RS + 512]
    ones = cst[:, K_ONE:K_ONE + 128]
    P.op("dve", lambda: V.tensor_scalar(out=vec[:, 11:15], in0=vec[:, 0:4], scalar1=-1.0, scalar2=1.0,
                                        op0=ALU.mult, op1=ALU.add), reads=["vec"], writes=["vec"])
    P.op("dve", lambda: V.tensor_scalar(out=vec[:, 15:16], in0=vec[:, 7:8], scalar1=-1.0, scalar2=1.0,
                                        op0=ALU.mult, op1=ALU.add), reads=["vec"], writes=["vec"])
    P.op("dve", lambda: V.tensor_scalar(out=vec[:, 16:18], in0=vec[:, 4:6], scalar1=0.5, scalar2=None,
                                        op0=ALU.mult), reads=["vec"], writes=["vec"])
    P.op("dve", lambda: V.tensor_scalar(out=fbh[:], in0=fbt[:], scalar1=0.5, scalar2=None, op0=ALU.mult),
         reads=["fbt"], writes=["fbh"])
    uib = P.sb("uib", [128, 128], BF16)
    idb = P.sb("idb", [128, 128], BF16)
    P.op("dve", lambda: V.tensor_copy(out=idb[:], in_=cst[:, K_ID:K_ID + 128]), reads=["cst"], writes=["idb"])
    P.op("dve", lambda: V.tensor_copy(out=uib[:], in_=cst[:, K_UI:K_UI + 128]), reads=["cst"], writes=["uib"])

    wsb = P.sb("wsb", [128, 8, NCOL], BF16)
    wst = [P.sb(f"wst{i}", [128, 8, 64], F32) for i in range(2)]
    wv = wcore.rearrange("(c p) n -> p c n", p=128)
    pieces = [(s, min(64, NCOL - s)) for s in range(0, NCOL, 64)]
    for pi, (s, n) in enumerate(pieces):
        b = pi % 2
        P.dma("sp", wst[b][:, :, 0:n], wv[:, :, s:s + n], f"ld_w{b}", writes=[f"wst{b}"])
        eng = "act" if pi % 2 == 0 else "dve"
        if eng == "act":
            P.op("act", lambda: A.copy(out=wsb[:, :, s:s + n], in_=wst[b][:, :, 0:n]),
                 reads=[f"wst{b}"], writes=["wsb"])
        else:
            P.op("dve", lambda: V.tensor_copy(out=wsb[:, :, s:s + n], in_=wst[b][:, :, 0:n]),
                 reads=[f"wst{b}"], writes=["wsb"])

    hTt = [P.sb(f"hT{i}", [128, 8, 512], BF16) for i in range(2)]
    if h_src is None:
        hv = hT.rearrange("(c p) t -> p c t", p=128)
        h_src = lambda j: hv[:, :, j * 512:(j + 1) * 512]
    if y_dst is None:
        y_dst = lambda r0, r1, j: yT[r0:r1, j * 512:(j + 1) * 512]
    NSB = 3 if OVERLAP_PROJ else (6 - NEU_BANKS)
    NPT = 4
    LOOK = 2 if OVERLAP_PROJ else (5 - NEU_BANKS)
    psS = [P.ps(f"psS{i}", [128, 512]) for i in range(NSB)]
    psO = P.ps("psO", [128, 512])
    psA = [P.ps(f"psA{i}", [128, 512]) for i in range(NEU_BANKS)]
    ppb = P.ps("ppb", [128, 512]) if OVERLAP_PROJ else None
    psY = P.ps("psY", [128, 512])
    ppc = [0]

    def nextpp():
        if OVERLAP_PROJ:
            return ppb, "ppb"
        i = ppc[0] % 2
        ppc[0] += 1
        return psA[i], f"psA{i}"

    KT = P.sb("KT", [128, T], BF16)
    Vaug = P.sb("Vaug", [128, NB, 2, 65], BF16)
    ckres = P.sb("ckres", [128, NB, 2], F32)
    P.op("pool", lambda: G.memset(Vaug[:, :, :, 64:65], 1.0), writes=["Vaug_ones"])
    QT = [[P.sb(f"QT{i}_{h}", [128, 512], BF16) for h in range(2)] for i in range(2)]
    for i in range(2):
        for h in range(2):
            P.op("pool", lambda: G.memset(QT[i][h][:], 0.0), writes=[f"QT{i}"])
    sgb = [[P.sb(f"sgb{i}_{h}", [64, 512], F32) for h in range(2)] for i in range(2)]
    rrow = P.sb("rrow", [128, 512], F32)
    lft = P.sb("lft", [2, 512], F32)
    lf = lft[:, :]
    onesrow = rrow[0:2, :]
    cum = [P.sb(f"cum{i}", [2, 512], F32) for i in range(2)]
    P.op("dve", lambda: V.memset(cum[1][:], 0.0), writes=["cum1"])
    i2 = P.sb("i2", [2, 2], F32)
    basebc = [P.sb(f"basebc{i}", [128, 2], F32) for i in range(2)]
    biasj = [P.sb(f"biasj{i}", [128, NB, 2], F32) for i in range(2)]
    PT = [P.sb(f"PT{i}", [128, 512], BF16) for i in range(NPT)]
    t1 = P.sb("t1", [64, 512], F32)
    osb = P.sb("osb", [65, 512], F32)
    ybo = [P.sb(f"ybo{i}", [64, 512], BF16) for i in range(2)]

    raw = {g: P.sb(f"raw_{g}", [128, 513], F32) for g in "rkvw"}
    for g in "rkvw":
        P.op("pool", lambda: G.memset(raw[g][:, 0:1], 0.0), writes=[f"raw_{g}c"])
    tmp = P.sb("tmp", [128, 512], F32)
    sh = {g: P.sb(f"sh_{g}", [128, 512], F32) for g in "rkvw"}
    a_t = P.sb("a_t", [128, 512], F32)
    ld = P.sb("ld", [128, 512], F32)
    cl = P.sb("cl", [128, 512], F32)
    einc = P.sb("einc", [128, 512], F32)
    eneg = P.sb("eneg", [128, 512], F32)
    kk = P.sb("kk", [128, 512], F32)
    kmod = tmp
    P.alias("kmod", "tmp")
    w1 = P.sb("w1", [128, 512], F32)
    w2 = P.sb("w2", [128, 512], F32)
    KRz = [P.sb(f"KRz{i}", [128, 4, 2, 128], RW_DT) for i in range(2)]
    for i in range(2):
        P.op("pool", lambda: G.memset(KRz[i][:], 0.0), writes=["KR"])
    Bt = P.sb("Bt", [128, 512], RW_DT)
    Kt = P.sb("Kt", [128, 512], RW_DT)
    bonus = ld
    P.alias("bonus", "ld")
    sgaL = [P.sb(f"sga{i}", [128, 512], F32) for i in range(2)]
    Vtok = P.sb("Vtok", [128, 4, 128], RW_DT)
    Btok = P.sb("Btok", [128, 4, 128], RW_DT)
    Ktok = P.sb("Ktok", [128, 4, 128], RW_DT)
    ST = P.sb("ST", [128, 64], F32)
    STw = P.sb("STw", [128, 64], F32)
    STb = P.sb("STb", [128, 128], RW_DT)
    P.op("dve", lambda: V.memset(ST[:], 0.0), writes=["ST0", "ST1"])
    P.op("dve", lambda: V.memset(STb[:], 0.0), writes=["STb0", "STb1"])
    AT = [[P.sb(f"AT{i}_{k}", [128, 256], RW_DT) for k in range(4)] for i in range(2)]
    AK = [[P.sb(f"AK{i}_{k}", [128, 256], RW_DT) for k in range(4)] for i in range(2)]
    XL = [[P.sb(f"XL{i}_{k}", [128, 384], RW_DT) for k in range(4)] for i in range(2)]
    Tm = [[XL[i][k][:, 256:384] for k in range(4)] for i in range(2)]
    TmF = Tm
    Zs = P.sb("Zs", [128, 128], RW_DT)
    Us = P.sb("Us", [128, 128], RW_DT)
    P.op("pool", lambda: G.memset(Us[:], 0.0), writes=["Us0", "Us1"])
    P.op("pool", lambda: G.memset(Zs[:], 0.0), writes=["Zs0", "Zs1"])
    ysb = cl
    for k_ in ("ysb", "ysb0", "ysb1"):
        P.alias(k_, "cl")
    yao = [P.sb(f"yao{i}", [128, 512], BF16) for i in range(2)]

    def proj_group(j, col, M, rhs_tile, rhs_key):
        ps, pk = nextpp()
        for c in range(8):
            P.op("pe", lambda: PE.matmul(ps[0:M, :], lhsT=wsb[:, c, col:col + M], rhs=rhs_tile[:, c, :],
                                         start=(c == 0), stop=(c == 7)),
                 reads=["wsb", rhs_key], writes=[pk])
        return ps, pk

    def load_h(j):
        b = j % 2
        P.dma("sp", hTt[b][:], h_src(j), f"ld_h{b}", reads=(h_keys(j) if h_keys else ()), writes=[f"hT{b}"])

    def tile_proj(j):
        b = j % 2
        ht, hk = hTt[b], f"hT{b}"
        sga = sgaL[b]
        sgak = f"sga{b}"
        if do_rwkv:
            for gi, (g, col) in enumerate((("r", C_R), ("k", C_K), ("v", C_V), ("w", C_WA))):
                ps, pk = proj_group(j, col, 128, ht, hk)
                rw = raw[g]
                if j > 0:
                    P.op("act", lambda: A.copy(out=rw[:, 0:1], in_=rw[:, 512:513]), reads=[f"raw_{g}"],
                         writes=[f"raw_{g}c"])
                P.op("dve", lambda: V.tensor_copy(out=rw[:, 1:513], in_=ps[:, :]), reads=[pk], writes=[f"raw_{g}"])
                P.op("act", lambda: A.activation(out=tmp[:], in_=ps[:, :], func=AF.Identity,
                                                 scale=vec[:, 11 + gi:12 + gi]),
                     reads=[pk, "vec"], writes=["tmp"])
                P.op("dve", lambda: V.scalar_tensor_tensor(out=sh[g][:], in0=rw[:, 0:512], scalar=vec[:, gi:gi + 1],
                                                           in1=tmp[:], op0=ALU.mult, op1=ALU.add),
                     reads=[f"raw_{g}", f"raw_{g}c", "tmp", "vec"], writes=[f"sh_{g}"])
                yield
            ps, pk = proj_group(j, C_GA, 128, ht, hk)
            P.op("act", lambda: A.activation(out=sga[:], in_=ps[:, :], func=AF.Tanh, scale=0.5), reads=[pk],
                 writes=[sgak])
            P.op("dve", lambda: V.scalar_tensor_tensor(out=sga[:], in0=sga[:], scalar=1.0, in1=ps[:, :],
                                                       op0=ALU.add, op1=ALU.mult), reads=[pk, "sga"], writes=[sgak])
        if do_fox:
            ps, pk = proj_group(j, C_FQ, 128, ht, hk)
            for h in range(2):
                hp_ = slice(64 * h, 64 * h + 64)
                P.op("act", lambda: A.activation(out=QT[b][h][hp_, :], in_=ps[hp_, :], func=AF.Copy, scale=0.125),
                     reads=[pk], writes=[f"QT{b}"])
            yield
            ps, pk = proj_group(j, C_FK, 128, ht, hk)
            P.op("dve", lambda: V.tensor_copy(out=KT[:, j * 512:(j + 1) * 512], in_=ps[:, :]),
                 reads=[pk], writes=[f"KT{j}"])
            yield
            for h, col in ((0, C_GB0), (1, C_GB1)):
                ps, pk = proj_group(j, col, 64, ht, hk)
                P.op("act", lambda: A.activation(out=sgb[b][h][:], in_=ps[0:64, :], func=AF.Tanh, scale=0.5),
                     reads=[pk], writes=[f"sgb{b}_{h}"])
                P.op("dve", lambda: V.scalar_tensor_tensor(out=sgb[b][h][:], in0=sgb[b][h][:], scalar=1.0,
                                                           in1=ps[0:64, :], op0=ALU.add, op1=ALU.mult),
                     reads=[pk, f"sgb{b}_{h}"], writes=[f"sgb{b}_{h}"])
                yield
            ps, pk = nextpp()
            for q in range(4):
                for c in range(8):
                    P.op("pe", lambda: PE.matmul(ps[:, q * 128:(q + 1) * 128], lhsT=ht[:, c, q * 128:(q + 1) * 128],
                                                 rhs=wsb[:, c, C_FV:C_FV + 128], start=(c == 0), stop=(c == 7)),
                         reads=["wsb", hk], writes=[pk])
            P.op("dve", lambda: V.tensor_copy(
                out=Vaug[:, 4 * j:4 * j + 4, :, 0:64],
                in_=ps[:, :].rearrange("p (q h d) -> p q h d", q=4, h=2)),
                reads=[pk], writes=[f"V{j}"])
            yield
            ps, pk = proj_group(j, C_F, 2, ht, hk)
            P.op("act", lambda: A.activation(out=lf, in_=ps[0:2, :], func=AF.Tanh, bias=fbh[:, 0:1], scale=0.5),
                 reads=[pk, "fbh"], writes=["lf"])
            P.op("dve", lambda: V.tensor_scalar(out=lf, in0=lf, scalar1=0.5, scalar2=0.5, op0=ALU.mult, op1=ALU.add),
                 reads=["lf"], writes=["lf"])
            P.op("act", lambda: A.activation(out=lf, in_=lf, func=AF.Ln), reads=["lf"], writes=["lf"])
            cprev, cb = cum[1 - b], cum[b]
            P.op("dve", lambda: V.tensor_tensor_scan(out=cb[:], data0=onesrow, data1=lf,
                                                     initial=cprev[:, 511:512], op0=ALU.mult, op1=ALU.add),
                 reads=["lf", f"cum{1-b}", "onesrow"], writes=[f"cum{b}"])
            yield
            ps, pk = nextpp()
            for q in range(4):
                P.op("pe", lambda: PE.matmul(ps[:, 2 * q:2 * q + 2], lhsT=cb[0:2, q * 128:(q + 1) * 128],
                                             rhs=cst[0:2, K_ID:K_ID + 2], start=True, stop=True),
                     reads=[f"cum{b}", "cst"], writes=[pk])
            P.op("dve", lambda: V.tensor_scalar(out=i2[:], in0=cst[0:2, K_ID:K_ID + 2], scalar1=cb[:, 255:256],
                                                scalar2=None, op0=ALU.mult),
                 reads=[f"cum{b}", "cst"], writes=["i2"])
            P.op("pe", lambda: PE.matmul(ps[:, 8:10], lhsT=ones[0:2, :], rhs=i2[:, :], start=True, stop=True),
                 reads=["i2", "cst"], writes=[pk])
            P.op("dve", lambda: V.tensor_copy(out=ckres[:, 4 * j:4 * j + 4, :],
                                              in_=ps[:, 0:8].rearrange("p (q h) -> p q h", q=4)),
                 reads=[pk], writes=[f"ck{j}"])
            P.op("dve", lambda: V.tensor_copy(out=basebc[b][:], in_=ps[:, 8:10]), reads=[pk], writes=[f"basebc{b}"])
        yield

    P.op("dve", lambda: V.memset(onesrow, 1.0), writes=["onesrow"])

    def attn_tile(j):
        b = j % 2
        nb = 4 * j + 4
        steps = [(h, kb) for h in range(2) for kb in range(nb)]
        for h in range(2):
            bj = biasj[h]
            P.op("dve", lambda: V.tensor_scalar(out=bj[:, 0:nb, h], in0=ckres[:, 0:nb, h], scalar1=-1.0,
                                                scalar2=basebc[b][:, h:h + 1], op0=ALU.mult, op1=ALU.add),
                 reads=[f"ck{jj}" for jj in range(j + 1)] + [f"basebc{b}"], writes=[f"biasj{h}"])
        yield

        def q0_of(kb):
            m = kb - 4 * j
            return 128 * m if m > 0 else 0

        def emit_S(i):
            h, kb = steps[i]
            hp = slice(64 * h, 64 * h + 64)
            q0 = q0_of(kb)
            si = i % NSB
            P.op("pe", lambda: PE.matmul(psS[si][:, q0:512], lhsT=KT[:, kb * 128:(kb + 1) * 128],
                                         rhs=QT[b][h][:, q0:512], start=True, stop=True),
                 reads=[f"KT{kb // 4}", f"QT{b}"], writes=[f"psS{si}"])

        for i0 in range(min(LOOK, len(steps))):
            emit_S(i0)
        for i, (h, kb) in enumerate(steps):
            if i + LOOK < len(steps):
                emit_S(i + LOOK)
            bj = biasj[h]
            m = kb - 4 * j
            q0 = q0_of(kb)
            si = i % NSB
            pi = i % NPT
            P.op("act", lambda: A.activation(out=PT[pi][:, q0:512], in_=psS[si][:, q0:512], func=AF.Exp,
                                             bias=bj[:, kb, h:h + 1], scale=1.0),
                 reads=[f"psS{si}", f"biasj{h}"], writes=[f"PT{pi}"])
            if m >= 0:
                P.op("pool", lambda: G.tensor_tensor(out=PT[pi][:, q0:q0 + 128], in0=PT[pi][:, q0:q0 + 128],
                                                     in1=uib[:], op=ALU.mult),
                     reads=[f"PT{pi}", "uib"], writes=[f"PT{pi}"])
            P.op("pe", lambda: PE.matmul(psO[0:65, q0:512], lhsT=Vaug[:, kb, h, :], rhs=PT[pi][:, q0:512],
                                         start=(kb == 0), stop=(kb == nb - 1)),
                 reads=[f"PT{pi}", f"V{kb // 4}", "Vaug_ones"], writes=["psO"])
            if kb == nb - 1:
                P.op("dve", lambda: V.tensor_copy(out=osb[0:65, :], in_=psO[0:65, :]), reads=["psO"], writes=["osb"])
                P.op("dve", lambda: V.reciprocal(out=rrow[64:65, :], in_=osb[64:65, :]), reads=["osb"],
                     writes=["rrow"])
                pq, pqk = psS[si], f"psS{si}"
                P.op("pe", lambda: PE.matmul(pq[0:64, :], lhsT=ones[64:65, 0:64], rhs=rrow[64:65, :],
                                             start=True, stop=True),
                     reads=["rrow", "cst"], writes=[pqk])
                P.op("dve", lambda: V.scalar_tensor_tensor(out=t1[:], in0=pq[0:64, :], scalar=0.5, in1=sgb[b][h][:],
                                                           op0=ALU.mult, op1=ALU.mult),
                     reads=[pqk, f"sgb{b}_{h}"], writes=["t1"])
                P.op("pool", lambda: G.tensor_tensor(out=ybo[h][:], in0=osb[0:64, :], in1=t1[:], op=ALU.mult),
                     reads=["osb", "t1"], writes=[f"ybo{h}"])
                P.dma("sp", y_dst(128 + 64 * h, 192 + 64 * h, j), ybo[h][:], f"st_yb{h}",
                      reads=[f"ybo{h}"], writes=[f"yTb{h}", f"ytile{j}_b{h}"])
            yield

    def SLK(h, i0=0, i1=4):
        return [f"psA{h}"]

    psC = psY
    CK = ["psC"]

    progress = [0, 0]

    def rwkv_prep(j):
        for h_ in range(2):
            for c_ in range(4):
                ndone[h_][c_] = False
        pa, pak = psA[0], SLK(0)
        pb, pbk = psA[1], SLK(1)
        th = tmp
        P.op("act", lambda: A.activation(out=th[0:64, :], in_=sh["w"][0:64, :], func=AF.Tanh),
             reads=["sh_w"], writes=["tmp"])
        P.op("pe", lambda: PE.matmul(pa[:, :], lhsT=upt[0:64, :], rhs=th[0:64, :], start=True, stop=True),
             reads=["upt", "tmp"], writes=pak)
        P.op("pe", lambda: PE.matmul(pb[:, :], lhsT=upt[64:128, :], rhs=sh["w"][64:128, :], start=True, stop=True),
             reads=["upt", "sh_w"], writes=pbk)
        P.op("act", lambda: A.activation(out=ld[:], in_=pa[:, :], func=AF.Tanh, bias=vec[:, 16:17], scale=0.5),
             reads=pak + ["vec"], writes=["ld"])
        P.op("act", lambda: A.activation(out=a_t[:], in_=pb[:, :], func=AF.Tanh, bias=vec[:, 17:18], scale=0.5),
             reads=pbk + ["vec"], writes=["a_t"])
        yield
        P.op("dve", lambda: V.tensor_scalar(out=ld[:], in0=ld[:], scalar1=1.0, scalar2=-0.5 * float(np.exp(-0.5)),
                                            op0=ALU.add, op1=ALU.mult), reads=["ld"], writes=["ld"])
        P.op("pool", lambda: G.tensor_scalar(out=a_t[:], in0=a_t[:], scalar1=0.5, scalar2=0.5, op0=ALU.mult,
                                             op1=ALU.add), reads=["a_t"], writes=["a_t"])
        P.op("dve", lambda: V.tensor_tensor_scan(out=cl[:], data0=rsm, data1=ld[:], initial=0.0,
                                                 op0=ALU.mult, op1=ALU.add),
             reads=["ld", "cst"], writes=["cl"])
        P.op("act", lambda: A.activation(out=einc[:], in_=cl[:], func=AF.Exp), reads=["cl"], writes=["einc"])
        P.op("act", lambda: A.activation(out=eneg[:], in_=cl[:], func=AF.Exp, scale=-1.0),
             reads=["cl"], writes=["eneg"])
        P.op("dve", lambda: V.tensor_tensor(out=w1[:], in0=cl[:], in1=ld[:], op=ALU.subtract),
             reads=["cl", "ld"], writes=["w1"])
        P.op("act", lambda: A.activation(out=w1[:], in_=w1[:], func=AF.Exp), reads=["w1"], writes=["w1"])
        yield
        P.op("dve", lambda: V.tensor_scalar(out=kk[:], in0=sh["k"][:], scalar1=vec[:, 6:7], scalar2=None,
                                            op0=ALU.mult), reads=["sh_k", "vec"], writes=["kk"])
        P.op("dve", lambda: V.tensor_tensor(out=w2[:], in0=kk[:], in1=kk[:], op=ALU.mult),
             reads=["kk"], writes=["w2"])
        P.op("pe", lambda: PE.matmul(pa[:, :], lhsT=bones, rhs=w2[:], start=True, stop=True),
             reads=["cst", "w2"], writes=pak)
        P.op("dve", lambda: V.tensor_scalar(out=w2[:], in0=pa[:, :], scalar1=1e-24, scalar2=None, op0=ALU.max),
             reads=pak, writes=["w2"])
        P.op("act", lambda: A.activation(out=w2[:], in_=w2[:], func=AF.Ln), reads=["w2"], writes=["w2"])
        P.op("act", lambda: A.activation(out=w2[:], in_=w2[:], func=AF.Exp, scale=-0.5), reads=["w2"], writes=["w2"])
        P.op("dve", lambda: V.tensor_tensor(out=kk[:], in0=kk[:], in1=w2[:], op=ALU.mult),
             reads=["kk", "w2"], writes=["kk"])
        yield
        P.op("dve", lambda: V.tensor_scalar(out=w2[:], in0=a_t[:], scalar1=vec[:, 7:8], scalar2=vec[:, 15:16],
                                            op0=ALU.mult, op1=ALU.add), reads=["a_t", "vec"], writes=["w2"])
        P.op("dve", lambda: V.tensor_tensor(out=kmod[:], in0=sh["k"][:], in1=w2[:], op=ALU.mult),
             reads=["sh_k", "w2"], writes=["kmod"])
        P.op("dve", lambda: V.tensor_tensor(out=w2[:], in0=kk[:], in1=a_t[:], op=ALU.mult),
             reads=["kk", "a_t"], writes=["w2"])
        P.op("dve", lambda: V.tensor_tensor(out=Bt[:], in0=w2[:], in1=eneg[:], op=ALU.mult),
             reads=["w2", "eneg"], writes=["Bt"])
        P.op("pool", lambda: G.tensor_tensor(out=Kt[:], in0=kmod[:], in1=eneg[:], op=ALU.mult),
             reads=["kmod", "eneg"], writes=["Kt"])
        for hh in range(2):
            hq = slice(64 * hh, 64 * hh + 64)
            P.op("dve", lambda: V.tensor_tensor(out=KRz[hh][hq, :, 0, :],
                                                in0=kk[hq, :].rearrange("p (c t) -> p c t", c=4),
                                                in1=w1[hq, :].rearrange("p (c t) -> p c t", c=4), op=ALU.mult),
                 reads=["kk", "w1"], writes=["KR"])
            P.op("pool", lambda: G.tensor_tensor(out=KRz[hh][hq, :, 1, :],
                                                 in0=sh["r"][hq, :].rearrange("p (c t) -> p c t", c=4),
                                                 in1=einc[hq, :].rearrange("p (c t) -> p c t", c=4), op=ALU.mult),
                 reads=["sh_r", "einc"], writes=["KR"])
        yield
        P.op("dve", lambda: V.scalar_tensor_tensor(out=w2[:], in0=sh["r"][:], scalar=vec[:, 8:9], in1=kmod[:],
                                                   op0=ALU.mult, op1=ALU.mult),
             reads=["sh_r", "vec", "kmod"], writes=["w2"])
        P.op("pe", lambda: PE.matmul(pb[:, :], lhsT=bones, rhs=w2[:], start=True, stop=True),
             reads=["cst", "w2"], writes=pbk)
        P.op("dve", lambda: V.tensor_tensor(out=bonus[:], in0=pb[:, :], in1=sh["v"][:], op=ALU.mult),
             reads=pbk + ["sh_v"], writes=["bonus"])
        yield
        for (src, skey, dst, dkey, pq, pqk) in ((sh["v"], "sh_v", Vtok, "Vtok", pa, pak),
                                                (Bt, "Bt", Btok, "Btok", pb, pbk),
                                                (Kt, "Kt", Ktok, "Ktok", pa, pak)):
            isb = (src.dtype == BF16)
            pqv = pq[:, :].bitcast(BF16) if isb else pq[:, :]
            for c in range(4):
                P.op("pe", lambda: PE.transpose(pqv[:, c * 128:(c + 1) * 128], src[:, c * 128:(c + 1) * 128],
                                                idb[:] if isb else ident),
                     reads=[skey, "cst", "idb"], writes=pqk)
            P.op("dve", lambda: V.tensor_copy(out=dst[:].rearrange("p c t -> p (c t)"), in_=pqv[:, 0:512]),
                 reads=pqk, writes=[dkey])
            yield

    ndone = [[False] * 4, [False] * 4]

    def neumann(j, h, c):
        bi = h if NEU_BANKS == 2 else (2 * c + h) % NEU_BANKS
        pa = psA[bi]
        pk = SLK(bi)
        cs = slice(c * 128, (c + 1) * 128)
        at, atk = AT[h][c], f"AT{h}_{c}"
        ak, akk = AK[h][c], f"AK{h}_{c}"
        X, Xk = XL[h][c], f"TmF{h}_{c}"
        Lc, Ltc, Tc = X[:, 0:128], X[:, 128:256], X[:, 256:384]
        krc = KRz[h][:, c, :, :].rearrange("p a t -> p (a t)")
        P.op("pe", lambda: PE.matmul(pa[:, 0:256], lhsT=Bt[:, cs], rhs=krc, start=True, stop=True),
             reads=["Bt", "KR"], writes=pk)
        P.op("pe", lambda: PE.matmul(pa[:, 256:384], lhsT=KRz[h][:, c, 0, :], rhs=Bt[:, cs], start=True, stop=True),
             reads=["Bt", "KR"], writes=pk)
        P.op("dve", lambda: V.tensor_tensor(out=at[:], in0=pa[:, 0:256], in1=mask2, op=ALU.mult),
             reads=pk + ["cst"], writes=[atk])
        P.op("dve", lambda: V.tensor_tensor(out=Lc, in0=pa[:, 256:384], in1=msl, op=ALU.mult),
             reads=pk + ["cst"], writes=[Xk])
        P.op("pool", lambda: G.tensor_tensor(out=Tc, in0=ident, in1=at[:, 0:128], op=ALU.subtract),
             reads=[atk, "cst", Xk], writes=[Xk])
        yield
        P.op("pe", lambda: PE.matmul(pa[:, 0:128], lhsT=at[:, 0:128], rhs=Lc, start=True, stop=True),
             reads=[atk, Xk], writes=pk)
        P.op("pe", lambda: PE.matmul(pa[:, 128:256], lhsT=Lc, rhs=at[:, 0:128], start=True, stop=True),
             reads=[atk, Xk], writes=pk)
        P.op("pe", lambda: PE.matmul(pa[:, 256:512], lhsT=Kt[:, cs], rhs=krc, start=True, stop=True),
             reads=["Kt", "KR"], writes=pk)
        P.op("dve", lambda: V.tensor_copy(out=X[:, 0:256], in_=pa[:, 0:256]), reads=pk + [Xk], writes=[Xk])
        P.op("dve", lambda: V.tensor_tensor(out=ak[:], in0=pa[:, 256:512], in1=mask2, op=ALU.mult),
             reads=pk + ["cst"], writes=[akk])
        yield
        for k in range(1, 7):
            P.op("pe", lambda: PE.matmul(pa[:, 256:384], lhsT=Lc, rhs=Tc, start=True, stop=False),
                 reads=[Xk], writes=pk)
            P.op("pe", lambda: PE.matmul(pa[:, 256:384], lhsT=idb[:], rhs=Tc, start=False, stop=True),
                 reads=[Xk, "idb"], writes=pk)
            if k < 6:
                P.op("pe", lambda: PE.matmul(pa[:, 0:128], lhsT=Ltc, rhs=Lc, start=True, stop=True),
                     reads=[Xk], writes=pk)
            if k < 5:
                P.op("pe", lambda: PE.matmul(pa[:, 128:256], lhsT=Lc, rhs=Ltc, start=True, stop=True),
                     reads=[Xk], writes=pk)
            lo = 0 if k < 6 else 256
            P.op("dve", lambda: V.tensor_copy(out=X[:, lo:384], in_=pa[:, lo:384]), reads=pk + [Xk], writes=[Xk])
            yield
        ndone[h][c] = True

    def chain(j, h):
        pa = psC
        hp = slice(64 * h, 64 * h + 64)
        hc = slice(64 * h, 64 * h + 64)
        o0 = 256 * h
        stk, stbk, stwk, zk, uk = f"ST{h}", f"STb{h}", f"STw{h}", f"Zs{h}", f"Us{h}"
        for c in range(4):
            while not ndone[h][c]:
                yield
            par = c
            cs = slice(c * 128, (c + 1) * 128)
            wc = einc[hp, c * 128 + 127:c * 128 + 128]
            P.op("pe", lambda: PE.matmul(pa[:, o0:o0 + 64], lhsT=KRz[h][:, c, 0, :], rhs=STb[:, 0:64], start=True, stop=False),
                 reads=["KR", stbk], writes=CK)
            P.op("pe", lambda: PE.matmul(pa[:, o0:o0 + 64], lhsT=AK[h][par][:, 0:128], rhs=Vtok[:, c, hc],
                                         start=False, stop=True),
                 reads=[f"AK{h}_{par}", "Vtok"], writes=CK)
            P.op("dve", lambda: V.tensor_copy(out=Zs[:, hc], in_=pa[:, o0:o0 + 64]), reads=CK, writes=[zk])
            P.op("pool", lambda: G.tensor_scalar(out=STw[hp, :], in0=ST[hp, :], scalar1=wc, scalar2=None,
                                                 op0=ALU.mult), reads=[stk, "einc"], writes=[stwk])
            yield
            P.op("pe", lambda: PE.matmul(pa[:, o0 + 64:o0 + 128], lhsT=TmF[h][par], rhs=Zs[:, hc], start=True, stop=True),
                 reads=[f"TmF{h}_{par}", zk], writes=CK)
            P.op("dve", lambda: V.tensor_scalar(out=Us[:, hc], in0=pa[:, o0 + 64:o0 + 128], scalar1=-1.0,
                                                scalar2=None, op0=ALU.mult), reads=CK, writes=[uk])
            yield
            yo = pa[:, o0 + 128:o0 + 256]
            P.op("pe", lambda: PE.matmul(yo, lhsT=STb[:, :], rhs=KRz[h][:, c, 1, :], start=True, stop=False),
                 reads=[stbk, "KR"], writes=CK)
            P.op("pe", lambda: PE.matmul(yo, lhsT=Us[:, :], rhs=AT[h][par][:, 128:256], start=False, stop=False),
                 reads=[uk, f"AT{h}_{par}"], writes=CK)
            P.op("pe", lambda: PE.matmul(yo, lhsT=Vtok[:, c, :], rhs=AK[h][par][:, 128:256], start=False,
                                         stop=True), reads=["Vtok", f"AK{h}_{par}"], writes=CK)
            P.op("dve", lambda: V.tensor_copy(out=ysb[hp, cs], in_=pa[hp, o0 + 128:o0 + 256]), reads=CK,
                 writes=[f"ysb{h}"])
            P.op("pe", lambda: PE.matmul(pa[:, o0:o0 + 64], lhsT=Btok[:, c, :], rhs=Us[:, hc], start=True, stop=False),
                 reads=["Btok", uk], writes=CK)
            P.op("pe", lambda: PE.matmul(pa[:, o0:o0 + 64], lhsT=Ktok[:, c, :], rhs=Vtok[:, c, hc], start=False, stop=True),
                 reads=["Ktok", "Vtok"], writes=CK)
            P.op("dve", lambda: V.scalar_tensor_tensor(out=ST[hp, :], in0=pa[hp, o0:o0 + 64], scalar=wc, in1=STw[hp, :],
                                                       op0=ALU.mult, op1=ALU.add),
                 reads=CK + ["einc", stwk], writes=[stk])
            P.op("pool", lambda: G.tensor_copy(out=STb[hp, 0:64], in_=ST[hp, :]), reads=[stk], writes=[stbk])
            P.op("dve", lambda: V.tensor_copy(out=STb[hp, 64:128], in_=ST[hp, :]), reads=[stk], writes=[stbk])
            yield

    def rwkv_fin(j):
        b = j % 2
        pa, pak = psA[0], SLK(0)
        pb, pbk = psA[1], SLK(1)
        P.op("pool", lambda: G.tensor_tensor(out=w2[:], in0=ysb[:], in1=ysb[:], op=ALU.mult),
             reads=["ysb0", "ysb1", "ysb"], writes=["w2"])
        yield
        P.op("pe", lambda: PE.matmul(pa[:, :], lhsT=bones, rhs=ysb[:], start=True, stop=True),
             reads=["cst", "ysb0", "ysb1", "ysb"], writes=pak)
        P.op("pe", lambda: PE.matmul(pb[:, :], lhsT=bones, rhs=w2[:], start=True, stop=True),
             reads=["cst", "w2"], writes=pbk)
        P.op("act", lambda: A.activation(out=w1[:], in_=pa[:, :], func=AF.Square, scale=1.0 / 64), reads=pak,
             writes=["w1"])
        P.op("dve", lambda: V.scalar_tensor_tensor(out=ysb[:], in0=pa[:, :], scalar=-1.0 / 64, in1=ysb[:],
                                                   op0=ALU.mult, op1=ALU.add), reads=pak + ["ysb", "ysb0", "ysb1"],
             writes=["ysb", "ysb0", "ysb1"])
        P.op("dve", lambda: V.scalar_tensor_tensor(out=w1[:], in0=pb[:, :], scalar=1.0 / 64, in1=w1[:],
                                                   op0=ALU.mult, op1=ALU.subtract), reads=pbk + ["w1"], writes=["w1"])
        yield
        P.op("dve", lambda: V.tensor_scalar(out=w1[:], in0=w1[:], scalar1=GN_EPS, scalar2=None, op0=ALU.add),
             reads=["w1"], writes=["w1"])
        P.op("act", lambda: A.activation(out=w1[:], in_=w1[:], func=AF.Ln), reads=["w1"], writes=["w1"])
        P.op("act", lambda: A.activation(out=w1[:], in_=w1[:], func=AF.Exp, scale=-0.5), reads=["w1"], writes=["w1"])
        yield
        P.op("pool", lambda: G.tensor_tensor(out=ysb[:], in0=ysb[:], in1=w1[:], op=ALU.mult),
             reads=["ysb", "w1"], writes=["ysb"])
        P.op("dve", lambda: V.tensor_scalar(out=ysb[:], in0=ysb[:], scalar1=vec[:, 9:10], scalar2=vec[:, 10:11],
                                            op0=ALU.mult, op1=ALU.add), reads=["ysb", "vec"], writes=["ysb"])
        yield
        P.op("pool", lambda: G.tensor_tensor(out=ysb[:], in0=ysb[:], in1=bonus[:], op=ALU.add),
             reads=["ysb", "bonus"], writes=["ysb"])
        P.op("dve", lambda: V.scalar_tensor_tensor(out=yao[b][:], in0=ysb[:], scalar=0.5, in1=sgaL[b][:], op0=ALU.mult,
                                                   op1=ALU.mult), reads=["ysb", f"sga{b}"], writes=[f"yao{b}"])
        P.dma("sp", y_dst(0, 128, j), yao[b][:], f"st_ya{b}", reads=[f"yao{b}"], writes=[f"yTa{b}", f"ytile{j}_a"])
        yield

    def run_stage(prims, bg, ratio):
        alive = list(prims)
        acc = 0.0
        while alive:
            for g in list(alive):
                try:
                    next(g)
                except StopIteration:
                    alive.remove(g)
            if bg[0] is not None:
                acc += ratio
                while acc >= 1.0 and bg[0] is not None:
                    acc -= 1.0
                    try:
                        next(bg[0])
                    except StopIteration:
                        bg[0] = None

    def run_tile(j, nxt):
        bg = [attn_tile(j) if do_fox else None]
        n_attn = 2 * (4 * j + 4) + 1
        drain = bool(nxt) and DRAIN_OVERLAP and n_attn >= DRAIN_MIN
        r = n_attn / (RW_ROUNDS + (14.0 if drain else 0.0))
        if do_rwkv:
            run_stage([rwkv_prep(j)], bg, r)
            run_stage([neumann(j, h_, c_) for c_ in range(4) for h_ in range(2)] + [chain(j, 0), chain(j, 1)] + ([nxt] if (nxt and OVERLAP_PROJ) else []), bg, r)
            run_stage([rwkv_fin(j)] + ([nxt] if (nxt and FIN_OVERLAP) else []), bg, r)
            if drain:
                run_stage([nxt], bg, r)
        elif nxt:
            run_stage([nxt], bg, r)
        while bg[0] is not None:
            try:
                next(bg[0])
            except StopIteration:
                bg[0] = None

    load_h(0)
    if NT > 1:
        load_h(1)
    for _ in tile_proj(0):
        pass
    for j in range(NT):
        nxt = tile_proj(j + 1) if j + 1 < NT else None
        if j + 2 < NT:
            load_h(j + 2)
        if OVERLAP_PROJ:
            run_tile(j, nxt)
        else:
            run_tile(j, nxt if (FIN_OVERLAP or DRAIN_OVERLAP) else None)
            if nxt:
                for _ in nxt:
                    pass
        if after_tile is not None:
            after_tile(j)
    P.wait_all("sp", ["yTa0", "yTa1", "yTb0", "yTb1"])


def build_mix(T, **kw):
    nc = bass.Bass("TRN2", target_bir_lowering=False)
    hT = nc.dram_tensor("hT", [1024, T], BF16, kind="ExternalInput").ap()
    wcore = nc.dram_tensor("wcore", [1024, NCOL], F32, kind="ExternalInput").ap()
    vecs = nc.dram_tensor("vecs", [128, 11], F32, kind="ExternalInput").ap()
    up = nc.dram_tensor("up", [128, 128], F32, kind="ExternalInput").ap()
    fbf = nc.dram_tensor("fbf", [2, 1], F32, kind="ExternalInput").ap()
    consts = nc.dram_tensor("consts", [128, NCONST], F32, kind="ExternalInput").ap()
    yT = nc.dram_tensor("yT", [256, T], BF16, kind="ExternalOutput").ap()
    P = Prog(nc)
    emit_mix(P, nc, T, hT, wcore, vecs, up, fbf, consts, yT, **kw)
    print("mix ops", P.n_ops, "waits", P.n_waits)
    P.close()
    return nc

import numpy as np

LN_EPS = 1e-5
ALPHA = float(4 ** 0.25)
K_ID, K_ONE = 0, 1152


def emit_tok(P, nc, NTOK, consts, c_in, *, x_in=None, embg=None, embb=None,
             yT=None, xprev=None, wout=None, wada_g=None, bada_g=None, lng=None, lnb=None,
             wada_f=None, bada_f=None, x_out=None, hT_out=None, pfx="t",
             y_src=None, y_keys=(), h_dst=None, after_h=None):
    do_back = (yT is not None) or (y_src is not None)
    do_front = (hT_out is not None) or (h_dst is not None)
    if do_back and y_src is None:
        y_src = lambda s: yT.rearrange("(c p) t -> p c t", p=128)[:, :, s * 512:(s + 1) * 512]
    if do_front and h_dst is None:
        h_dst = lambda s: hT_out.rearrange("(c p) t -> p c t", p=128)[:, :, s * 512:(s + 1) * 512]
    NS = NTOK // 512
    V, A, G, PE = nc.vector, nc.scalar, nc.gpsimd, nc.tensor
    k = lambda s: pfx + s

    cst = P.sb(k("cst"), [128, 1280], F32)
    P.dma("sp", cst[:], consts[:, :], k("ld_cst"), writes=[k("cst")])
    ident = cst[:, K_ID:K_ID + 128]
    ones = cst[:, K_ONE:K_ONE + 128]
    c_sb = P.sb(k("c_sb"), [128, 8], F32)
    P.dma("sp", c_sb[:], c_in.rearrange("(c p) -> p c", p=128), k("ld_c"), writes=[k("c_sb")],
          allow_slow_non_contiguous=True)
    cbc = P.sb(k("cbc"), [128, 8, 128], F32)
    for c in range(8):
        P.op("dve", lambda: V.tensor_scalar(out=cbc[:, c, :], in0=ones, scalar1=c_sb[:, c:c + 1], scalar2=None,
                                            op0=ALU.mult), reads=[k("cst"), k("c_sb")], writes=[k("cbc")])
    pp = [P.ps(k(f"pp{i}"), [128, 512]) for i in range(4)]
    ppc = [0]

    def nextpp():
        i = ppc[0] % 4
        ppc[0] += 1
        return pp[i], k(f"pp{i}")

    wad = [P.sb(k(f"wad{i}"), [128, 8, 512], F32) for i in range(2)]
    wadc = [0]

    def mod_bc(wada, bada, N, name):
        mt = P.sb(k(name), [128, N], F32)
        P.dma("sp", mt[:], bada.partition_broadcast(128), k("ld_" + name), writes=[k(name)])
        wv = wada.rearrange("(c p) n -> p c n", p=128)
        for n0 in range(0, N, 512):
            b = wadc[0] % 2
            wadc[0] += 1
            P.dma("sp", wad[b][:], wv[:, :, n0:n0 + 512], k(f"ld_wad{b}"), writes=[k(f"wad{b}")])
            ps, pk = nextpp()
            for c in range(8):
                P.op("pe", lambda: PE.matmul(ps[:, :], lhsT=cbc[:, c, :], rhs=wad[b][:, c, :], start=(c == 0),
                                             stop=(c == 7)), reads=[k("cbc"), k(f"wad{b}")], writes=[pk])
            P.op("dve", lambda: V.tensor_tensor(out=mt[:, n0:n0 + 512], in0=ps[:, :], in1=mt[:, n0:n0 + 512],
                                                op=ALU.add), reads=[pk, k(name)], writes=[k(name)])
        return mt

    def bc_load(vec_ap, name):
        t = P.sb(k(name), [128, 1024], F32)
        P.dma("sp", t[:], vec_ap.partition_broadcast(128), k("ld_" + name), writes=[k(name)])
        return t

    if do_back:
        mg = mod_bc(wada_g, bada_g, 1024, "mg")
        P.op("dve", lambda: V.tensor_scalar(out=mg[:], in0=mg[:], scalar1=1.0, scalar2=None, op0=ALU.add),
             reads=[k("mg")], writes=[k("mg")])
        wo = P.sb(k("wo"), [128, 8, 1024], BF16)
        wov = wout.rearrange("(c p) n -> p c n", p=128)
        for pi, n0 in enumerate(range(0, 1024, 512)):
            b = wadc[0] % 2
            wadc[0] += 1
            P.dma("sp", wad[b][:], wov[:, :, n0:n0 + 512], k(f"ld_wad{b}"), writes=[k(f"wad{b}")])
            for c in range(8):
                e = "dve" if c % 2 == 0 else "pool"
                eng = V if c % 2 == 0 else G
                P.op(e, lambda: eng.tensor_tensor(out=wo[:, c, n0:n0 + 512], in0=wad[b][:, c, :],
                                                  in1=mg[:, n0:n0 + 512], op=ALU.mult),
                     reads=[k(f"wad{b}"), k("mg")], writes=[k("wo")])
        g_bc = bc_load(lng, "lng")
        b_bc = bc_load(lnb, "lnb")
    else:
        g_bc = bc_load(embg, "embg")
        b_bc = bc_load(embb, "embb")
    if do_front:
        mf = mod_bc(wada_f, bada_f, 2048, "mf")
        P.op("dve", lambda: V.tensor_scalar(out=mf[:, 1024:2048], in0=mf[:, 1024:2048], scalar1=1.0, scalar2=None,
                                            op0=ALU.add), reads=[k("mf")], writes=[k("mf")])
        fm = P.sb(k("fm"), [128, 16], F32)
        for q in range(4):
            ps, pk = nextpp()
            for u in range(4):
                cc = q * 4 + u
                P.op("pe", lambda: PE.transpose(ps[:, u * 128:(u + 1) * 128], mf[:, cc * 128:(cc + 1) * 128], ident),
                     reads=[k("mf"), k("cst")], writes=[pk])
            P.op("dve", lambda: V.tensor_copy(out=fm[:, q * 4:q * 4 + 4],
                                              in_=ps[:, :].rearrange("p (c t) -> p c t", t=128)[:, :, 0]),
                 reads=[pk], writes=[k("fm")])

    xt = [P.sb(k(f"xt{i}"), [128, 4, 1024], F32) for i in range(2)]
    yt = [P.sb(k(f"yt{i}"), [128, 8, 512], BF16) for i in range(2)] if do_back else None
    ht = [P.sb(k(f"ht{i}"), [128, 8, 512], BF16) for i in range(2)] if do_front else None
    xsrc = xprev if do_back else x_in

    def load(s):
        b = s % 2
        P.dma("sp", xt[b][:], xsrc[s * 512:(s + 1) * 512, :].rearrange("(u p) d -> p u d", p=128),
              k(f"ld_x{b}"), writes=[k(f"xt{b}")] + [k(f"xt{b}_{u}") for u in range(4)])
        if do_back:
            P.dma("sp", yt[b][:], y_src(s), k(f"ld_y{b}"), reads=y_keys, writes=[k(f"yt{b}")])

    stats4 = [P.sb(k(f"stats4_{i}"), [128, 4, 2, 6], F32) for i in range(2)]
    mv4 = [P.sb(k(f"mv4_{i}"), [128, 4, 2], F32) for i in range(2)]
    rs4 = [P.sb(k(f"rs4_{i}"), [128, 4], F32) for i in range(2)]
    nb4 = [P.sb(k(f"nb4_{i}"), [128, 4], F32) for i in range(2)]
    load(0)
    for s in range(NS):
        b = s % 2
        if s + 1 < NS:
            load(s + 1)
        xk = k(f"xt{b}")
        xku = [k(f"xt{b}_{u}") for u in range(4)]
        for u in range(4):
            xs = xt[b][:, u, :]
            if do_back:
                for n in range(2):
                    ps, pk = nextpp()
                    for c in range(8):
                        P.op("pe", lambda: PE.matmul(ps[:, :], lhsT=yt[b][:, c, u * 128:(u + 1) * 128],
                                                     rhs=wo[:, c, n * 512:(n + 1) * 512], start=(c == 0),
                                                     stop=(c == 7)), reads=[k(f"yt{b}"), k("wo")], writes=[pk])
                    P.op("dve", lambda: V.scalar_tensor_tensor(out=xs[:, n * 512:(n + 1) * 512],
                                                               in0=xs[:, n * 512:(n + 1) * 512], scalar=ALPHA,
                                                               in1=ps[:, :], op0=ALU.mult, op1=ALU.add),
                         reads=[xk, xku[u], pk], writes=[xku[u]])
            for n in range(2):
                P.op("dve", lambda: V.bn_stats(out=stats4[b][:, u, n, :], in_=xs[:, n * 512:(n + 1) * 512]),
                     reads=[xk, xku[u]], writes=[k(f"st4_{b}_{u}")])
            P.op("dve", lambda: V.bn_aggr(out=mv4[b][:, u, :], in_=stats4[b][:, u, :, :].rearrange("p a b -> p (a b)")),
                 reads=[k(f"st4_{b}_{u}")], writes=[k(f"mv4_{b}")])
        P.op("dve", lambda: V.tensor_scalar(out=rs4[b][:], in0=mv4[b][:, :, 1], scalar1=LN_EPS, scalar2=None,
                                            op0=ALU.add), reads=[k(f"mv4_{b}")], writes=[k(f"rs4_{b}")])
        P.op("act", lambda: A.activation(out=rs4[b][:], in_=rs4[b][:], func=AF.Sqrt), reads=[k(f"rs4_{b}")],
             writes=[k(f"rs4_{b}")])
        P.op("dve", lambda: V.reciprocal(out=rs4[b][:], in_=rs4[b][:]), reads=[k(f"rs4_{b}")], writes=[k(f"rs4_{b}")])
        P.op("dve", lambda: V.scalar_tensor_tensor(out=nb4[b][:], in0=mv4[b][:, :, 0], scalar=-1.0, in1=rs4[b][:],
                                                   op0=ALU.mult, op1=ALU.mult),
             reads=[k(f"mv4_{b}"), k(f"rs4_{b}")], writes=[k(f"nb4_{b}")])
        for u in range(4):
            xs = xt[b][:, u, :]
            P.op("act", lambda: A.activation(out=xs, in_=xs, func=AF.Identity, scale=rs4[b][:, u:u + 1],
                                             bias=nb4[b][:, u:u + 1]),
                 reads=[xk, xku[u], k(f"rs4_{b}"), k(f"nb4_{b}")], writes=[xku[u]])
            P.op("dve", lambda: V.tensor_tensor(out=xs, in0=xs, in1=g_bc[:], op=ALU.mult),
                 reads=[xku[u], k("lng"), k("embg")], writes=[xku[u]])
            P.op("pool", lambda: G.tensor_tensor(out=xs, in0=xs, in1=b_bc[:], op=ALU.add),
                 reads=[xku[u], k("lnb"), k("embb")], writes=[xku[u]])
        if x_out is not None:
            P.dma("sp", x_out[s * 512:(s + 1) * 512, :].rearrange("(u p) d -> p u d", p=128), xt[b][:],
                  k(f"st_x{b}"), reads=[xk] + xku, writes=[k(f"xo{b}")])
        if do_front:
            hk = k(f"ht{b}")
            for c in range(8):
                ps, pk = nextpp()
                for u in range(4):
                    P.op("pe", lambda: PE.transpose(ps[:, u * 128:(u + 1) * 128], xt[b][:, u, c * 128:(c + 1) * 128],
                                                    ident), reads=[xk, xku[u], k("cst")], writes=[pk])
                P.op("act", lambda: A.activation(out=ht[b][:, c, :], in_=ps[:, :], func=AF.Identity,
                                                 scale=fm[:, 8 + c:9 + c], bias=fm[:, c:c + 1]),
                     reads=[pk, k("fm")], writes=[hk])
            P.dma("sp", h_dst(s), ht[b][:], k(f"st_h{b}"), reads=[hk], writes=[k(f"ho{b}"), f"htile{s}"])
            if after_h is not None:
                after_h(s)
    P.wait_all("sp", [k("xo0"), k("xo1"), k("ho0"), k("ho1")])


def build_tok(NTOK, mode):
    nc = bass.Bass("TRN2", target_bir_lowering=False)
    dt = lambda n, s, d=F32, kind="ExternalInput": nc.dram_tensor(n, s, d, kind=kind).ap()
    consts = dt("consts", [128, 1280])
    c_in = dt("c", [1024])
    kw = {}
    if mode == "pre":
        kw.update(x_in=dt("x", [NTOK, 1024]), embg=dt("embg", [1024]), embb=dt("embb", [1024]))
    else:
        kw.update(yT=dt("yT", [1024, NTOK], BF16), xprev=dt("xprev", [NTOK, 1024]), wout=dt("wout", [1024, 1024]),
                  wada_g=dt("wada_g", [1024, 1024]), bada_g=dt("bada_g", [1024]), lng=dt("lng", [1024]),
                  lnb=dt("lnb", [1024]))
    if mode != "post":
        kw.update(wada_f=dt("wada_f", [1024, 2048]), bada_f=dt("bada_f", [2048]),
                  hT_out=dt("hT", [1024, NTOK], BF16, kind="ExternalOutput"))
    kw.update(x_out=dt("xo", [NTOK, 1024], kind="ExternalOutput"))
    P = Prog(nc)
    emit_tok(P, nc, NTOK, consts, c_in, **kw)
    print("tok", mode, "ops", P.n_ops, "waits", P.n_waits)
    P.close()
    return nc

import numpy as np

D = 1024
RWW = 512
RW_R0, RW_K0, RW_V0, RW_WD0, RW_AD0, RW_END = 0, 512, 1024, 1536, 1600, 1664
FX_Q0, FX_K0, FX_V0, FX_F0, FX_END = 1664, 2176, 2688, 3200, 3208
GATE0 = 3208


def core_cols(g):
    ch = np.arange(128 * g, 128 * g + 128)
    l64 = np.arange(64)
    return np.concatenate([
        RW_R0 + ch, RW_K0 + ch, RW_V0 + ch, RW_WD0 + l64, RW_AD0 + l64, GATE0 + ch,
        FX_Q0 + ch, FX_K0 + ch,
        GATE0 + 512 + 128 * g + l64, GATE0 + 512 + 128 * g + 64 + l64,
        FX_V0 + ch, FX_F0 + np.array([2 * g, 2 * g + 1])])


def pack_core(inp, l, g):
    ch = np.arange(128 * g, 128 * g + 128)
    l64 = np.arange(64)
    wcore = np.ascontiguousarray(inp["w_in"][l][:, core_cols(g)])
    mix = inp["rwkv_mix"][l]
    vecs = np.stack([
        mix[RW_R0 + ch], mix[RW_K0 + ch], mix[RW_V0 + ch],
        np.concatenate([mix[RW_WD0 + l64], mix[RW_AD0 + l64]]),
        inp["w0"][l][ch], inp["a0"][l][ch], inp["k_k"][l][ch], inp["k_a"][l][ch],
        inp["r_k"][l][ch], inp["gn_g"][l][ch], inp["gn_b"][l][ch]], axis=1).astype(np.float32)
    up = np.concatenate([inp["w_up"][l][:, ch], inp["a_up"][l][:, ch]], axis=0).astype(np.float32)
    fbf = inp["fox_bf"][l][[2 * g, 2 * g + 1]][:, None].astype(np.float32)
    return dict(wcore=wcore, vecs=np.ascontiguousarray(vecs), up=np.ascontiguousarray(up),
                fbf=np.ascontiguousarray(fbf))


T_SEQ = 16384
NTOK = 4096
RG = [[0, 1, 2, 3], [4, 5, 6, 7]]
_NC_CACHE = {}


def build_fused(T=T_SEQ, NT=NTOK):
    nc = bass.Bass("TRN2", target_bir_lowering=False)
    dt = lambda n, s, d=F32, kind="ExternalInput": nc.dram_tensor(n, s, d, kind=kind).ap()
    NS = NT // 512
    NK = T // 2048
    consts = dt("consts", [128, 1280])
    c_in = dt("c", [1024])
    x_in = dt("x", [NT, 1024])
    embg = dt("embg", [1024])
    embb = dt("embb", [1024])
    L = []
    for l in range(2):
        L.append(dict(
            wada_f=dt(f"wada_f{l}", [1024, 2048]), bada_f=dt(f"bada_f{l}", [2048]),
            wada_g=dt(f"wada_g{l}", [1024, 1024]), bada_g=dt(f"bada_g{l}", [1024]),
            wout=dt(f"wout{l}", [1024, 1024]), lng=dt(f"lng{l}", [1024]), lnb=dt(f"lnb{l}", [1024]),
            wcore=dt(f"wcore{l}", [1024, NCOL]), vecs=dt(f"vecs{l}", [128, 11]), up=dt(f"up{l}", [128, 128]),
            fbf=dt(f"fbf{l}", [2, 1])))
    xo = dt("xo", [NT, 1024], kind="ExternalOutput")
    xd = [nc.dram_tensor(f"xd{l}", [NT, 1024], F32).ap() for l in range(2)]
    hloc = [nc.dram_tensor(f"hloc{l}", [NS, 1024, 512], BF16).ap() for l in range(2)]
    hg = [nc.dram_tensor(f"hg{l}", [NS, 4, 1024, 512], BF16).ap() for l in range(2)]
    yc = [nc.dram_tensor(f"yc{l}", [NK, 256, 2048], BF16).ap() for l in range(2)]
    yg = [nc.dram_tensor(f"yg{l}", [NK, 4, 256, 2048], BF16).ap() for l in range(2)]
    q = nc.partition_id() % 4
    P = Prog(nc)

    def front_hooks(l):
        h_dst = lambda s: hloc[l][s].rearrange("(c p) t -> p c t", p=128)
        after_h = lambda s: P.cc("AllGather", hloc[l][s].opt(), hg[l][s].opt(), RG,
                                 reads=[f"htile{s}"], writes=[f"hg{s}"])
        return dict(h_dst=h_dst, after_h=after_h, wada_f=L[l]["wada_f"], bada_f=L[l]["bada_f"])

    P.begin_scope()
    emit_tok(P, nc, NT, consts, c_in, x_in=x_in, embg=embg, embb=embb, x_out=xd[0], pfx="a", **front_hooks(0))
    P.end_scope()
    for l in range(2):
        P.begin_scope()

        def after_tile(j, l=l):
            if j % 4 == 3:
                kk = j // 4
                keys = []
                for jj in range(j - 3, j + 1):
                    keys += [f"ytile{jj}_a", f"ytile{jj}_b0", f"ytile{jj}_b1"]
                P.cc("AllGather", yc[l][kk].opt(), yg[l][kk].opt(), RG, reads=keys, writes=[f"yg{kk}"])

        emit_mix(P, nc, T, None, L[l]["wcore"], L[l]["vecs"], L[l]["up"], L[l]["fbf"], consts, None,
                 h_src=lambda j, l=l: hg[l][j % NS, j // NS].rearrange("(c p) t -> p c t", p=128),
                 y_dst=lambda r0, r1, j, l=l: yc[l][j // 4, r0:r1, (j % 4) * 512:(j % 4 + 1) * 512],
                 after_tile=after_tile, h_keys=lambda j: [f"hg{j % NS}"])
        P.end_scope()
        P.begin_scope()

        def y_src(s, l=l):
            v = yg[l][bass.ds(2 * q + s // 4, 1)]
            return v.rearrange("o r (h p) t -> p (o r h) t", p=128)[:, :, (s % 4) * 512:(s % 4 + 1) * 512]

        kw = dict(y_src=y_src, y_keys=[f"yg{k_}" for k_ in range(NK)], xprev=xd[l], wout=L[l]["wout"], wada_g=L[l]["wada_g"], bada_g=L[l]["bada_g"],
                  lng=L[l]["lng"], lnb=L[l]["lnb"], pfx=f"b{l}")
        if l == 0:
            kw.update(front_hooks(1))
            kw.update(x_out=xd[1])
        else:
            kw.update(x_out=xo)
        emit_tok(P, nc, NT, consts, c_in, **kw)
        P.end_scope()
    print("fused ops", P.n_ops, "waits", P.n_waits)
    P.close()
    return nc


def wout_perm():
    idx = []
    for r in range(4):
        idx += list(range(128 * r, 128 * r + 128))
        idx += list(range(512 + 128 * r, 512 + 128 * r + 128))
    return np.array(idx)


def kernel(**inp):
    inp = {k: np.ascontiguousarray(np.asarray(v)) for k, v in inp.items()}
    consts = make_consts()
    ca = np.ascontiguousarray
    if "nc" not in _NC_CACHE:
        _NC_CACHE["nc"] = build_fused()
    nc = _NC_CACHE["nc"]
    perm = wout_perm()
    maps = []
    for core in range(8):
        b, g = core // 4, core % 4
        m = dict(consts=consts, c=ca(inp["c"][b]), x=ca(inp["x"][b][g * NTOK:(g + 1) * NTOK]),
                 embg=inp["emb_ln_g"], embb=inp["emb_ln_b"])
        for l in range(2):
            pk = pack_core(inp, l, g)
            m.update({f"wada_f{l}": ca(inp["w_ada"][l][:, 0:2048]), f"bada_f{l}": ca(inp["b_ada"][l][0:2048]),
                      f"wada_g{l}": ca(inp["w_ada"][l][:, 2048:3072]), f"bada_g{l}": ca(inp["b_ada"][l][2048:3072]),
                      f"wout{l}": ca(inp["w_out"][l][perm]), f"lng{l}": inp["ln_g"][l], f"lnb{l}": inp["ln_b"][l],
                      f"wcore{l}": pk["wcore"], f"vecs{l}": pk["vecs"], f"up{l}": pk["up"], f"fbf{l}": pk["fbf"]})
        maps.append(m)
    res = run_bass_kernel_spmd(nc, maps, core_ids=list(range(8))).results
    out = np.stack([np.concatenate([res[b * 4 + g]["xo"] for g in range(4)], axis=0) for b in range(2)], axis=0)
    return np.asarray(out, dtype=np.float32)
```

```python
import contextlib
import numpy as np
import concourse.bass as bass
import concourse.mybir as mybir
from concourse.bass_utils import run_bass_kernel_spmd

F32 = mybir.dt.float32
BF16 = mybir.dt.bfloat16
AF = mybir.ActivationFunctionType
ALU = mybir.AluOpType
AX = mybir.AxisListType

SEM_EPOCH = 20000


class Prog:
    def __init__(self, nc):
        self.nc = nc
        self.es = contextlib.ExitStack()
        self.eng = {"pe": nc.tensor, "act": nc.scalar, "dve": nc.vector,
                    "pool": nc.gpsimd, "sp": nc.sync}
        self.cnt = {e: 0 for e in self.eng}
        self.epoch = {e: 0 for e in self.eng}
        self.esem = {}
        for e in self.eng:
            self.esem[e] = self._newsem(f"s_{e}_0")
        self.seen = {e: {} for e in self.eng}
        self.sems = {}
        self.dcnt = {}
        self.lastw = {}
        self.reads = {}
        self.n_ops = 0
        self.n_waits = 0
        self.tes = self.es
        self.scope_id = 0
        self.ccsem = self._newsem("s_cc")
        self.ccn = 0
        self.kalias = {}

    def _newsem(self, name):
        return self.es.enter_context(self.nc.semaphore(name))

    def sb(self, name, shape, dt):
        return self.tes.enter_context(self.nc.sbuf_tensor(f"s{self.scope_id}_{name}", list(shape), dt))

    def ps(self, name, shape, dt=F32):
        return self.tes.enter_context(self.nc.psum_tensor(f"s{self.scope_id}_{name}", list(shape), dt))

    def begin_scope(self):
        self.scope_id += 1
        self.tes = contextlib.ExitStack()

    def end_scope(self):
        self.barrier()
        self.tes.close()
        self.tes = self.es

    def barrier(self):
        toks = []
        for f in self.eng:
            if self.cnt[f] > 0:
                toks.append((self.esem[f], self.cnt[f], f))
        for name, sem in self.sems.items():
            if self.dcnt[name] > 0:
                toks.append((sem, self.dcnt[name], "dma"))
        for e in self.eng:
            need = {id(sm): (sm, v) for (sm, v, o) in toks if o != e}
            self._emit_waits(e, need)
        keep = {k: t for k, t in self.lastw.items() if t[2] == "cc"}
        self.lastw.clear()
        self.reads.clear()
        self.lastw.update(keep)

    def cc(self, kind, in_ap, out_ap, rg, reads=(), writes=()):
        need = self._need("pool", reads, writes)
        self._emit_waits("pool", need)
        ins = self.nc.gpsimd.collective_compute(kind, ALU.bypass, replica_groups=rg, ins=[in_ap], outs=[out_ap])
        self.ccn += 1
        ins.then_inc(self.ccsem, 1)
        tok = (self.ccsem, self.ccn, "cc")
        self._commit("cc", tok, reads, writes)
        self.n_ops += 1
        return tok

    def close(self):
        if self.ccn > 0:
            self._emit_waits("pool", {id(self.ccsem): (self.ccsem, self.ccn)})
        self.es.close()

    def _need(self, e, reads, writes):
        need = {}

        def add(tok, kind):
            if tok is None:
                return
            sem, val, owner = tok
            if owner == e:
                if e in ("pe", "sp"):
                    return
                if kind == "war":
                    return
            k = id(sem)
            if k not in need or need[k][1] < val:
                need[k] = (sem, val)

        for r in reads:
            add(self.lastw.get(r), "raw")
        for w in writes:
            add(self.lastw.get(w), "waw")
            for tok in self.reads.get(w, {}).values():
                add(tok, "war")
        return need

    def _emit_waits(self, e, need):
        eng = self.eng[e]
        seen = self.seen[e]
        for k, (sem, val) in need.items():
            if seen.get(k, 0) >= val:
                continue
            eng.wait_ge(sem, val)
            seen[k] = val
            self.n_waits += 1

    def _commit(self, e, tok, reads, writes):
        for w in writes:
            self.lastw[w] = tok
            self.reads[w] = {}
        for r in reads:
            self.reads.setdefault(r, {})[(e, id(tok[0]))] = tok

    @staticmethod
    def _is_psum(k):
        return isinstance(k, str) and (k.startswith("ps") or "pp" in k)

    def alias(self, a, b):
        self.kalias[a] = b

    def _ka(self, keys):
        if not self.kalias:
            return keys
        return [self.kalias.get(k, k) for k in keys]

    def op(self, e, fn, reads=(), writes=()):
        reads, writes = self._ka(reads), self._ka(writes)
        px = [k for k in reads if self._is_psum(k)]
        if px:
            reads = [k for k in reads if not self._is_psum(k)]
            writes = list(writes) + px
        need = self._need(e, reads, writes)
        self._emit_waits(e, need)
        if self.cnt[e] >= SEM_EPOCH:
            self.epoch[e] += 1
            self.esem[e] = self._newsem(f"s_{e}_{self.epoch[e]}")
            self.cnt[e] = 0
        ins = fn()
        self.cnt[e] += 1
        ins.then_inc(self.esem[e], 1)
        tok = (self.esem[e], self.cnt[e], e)
        self._commit(e, tok, reads, writes)
        self.n_ops += 1
        return tok

    def dma(self, q, out, in_, dsem, reads=(), writes=(), **kw):
        reads, writes = self._ka(reads), self._ka(writes)
        need = self._need(q, reads, writes)
        self._emit_waits(q, need)
        if dsem not in self.sems:
            self.sems[dsem] = self._newsem("d_" + dsem)
            self.dcnt[dsem] = 0
        sem = self.sems[dsem]
        ins = self.eng[q].dma_start(out=out, in_=in_, **kw)
        self.dcnt[dsem] += 16
        ins.then_inc(sem, 16)
        tok = (sem, self.dcnt[dsem], "dma")
        self._commit("dma", tok, reads, writes)
        self.n_ops += 1
        return tok

    def wait_all(self, e, keys):
        need = {}
        for k in keys:
            tok = self.lastw.get(k)
            if tok is None:
                continue
            kk = id(tok[0])
            if kk not in need or need[kk][1] < tok[1]:
                need[kk] = (tok[0], tok[1])
        self._emit_waits(e, need)

import numpy as np

NCOL = 1154
C_R, C_K, C_V, C_WA, C_GA, C_FQ, C_FK = 0, 128, 256, 384, 512, 640, 768
C_GB0, C_GB1, C_FV, C_F = 896, 960, 1024, 1152
K_ID, K_BO, K_SU, K_UI, K_SL, K_RS, K_ONE = 0, 128, 256, 384, 512, 640, 1152
NCONST = 1280
GN_EPS = 64e-5
RW_DT = BF16
RW_ROUNDS = 33.0
OVERLAP_PROJ = False
NEU_BANKS = 3
FIN_OVERLAP = False
DRAIN_OVERLAP = False
DRAIN_MIN = 100


def make_consts():
    c = np.zeros((128, NCONST), np.float32)
    i = np.arange(128)
    c[:, K_ID:K_ID + 128] = np.eye(128)
    c[:, K_BO:K_BO + 128] = (i[:, None] // 64 == i[None, :] // 64)
    c[:, K_SU:K_SU + 128] = (i[:, None] < i[None, :])
    c[:, K_UI:K_UI + 128] = (i[:, None] <= i[None, :])
    c[:, K_SL:K_SL + 128] = (i[:, None] > i[None, :])
    rs = np.ones(512, np.float32)
    rs[::128] = 0
    c[:, K_RS:K_RS + 512] = rs[None, :]
    c[:, K_ONE:K_ONE + 128] = 1.0
    return c


def interleave(gens, weights):
    gens = list(gens)
    acc = [0.0] * len(gens)
    alive = [True] * len(gens)
    while any(alive):
        for i, g in enumerate(gens):
            if not alive[i]:
                continue
            acc[i] += weights[i]
            while acc[i] >= 1.0 and alive[i]:
                acc[i] -= 1.0
                try:
                    next(g)
                except StopIteration:
                    alive[i] = False


def emit_mix(P, nc, T, hT, wcore, vecs, up, fbf, consts, yT, do_rwkv=True, do_fox=True,
             h_src=None, y_dst=None, after_tile=None, h_keys=None):
    NT = T // 512
    NB = T // 128
    V = nc.vector
    A = nc.scalar
    G = nc.gpsimd
    PE = nc.tensor

    cst = P.sb("cst", [128, NCONST], F32)
    vec = P.sb("vec", [128, 20], F32)
    fbh = P.sb("fbh", [2, 1], F32)
    upt = P.sb("upt", [128, 128], F32)
    fbt = P.sb("fbt", [2, 1], F32)
    P.dma("sp", cst[:], consts[:, :], "ld_cst", writes=["cst"])
    P.dma("sp", vec[:, 0:11], vecs[:, :], "ld_vec", writes=["vec"])
    P.dma("sp", upt[:], up[:, :], "ld_upt", writes=["upt"])
    P.dma("sp", fbt[:], fbf[:, :], "ld_fbt", writes=["fbt"])
    ident = cst[:, K_ID:K_ID + 128]
    bones = cst[:, K_BO:K_BO + 128]
    mask2 = cst[:, K_SU:K_SU + 256]
    msl = cst[:, K_SL:K_SL + 128]
    rsm = cst[:, K_RS:K_RS + 512]
    ones = cst[:, K_ONE:K_ONE + 128]
    P.op("dve", lambda: V.tensor_scalar(out=vec[:, 11:15], in0=vec[:, 0:4], scalar1=-1.0, scalar2=1.0,
                                        op0=ALU.mult, op1=ALU.add), reads=["vec"], writes=["vec"])
    P.op("dve", lambda: V.tensor_scalar(out=vec[:, 15:16], in0=vec[:, 7:8], scalar1=-1.0, scalar2=1.0,
                                        op0=ALU.mult, op1=ALU.add), reads=["vec"], writes=["vec"])
    P.op("dve", lambda: V.tensor_scalar(out=vec[:, 16:18], in0=vec[:, 4:6], scalar1=0.5, scalar2=None,
                                        op0=ALU.mult), reads=["vec"], writes=["vec"])
    P.op("dve", lambda: V.tensor_scalar(out=fbh[:], in0=fbt[:], scalar1=0.5, scalar2=None, op0=ALU.mult),
         reads=["fbt"], writes=["fbh"])
    uib = P.sb("uib", [128, 128], BF16)
    idb = P.sb("idb", [128, 128], BF16)
    P.op("dve", lambda: V.tensor_copy(out=idb[:], in_=cst[:, K_ID:K_ID + 128]), reads=["cst"], writes=["idb"])
    P.op("dve", lambda: V.tensor_copy(out=uib[:], in_=cst[:, K_UI:K_UI + 128]), reads=["cst"], writes=["uib"])

    wsb = P.sb("wsb", [128, 8, NCOL], BF16)
    wst = [P.sb(f"wst{i}", [128, 8, 64], F32) for i in range(2)]
    wv = wcore.rearrange("(c p) n -> p c n", p=128)
    pieces = [(s, min(64, NCOL - s)) for s in range(0, NCOL, 64)]
    for pi, (s, n) in enumerate(pieces):
        b = pi % 2
        P.dma("sp", wst[b][:, :, 0:n], wv[:, :, s:s + n], f"ld_w{b}", writes=[f"wst{b}"])
        eng = "act" if pi % 2 == 0 else "dve"
        if eng == "act":
            P.op("act", lambda: A.copy(out=wsb[:, :, s:s + n], in_=wst[b][:, :, 0:n]),
                 reads=[f"wst{b}"], writes=["wsb"])
        else:
            P.op("dve", lambda: V.tensor_copy(out=wsb[:, :, s:s + n], in_=wst[b][:, :, 0:n]),
                 reads=[f"wst{b}"], writes=["wsb"])

    hTt = [P.sb(f"hT{i}", [128, 8, 512], BF16) for i in range(2)]
    if h_src is None:
        hv = hT.rearrange("(c p) t -> p c t", p=128)
        h_src = lambda j: hv[:, :, j * 512:(j + 1) * 512]
    if y_dst is None:
        y_dst = lambda r0, r1, j: yT[r0:r1, j * 512:(j + 1) * 512]
    NSB = 3 if OVERLAP_PROJ else (6 - NEU_BANKS)
    NPT = 4
    LOOK = 2 if OVERLAP_PROJ else (5 - NEU_BANKS)
    psS = [P.ps(f"psS{i}", [128, 512]) for i in range(NSB)]
    psO = P.ps("psO", [128, 512])
    psA = [P.ps(f"psA{i}", [128, 512]) for i in range(NEU_BANKS)]
    ppb = P.ps("ppb", [128, 512]) if OVERLAP_PROJ else None
    psY = P.ps("psY", [128, 512])
    ppc = [0]

    def nextpp():
        if OVERLAP_PROJ:
            return ppb, "ppb"
        i = ppc[0] % 2
        ppc[0] += 1
        return psA[i], f"psA{i}"

    KT = P.sb("KT", [128, T], BF16)
    Vaug = P.sb("Vaug", [128, NB, 2, 65], BF16)
    ckres = P.sb("ckres", [128, NB, 2], F32)
    P.op("pool", lambda: G.memset(Vaug[:, :, :, 64:65], 1.0), writes=["Vaug_ones"])
    QT = [[P.sb(f"QT{i}_{h}", [128, 512], BF16) for h in range(2)] for i in range(2)]
    for i in range(2):
        for h in range(2):
            P.op("pool", lambda: G.memset(QT[i][h][:], 0.0), writes=[f"QT{i}"])
    sgb = [[P.sb(f"sgb{i}_{h}", [64, 512], F32) for h in range(2)] for i in range(2)]
    rrow = P.sb("rrow", [128, 512], F32)
    lft = P.sb("lft", [2, 512], F32)
    lf = lft[:, :]
    onesrow = rrow[0:2, :]
    cum = [P.sb(f"cum{i}", [2, 512], F32) for i in range(2)]
    P.op("dve", lambda: V.memset(cum[1][:], 0.0), writes=["cum1"])
    i2 = P.sb("i2", [2, 2], F32)
    basebc = [P.sb(f"basebc{i}", [128, 2], F32) for i in range(2)]
    biasj = [P.sb(f"biasj{i}", [128, NB, 2], F32) for i in range(2)]
    PT = [P.sb(f"PT{i}", [128, 512], BF16) for i in range(NPT)]
    t1 = P.sb("t1", [64, 512], F32)
    osb = P.sb("osb", [65, 512], F32)
    ybo = [P.sb(f"ybo{i}", [64, 512], BF16) for i in range(2)]

    raw = {g: P.sb(f"raw_{g}", [128, 513], F32) for g in "rkvw"}
    for g in "rkvw":
        P.op("pool", lambda: G.memset(raw[g][:, 0:1], 0.0), writes=[f"raw_{g}c"])
    tmp = P.sb("tmp", [128, 512], F32)
    sh = {g: P.sb(f"sh_{g}", [128, 512], F32) for g in "rkvw"}
    a_t = P.sb("a_t", [128, 512], F32)
    ld = P.sb("ld", [128, 512], F32)
    cl = P.sb("cl", [128, 512], F32)
    einc = P.sb("einc", [128, 512], F32)
    eneg = P.sb("eneg", [128, 512], F32)
    kk = P.sb("kk", [128, 512], F32)
    kmod = tmp
    P.alias("kmod", "tmp")
    w1 = P.sb("w1", [128, 512], F32)
    w2 = P.sb("w2", [128, 512], F32)
    KRz = [P.sb(f"KRz{i}", [128, 4, 2, 128], RW_DT) for i in range(2)]
    for i in range(2):
        P.op("pool", lambda: G.memset(KRz[i][:], 0.0), writes=["KR"])
    Bt = P.sb("Bt", [128, 512], RW_DT)
    Kt = P.sb("Kt", [128, 512], RW_DT)
    bonus = ld
    P.alias("bonus", "ld")
    sgaL = [P.sb(f"sga{i}", [128, 512], F32) for i in range(2)]
    Vtok = P.sb("Vtok", [128, 4, 128], RW_DT)
    Btok = P.sb("Btok", [128, 4, 128], RW_DT)
    Ktok = P.sb("Ktok", [128, 4, 128], RW_DT)
    ST = P.sb("ST", [128, 64], F32)
    STw = P.sb("STw", [128, 64], F32)
    STb = P.sb("STb", [128, 128], RW_DT)
    P.op("dve", lambda: V.memset(ST[:], 0.0), writes=["ST0", "ST1"])
    P.op("dve", lambda: V.memset(STb[:], 0.0), writes=["STb0", "STb1"])
    AT = [[P.sb(f"AT{i}_{k}", [128, 256], RW_DT) for k in range(4)] for i in range(2)]
    AK = [[P.sb(f"AK{i}_{k}", [128, 256], RW_DT) for k in range(4)] for i in range(2)]
    XL = [[P.sb(f"XL{i}_{k}", [128, 384], RW_DT) for k in range(4)] for i in range(2)]
    Tm = [[XL[i][k][:, 256:384] for k in range(4)] for i in range(2)]
    TmF = Tm
    Zs = P.sb("Zs", [128, 128], RW_DT)
    Us = P.sb("Us", [128, 128], RW_DT)
    P.op("pool", lambda: G.memset(Us[:], 0.0), writes=["Us0", "Us1"])
    P.op("pool", lambda: G.memset(Zs[:], 0.0), writes=["Zs0", "Zs1"])
    ysb = cl
    for k_ in ("ysb", "ysb0", "ysb1"):
        P.alias(k_, "cl")
    yao = [P.sb(f"yao{i}", [128, 512], BF16) for i in range(2)]

    def proj_group(j, col, M, rhs_tile, rhs_key):
        ps, pk = nextpp()
        for c in range(8):
            P.op("pe", lambda: PE.matmul(ps[0:M, :], lhsT=wsb[:, c, col:col + M], rhs=rhs_tile[:, c, :],
                                         start=(c == 0), stop=(c == 7)),
                 reads=["wsb", rhs_key], writes=[pk])
        return ps, pk

    def load_h(j):
        b = j % 2
        P.dma("sp", hTt[b][:], h_src(j), f"ld_h{b}", reads=(h_keys(j) if h_keys else ()), writes=[f"hT{b}"])

    def tile_proj(j):
        b = j % 2
        ht, hk = hTt[b], f"hT{b}"
        sga = sgaL[b]
        sgak = f"sga{b}"
        if do_rwkv:
            for gi, (g, col) in enumerate((("r", C_R), ("k", C_K), ("v", C_V), ("w", C_WA))):
                ps, pk = proj_group(j, col, 128, ht, hk)
                rw = raw[g]
                if j > 0:
                    P.op("act", lambda: A.copy(out=rw[:, 0:1], in_=rw[:, 512:513]), reads=[f"raw_{g}"],
                         writes=[f"raw_{g}c"])
                P.op("dve", lambda: V.tensor_copy(out=rw[:, 1:513], in_=ps[:, :]), reads=[pk], writes=[f"raw_{g}"])
                P.op("act", lambda: A.activation(out=tmp[:], in_=ps[:, :], func=AF.Identity,
                                                 scale=vec[:, 11 + gi:12 + gi]),
                     reads=[pk, "vec"], writes=["tmp"])
                P.op("dve", lambda: V.scalar_tensor_tensor(out=sh[g][:], in0=rw[:, 0:512], scalar=vec[:, gi:gi + 1],
                                                           in1=tmp[:], op0=ALU.mult, op1=ALU.add),
                     reads=[f"raw_{g}", f"raw_{g}c", "tmp", "vec"], writes=[f"sh_{g}"])
                yield
            ps, pk = proj_group(j, C_GA, 128, ht, hk)
            P.op("act", lambda: A.activation(out=sga[:], in_=ps[:, :], func=AF.Tanh, scale=0.5), reads=[pk],
                 writes=[sgak])
            P.op("dve", lambda: V.scalar_tensor_tensor(out=sga[:], in0=sga[:], scalar=1.0, in1=ps[:, :],
                                                       op0=ALU.add, op1=ALU.mult), reads=[pk, "sga"], writes=[sgak])
        if do_fox:
            ps, pk = proj_group(j, C_FQ, 128, ht, hk)
            for h in range(2):
                hp_ = slice(64 * h, 64 * h + 64)
                P.op("act", lambda: A.activation(out=QT[b][h][hp_, :], in_=ps[hp_, :], func=AF.Copy, scale=0.125),
                     reads=[pk], writes=[f"QT{b}"])
            yield
            ps, pk = proj_group(j, C_FK, 128, ht, hk)
            P.op("dve", lambda: V.tensor_copy(out=KT[:, j * 512:(j + 1) * 512], in_=ps[:, :]),
                 reads=[pk], writes=[f"KT{j}"])
            yield
            for h, col in ((0, C_GB0), (1, C_GB1)):
                ps, pk = proj_group(j, col, 64, ht, hk)
                P.op("act", lambda: A.activation(out=sgb[b][h][:], in_=ps[0:64, :], func=AF.Tanh, scale=0.5),
                     reads=[pk], writes=[f"sgb{b}_{h}"])
                P.op("dve", lambda: V.scalar_tensor_tensor(out=sgb[b][h][:], in0=sgb[b][h][:], scalar=1.0,
                                                           in1=ps[0:64, :], op0=ALU.add, op1=ALU.mult),
                     reads=[pk, f"sgb{b}_{h}"], writes=[f"sgb{b}_{h}"])
                yield
            ps, pk = nextpp()
            for q in range(4):
                for c in range(8):
                    P.op("pe", lambda: PE.matmul(ps[:, q * 128:(q + 1) * 128], lhsT=ht[:, c, q * 128:(q + 1) * 128],
                                                 rhs=wsb[:, c, C_FV:C_FV + 128], start=(c == 0), stop=(c == 7)),
                         reads=["wsb", hk], writes=[pk])
            P.op("dve", lambda: V.tensor_copy(
                out=Vaug[:, 4 * j:4 * j + 4, :, 0:64],
                in_=ps[:, :].rearrange("p (q h d) -> p q h d", q=4, h=2)),
                reads=[pk], writes=[f"V{j}"])
            yield
            ps, pk = proj_group(j, C_F, 2, ht, hk)
            P.op("act", lambda: A.activation(out=lf, in_=ps[0:2, :], func=AF.Tanh, bias=fbh[:, 0:1], scale=0.5),
                 reads=[pk, "fbh"], writes=["lf"])
            P.op("dve", lambda: V.tensor_scalar(out=lf, in0=lf, scalar1=0.5, scalar2=0.5, op0=ALU.mult, op1=ALU.add),
                 reads=["lf"], writes=["lf"])
            P.op("act", lambda: A.activation(out=lf, in_=lf, func=AF.Ln), reads=["lf"], writes=["lf"])
            cprev, cb = cum[1 - b], cum[b]
            P.op("dve", lambda: V.tensor_tensor_scan(out=cb[:], data0=onesrow, data1=lf,
                                                     initial=cprev[:, 511:512], op0=ALU.mult, op1=ALU.add),
                 reads=["lf", f"cum{1-b}", "onesrow"], writes=[f"cum{b}"])
            yield
            ps, pk = nextpp()
            for q in range(4):
                P.op("pe", lambda: PE.matmul(ps[:, 2 * q:2 * q + 2], lhsT=cb[0:2, q * 128:(q + 1) * 128],
                                             rhs=cst[0:2, K_ID:K_ID + 2], start=True, stop=True),
                     reads=[f"cum{b}", "cst"], writes=[pk])
            P.op("dve", lambda: V.tensor_scalar(out=i2[:], in0=cst[0:2, K_ID:K_ID + 2], scalar1=cb[:, 255:256],
                                                scalar2=None, op0=ALU.mult),
                 reads=[f"cum{b}", "cst"], writes=["i2"])
            P.op("pe", lambda: PE.matmul(ps[:, 8:10], lhsT=ones[0:2, :], rhs=i2[:, :], start=True, stop=True),
                 reads=["i2", "cst"], writes=[pk])
            P.op("dve", lambda: V.tensor_copy(out=ckres[:, 4 * j:4 * j + 4, :],
                                              in_=ps[:, 0:8].rearrange("p (q h) -> p q h", q=4)),
                 reads=[pk], writes=[f"ck{j}"])
            P.op("dve", lambda: V.tensor_copy(out=basebc[b][:], in_=ps[:, 8:10]), reads=[pk], writes=[f"basebc{b}"])
        yield

    P.op("dve", lambda: V.memset(onesrow, 1.0), writes=["onesrow"])

    def attn_tile(j):
        b = j % 2
        nb = 4 * j + 4
        steps = [(h, kb) for h in range(2) for kb in range(nb)]
        for h in range(2):
            bj = biasj[h]
            P.op("dve", lambda: V.tensor_scalar(out=bj[:, 0:nb, h], in0=ckres[:, 0:nb, h], scalar1=-1.0,
                                                scalar2=basebc[b][:, h:h + 1], op0=ALU.mult, op1=ALU.add),
                 reads=[f"ck{jj}" for jj in range(j + 1)] + [f"basebc{b}"], writes=[f"biasj{h}"])
        yield

        def q0_of(kb):
            m = kb - 4 * j
            return 128 * m if m > 0 else 0

        def emit_S(i):
            h, kb = steps[i]
            hp = slice(64 * h, 64 * h + 64)
            q0 = q0_of(kb)
            si = i % NSB
            P.op("pe", lambda: PE.matmul(psS[si][:, q0:512], lhsT=KT[:, kb * 128:(kb + 1) * 128],
                                         rhs=QT[b][h][:, q0:512], start=True, stop=True),
                 reads=[f"KT{kb // 4}", f"QT{b}"], writes=[f"psS{si}"])

        for i0 in range(min(LOOK, len(steps))):
            emit_S(i0)
        for i, (h, kb) in enumerate(steps):
            if i + LOOK < len(steps):
                emit_S(i + LOOK)
            bj = biasj[h]
            m = kb - 4 * j
            q0 = q0_of(kb)
            si = i % NSB
            pi = i % NPT
            P.op("act", lambda: A.activation(out=PT[pi][:, q0:512], in_=psS[si][:, q0:512], func=AF.Exp,
                                             bias=bj[:, kb, h:h + 1], scale=1.0),
                 reads=[f"psS{si}", f"biasj{h}"], writes=[f"PT{pi}"])
            if m >= 0:
                P.op("pool", lambda: G.tensor_tensor(out=PT[pi][:, q0:q0 + 128], in0=PT[pi][:, q0:q0 + 128],
                                                     in1=uib[:], op=ALU.mult),
                     reads=[f"PT{pi}", "uib"], writes=[f"PT{pi}"])
            P.op("pe", lambda: PE.matmul(psO[0:65, q0:512], lhsT=Vaug[:, kb, h, :], rhs=PT[pi][:, q0:512],
                                         start=(kb == 0), stop=(kb == nb - 1)),
                 reads=[f"PT{pi}", f"V{kb // 4}", "Vaug_ones"], writes=["psO"])
            if kb == nb - 1:
                P.op("dve", lambda: V.tensor_copy(out=osb[0:65, :], in_=psO[0:65, :]), reads=["psO"], writes=["osb"])
                P.op("dve", lambda: V.reciprocal(out=rrow[64:65, :], in_=osb[64:65, :]), reads=["osb"],
                     writes=["rrow"])
                pq, pqk = psS[si], f"psS{si}"
                P.op("pe", lambda: PE.matmul(pq[0:64, :], lhsT=ones[64:65, 0:64], rhs=rrow[64:65, :],
                                             start=True, stop=True),
                     reads=["rrow", "cst"], writes=[pqk])
                P.op("dve", lambda: V.scalar_tensor_tensor(out=t1[:], in0=pq[0:64, :], scalar=0.5, in1=sgb[b][h][:],
                                                           op0=ALU.mult, op1=ALU.mult),
                     reads=[pqk, f"sgb{b}_{h}"], writes=["t1"])
                P.op("pool", lambda: G.tensor_tensor(out=ybo[h][:], in0=osb[0:64, :], in1=t1[:], op=ALU.mult),
                     reads=["osb", "t1"], writes=[f"ybo{h}"])
                P.dma("sp", y_dst(128 + 64 * h, 192 + 64 * h, j), ybo[h][:], f"st_yb{h}",
                      reads=[f"ybo{h}"], writes=[f"yTb{h}", f"ytile{j}_b{h}"])
            yield

    def SLK(h, i0=0, i1=4):
        return [f"psA{h}"]

    psC = psY
    CK = ["psC"]

    progress = [0, 0]

    def rwkv_prep(j):
        for h_ in range(2):
            for c_ in range(4):
                ndone[h_][c_] = False
        pa, pak = psA[0], SLK(0)
        pb, pbk = psA[1], SLK(1)
        th = tmp
        P.op("act", lambda: A.activation(out=th[0:64, :], in_=sh["w"][0:64, :], func=AF.Tanh),
             reads=["sh_w"], writes=["tmp"])
        P.op("pe", lambda: PE.matmul(pa[:, :], lhsT=upt[0:64, :], rhs=th[0:64, :], start=True, stop=True),
             reads=["upt", "tmp"], writes=pak)
        P.op("pe", lambda: PE.matmul(pb[:, :], lhsT=upt[64:128, :], rhs=sh["w"][64:128, :], start=True, stop=True),
             reads=["upt", "sh_w"], writes=pbk)
        P.op("act", lambda: A.activation(out=ld[:], in_=pa[:, :], func=AF.Tanh, bias=vec[:, 16:17], scale=0.5),
             reads=pak + ["vec"], writes=["ld"])
        P.op("act", lambda: A.activation(out=a_t[:], in_=pb[:, :], func=AF.Tanh, bias=vec[:, 17:18], scale=0.5),
             reads=pbk + ["vec"], writes=["a_t"])
        yield
        P.op("dve", lambda: V.tensor_scalar(out=ld[:], in0=ld[:], scalar1=1.0, scalar2=-0.5 * float(np.exp(-0.5)),
                                            op0=ALU.add, op1=ALU.mult), reads=["ld"], writes=["ld"])
        P.op("pool", lambda: G.tensor_scalar(out=a_t[:], in0=a_t[:], scalar1=0.5, scalar2=0.5, op0=ALU.mult,
                                             op1=ALU.add), reads=["a_t"], writes=["a_t"])
        P.op("dve", lambda: V.tensor_tensor_scan(out=cl[:], data0=rsm, data1=ld[:], initial=0.0,
                                                 op0=ALU.mult, op1=ALU.add),
             reads=["ld", "cst"], writes=["cl"])
        P.op("act", lambda: A.activation(out=einc[:], in_=cl[:], func=AF.Exp), reads=["cl"], writes=["einc"])
        P.op("act", lambda: A.activation(out=eneg[:], in_=cl[:], func=AF.Exp, scale=-1.0),
             reads=["cl"], writes=["eneg"])
        P.op("dve", lambda: V.tensor_tensor(out=w1[:], in0=cl[:], in1=ld[:], op=ALU.subtract),
             reads=["cl", "ld"], writes=["w1"])
        P.op("act", lambda: A.activation(out=w1[:], in_=w1[:], func=AF.Exp), reads=["w1"], writes=["w1"])
        yield
        P.op("dve", lambda: V.tensor_scalar(out=kk[:], in0=sh["k"][:], scalar1=vec[:, 6:7], scalar2=None,
                                            op0=ALU.mult), reads=["sh_k", "vec"], writes=["kk"])
        P.op("dve", lambda: V.tensor_tensor(out=w2[:], in0=kk[:], in1=kk[:], op=ALU.mult),
             reads=["kk"], writes=["w2"])
        P.op("pe", lambda: PE.matmul(pa[:, :], lhsT=bones, rhs=w2[:], start=True, stop=True),
             reads=["cst", "w2"], writes=pak)
        P.op("dve", lambda: V.tensor_scalar(out=w2[:], in0=pa[:, :], scalar1=1e-24, scalar2=None, op0=ALU.max),
             reads=pak, writes=["w2"])
        P.op("act", lambda: A.activation(out=w2[:], in_=w2[:], func=AF.Ln), reads=["w2"], writes=["w2"])
        P.op("act", lambda: A.activation(out=w2[:], in_=w2[:], func=AF.Exp, scale=-0.5), reads=["w2"], writes=["w2"])
        P.op("dve", lambda: V.tensor_tensor(out=kk[:], in0=kk[:], in1=w2[:], op=ALU.mult),
             reads=["kk", "w2"], writes=["kk"])
        yield
        P.op("dve", lambda: V.tensor_scalar(out=w2[:], in0=a_t[:], scalar1=vec[:, 7:8], scalar2=vec[:, 15:16],
                                            op0=ALU.mult, op1=ALU.add), reads=["a_t", "vec"], writes=["w2"])
        P.op("dve", lambda: V.tensor_tensor(out=kmod[:], in0=sh["k"][:], in1=w2[:], op=ALU.mult),
             reads=["sh_k", "w2"], writes=["kmod"])
        P.op("dve", lambda: V.tensor_tensor(out=w2[:], in0=kk[:], in1=a_t[:], op=ALU.mult),
             reads=["kk", "a_t"], writes=["w2"])
        P.op("dve", lambda: V.tensor_tensor(out=Bt[:], in0=w2[:], in1=eneg[:], op=ALU.mult),
             reads=["w2", "eneg"], writes=["Bt"])
        P.op("pool", lambda: G.tensor_tensor(out=Kt[:], in0=kmod[:], in1=eneg[:], op=ALU.mult),
             reads=["kmod", "eneg"], writes=["Kt"])
        for hh in range(2):
            hq = slice(64 * hh, 64 * hh + 64)
            P.op("dve", lambda: V.tensor_tensor(out=KRz[hh][hq, :, 0, :],
                                                in0=kk[hq, :].rearrange("p (c t) -> p c t", c=4),
                                                in1=w1[hq, :].rearrange("p (c t) -> p c t", c=4), op=ALU.mult),
                 reads=["kk", "w1"], writes=["KR"])
            P.op("pool", lambda: G.tensor_tensor(out=KRz[hh][hq, :, 1, :],
                                                 in0=sh["r"][hq, :].rearrange("p (c t) -> p c t", c=4),
                                                 in1=einc[hq, :].rearrange("p (c t) -> p c t", c=4), op=ALU.mult),
                 reads=["sh_r", "einc"], writes=["KR"])
        yield
        P.op("dve", lambda: V.scalar_tensor_tensor(out=w2[:], in0=sh["r"][:], scalar=vec[:, 8:9], in1=kmod[:],
                                                   op0=ALU.mult, op1=ALU.mult),
             reads=["sh_r", "vec", "kmod"], writes=["w2"])
        P.op("pe", lambda: PE.matmul(pb[:, :], lhsT=bones, rhs=w2[:], start=True, stop=True),
             reads=["cst", "w2"], writes=pbk)
        P.op("dve", lambda: V.tensor_tensor(out=bonus[:], in0=pb[:, :], in1=sh["v"][:], op=ALU.mult),
             reads=pbk + ["sh_v"], writes=["bonus"])
        yield
        for (src, skey, dst, dkey, pq, pqk) in ((sh["v"], "sh_v", Vtok, "Vtok", pa, pak),
                                                (Bt, "Bt", Btok, "Btok", pb, pbk),
                                                (Kt, "Kt", Ktok, "Ktok", pa, pak)):
            isb = (src.dtype == BF16)
            pqv = pq[:, :].bitcast(BF16) if isb else pq[:, :]
            for c in range(4):
                P.op("pe", lambda: PE.transpose(pqv[:, c * 128:(c + 1) * 128], src[:, c * 128:(c + 1) * 128],
                                                idb[:] if isb else ident),
                     reads=[skey, "cst", "idb"], writes=pqk)
            P.op("dve", lambda: V.tensor_copy(out=dst[:].rearrange("p c t -> p (c t)"), in_=pqv[:, 0:512]),
                 reads=pqk, writes=[dkey])
            yield

    ndone = [[False] * 4, [False] * 4]

    def neumann(j, h, c):
        bi = h if NEU_BANKS == 2 else (2 * c + h) % NEU_BANKS
        pa = psA[bi]
        pk = SLK(bi)
        cs = slice(c * 128, (c + 1) * 128)
        at, atk = AT[h][c], f"AT{h}_{c}"
        ak, akk = AK[h][c], f"AK{h}_{c}"
        X, Xk = XL[h][c], f"TmF{h}_{c}"
        Lc, Ltc, Tc = X[:, 0:128], X[:, 128:256], X[:, 256:384]
        krc = KRz[h][:, c, :, :].rearrange("p a t -> p (a t)")
        P.op("pe", lambda: PE.matmul(pa[:, 0:256], lhsT=Bt[:, cs], rhs=krc, start=True, stop=True),
             reads=["Bt", "KR"], writes=pk)
        P.op("pe", lambda: PE.matmul(pa[:, 256:384], lhsT=KRz[h][:, c, 0, :], rhs=Bt[:, cs], start=True, stop=True),
             reads=["Bt", "KR"], writes=pk)
        P.op("dve", lambda: V.tensor_tensor(out=at[:], in0=pa[:, 0:256], in1=mask2, op=ALU.mult),
             reads=pk + ["cst"], writes=[atk])
        P.op("dve", lambda: V.tensor_tensor(out=Lc, in0=pa[:, 256:384], in1=msl, op=ALU.mult),
             reads=pk + ["cst"], writes=[Xk])
        P.op("pool", lambda: G.tensor_tensor(out=Tc, in0=ident, in1=at[:, 0:128], op=ALU.subtract),
             reads=[atk, "cst", Xk], writes=[Xk])
        yield
        P.op("pe", lambda: PE.matmul(pa[:, 0:128], lhsT=at[:, 0:128], rhs=Lc, start=True, stop=True),
             reads=[atk, Xk], writes=pk)
        P.op("pe", lambda: PE.matmul(pa[:, 128:256], lhsT=Lc, rhs=at[:, 0:128], start=True, stop=True),
             reads=[atk, Xk], writes=pk)
        P.op("pe", lambda: PE.matmul(pa[:, 256:512], lhsT=Kt[:, cs], rhs=krc, start=True, stop=True),
             reads=["Kt", "KR"], writes=pk)
        P.op("dve", lambda: V.tensor_copy(out=X[:, 0:256], in_=pa[:, 0:256]), reads=pk + [Xk], writes=[Xk])
        P.op("dve", lambda: V.tensor_tensor(out=ak[:], in0=pa[:, 256:512], in1=mask2, op=ALU.mult),
             reads=pk + ["cst"], writes=[akk])
        yield
        for k in range(1, 7):
            P.op("pe", lambda: PE.matmul(pa[:, 256:384], lhsT=Lc, rhs=Tc, start=True, stop=False),
                 reads=[Xk], writes=pk)
            P.op("pe", lambda: PE.matmul(pa[:, 256:384], lhsT=idb[:], rhs=Tc, start=False, stop=True),
                 reads=[Xk, "idb"], writes=pk)
            if k < 6:
                P.op("pe", lambda: PE.matmul(pa[:, 0:128], lhsT=Ltc, rhs=Lc, start=True, stop=True),
                     reads=[Xk], writes=pk)
            if k < 5:
                P.op("pe", lambda: PE.matmul(pa[:, 128:256], lhsT=Lc, rhs=Ltc, start=True, stop=True),
                     reads=[Xk], writes=pk)
            lo = 0 if k < 6 else 256
            P.op("dve", lambda: V.tensor_copy(out=X[:, lo:384], in_=pa[:, lo:384]), reads=pk + [Xk], writes=[Xk])
            yield
        ndone[h][c] = True

    def chain(j, h):
        pa = psC
        hp = slice(64 * h, 64 * h + 64)
        hc = slice(64 * h, 64 * h + 64)
        o0 = 256 * h
        stk, stbk, stwk, zk, uk = f"ST{h}", f"STb{h}", f"STw{h}", f"Zs{h}", f"Us{h}"
        for c in range(4):
            while not ndone[h][c]:
                yield
            par = c
            cs = slice(c * 128, (c + 1) * 128)
            wc = einc[hp, c * 128 + 127:c * 128 + 128]
            P.op("pe", lambda: PE.matmul(pa[:, o0:o0 + 64], lhsT=KRz[h][:, c, 0, :], rhs=STb[:, 0:64], start=True, stop=False),
                 reads=["KR", stbk], writes=CK)
            P.op("pe", lambda: PE.matmul(pa[:, o0:o0 + 64], lhsT=AK[h][par][:, 0:128], rhs=Vtok[:, c, hc],
                                         start=False, stop=True),
                 reads=[f"AK{h}_{par}", "Vtok"], writes=CK)
            P.op("dve", lambda: V.tensor_copy(out=Zs[:, hc], in_=pa[:, o0:o0 + 64]), reads=CK, writes=[zk])
            P.op("pool", lambda: G.tensor_scalar(out=STw[hp, :], in0=ST[hp, :], scalar1=wc, scalar2=None,
                                                 op0=ALU.mult), reads=[stk, "einc"], writes=[stwk])
            yield
            P.op("pe", lambda: PE.matmul(pa[:, o0 + 64:o0 + 128], lhsT=TmF[h][par], rhs=Zs[:, hc], start=True, stop=True),
                 reads=[f"TmF{h}_{par}", zk], writes=CK)
            P.op("dve", lambda: V.tensor_scalar(out=Us[:, hc], in0=pa[:, o0 + 64:o0 + 128], scalar1=-1.0,
                                                scalar2=None, op0=ALU.mult), reads=CK, writes=[uk])
            yield
            yo = pa[:, o0 + 128:o0 + 256]
            P.op("pe", lambda: PE.matmul(yo, lhsT=STb[:, :], rhs=KRz[h][:, c, 1, :], start=True, stop=False),
                 reads=[stbk, "KR"], writes=CK)
            P.op("pe", lambda: PE.matmul(yo, lhsT=Us[:, :], rhs=AT[h][par][:, 128:256], start=False, stop=False),
                 reads=[uk, f"AT{h}_{par}"], writes=CK)
            P.op("pe", lambda: PE.matmul(yo, lhsT=Vtok[:, c, :], rhs=AK[h][par][:, 128:256], start=False,
                                         stop=True), reads=["Vtok", f"AK{h}_{par}"], writes=CK)
            P.op("dve", lambda: V.tensor_copy(out=ysb[hp, cs], in_=pa[hp, o0 + 128:o0 + 256]), reads=CK,
                 writes=[f"ysb{h}"])
            P.op("pe", lambda: PE.matmul(pa[:, o0:o0 + 64], lhsT=Btok[:, c, :], rhs=Us[:, hc], start=True, stop=False),
                 reads=["Btok", uk], writes=CK)
            P.op("pe", lambda: PE.matmul(pa[:, o0:o0 + 64], lhsT=Ktok[:, c, :], rhs=Vtok[:, c, hc], start=False, stop=True),
                 reads=["Ktok", "Vtok"], writes=CK)
            P.op("dve", lambda: V.scalar_tensor_tensor(out=ST[hp, :], in0=pa[hp, o0:o0 + 64], scalar=wc, in1=STw[hp, :],
                                                       op0=ALU.mult, op1=ALU.add),
                 reads=CK + ["einc", stwk], writes=[stk])
            P.op("pool", lambda: G.tensor_copy(out=STb[hp, 0:64], in_=ST[hp, :]), reads=[stk], writes=[stbk])
            P.op("dve", lambda: V.tensor_copy(out=STb[hp, 64:128], in_=ST[hp, :]), reads=[stk], writes=[stbk])
            yield

    def rwkv_fin(j):
        b = j % 2
        pa, pak = psA[0], SLK(0)
        pb, pbk = psA[1], SLK(1)
        P.op("pool", lambda: G.tensor_tensor(out=w2[:], in0=ysb[:], in1=ysb[:], op=ALU.mult),
             reads=["ysb0", "ysb1", "ysb"], writes=["w2"])
        yield
        P.op("pe", lambda: PE.matmul(pa[:, :], lhsT=bones, rhs=ysb[:], start=True, stop=True),
             reads=["cst", "ysb0", "ysb1", "ysb"], writes=pak)
        P.op("pe", lambda: PE.matmul(pb[:, :], lhsT=bones, rhs=w2[:], start=True, stop=True),
             reads=["cst", "w2"], writes=pbk)
        P.op("act", lambda: A.activation(out=w1[:], in_=pa[:, :], func=AF.Square, scale=1.0 / 64), reads=pak,
             writes=["w1"])
        P.op("dve", lambda: V.scalar_tensor_tensor(out=ysb[:], in0=pa[:, :], scalar=-1.0 / 64, in1=ysb[:],
                                                   op0=ALU.mult, op1=ALU.add), reads=pak + ["ysb", "ysb0", "ysb1"],
             writes=["ysb", "ysb0", "ysb1"])
        P.op("dve", lambda: V.scalar_tensor_tensor(out=w1[:], in0=pb[:, :], scalar=1.0 / 64, in1=w1[:],
                                                   op0=ALU.mult, op1=ALU.subtract), reads=pbk + ["w1"], writes=["w1"])
        yield
        P.op("dve", lambda: V.tensor_scalar(out=w1[:], in0=w1[:], scalar1=GN_EPS, scalar2=None, op0=ALU.add),
             reads=["w1"], writes=["w1"])
        P.op("act", lambda: A.activation(out=w1[:], in_=w1[:], func=AF.Ln), reads=["w1"], writes=["w1"])
        P.op("act", lambda: A.activation(out=w1[:], in_=w1[:], func=AF.Exp, scale=-0.5), reads=["w1"], writes=["w1"])
        yield
        P.op("pool", lambda: G.tensor_tensor(out=ysb[:], in0=ysb[:], in1=w1[:], op=ALU.mult),
             reads=["ysb", "w1"], writes=["ysb"])
        P.op("dve", lambda: V.tensor_scalar(out=ysb[:], in0=ysb[:], scalar1=vec[:, 9:10], scalar2=vec[:, 10:11],
                                            op0=ALU.mult, op1=ALU.add), reads=["ysb", "vec"], writes=["ysb"])
        yield
        P.op("pool", lambda: G.tensor_tensor(out=ysb[:], in0=ysb[:], in1=bonus[:], op=ALU.add),
             reads=["ysb", "bonus"], writes=["ysb"])
        P.op("dve", lambda: V.scalar_tensor_tensor(out=yao[b][:], in0=ysb[:], scalar=0.5, in1=sgaL[b][:], op0=ALU.mult,
                                                   op1=ALU.mult), reads=["ysb", f"sga{b}"], writes=[f"yao{b}"])
        P.dma("sp", y_dst(0, 128, j), yao[b][:], f"st_ya{b}", reads=[f"yao{b}"], writes=[f"yTa{b}", f"ytile{j}_a"])
        yield

    def run_stage(prims, bg, ratio):
        alive = list(prims)
        acc = 0.0
        while alive:
            for g in list(alive):
                try:
                    next(g)
                except StopIteration:
                    alive.remove(g)
            if bg[0] is not None:
                acc += ratio
                while acc >= 1.0 and bg[0] is not None:
                    acc -= 1.0
                    try:
                        next(bg[0])
                    except StopIteration:
                        bg[0] = None

    def run_tile(j, nxt):
        bg = [attn_tile(j) if do_fox else None]
        n_attn = 2 * (4 * j + 4) + 1
        drain = bool(nxt) and DRAIN_OVERLAP and n_attn >= DRAIN_MIN
        r = n_attn / (RW_ROUNDS + (14.0 if drain else 0.0))
        if do_rwkv:
            run_stage([rwkv_prep(j)], bg, r)
            run_stage([neumann(j, h_, c_) for c_ in range(4) for h_ in range(2)] + [chain(j, 0), chain(j, 1)] + ([nxt] if (nxt and OVERLAP_PROJ) else []), bg, r)
            run_stage([rwkv_fin(j)] + ([nxt] if (nxt and FIN_OVERLAP) else []), bg, r)
            if drain:
                run_stage([nxt], bg, r)
        elif nxt:
            run_stage([nxt], bg, r)
        while bg[0] is not None:
            try:
                next(bg[0])
            except StopIteration:
                bg[0] = None

    load_h(0)
    if NT > 1:
        load_h(1)
    for _ in tile_proj(0):
        pass
    for j in range(NT):
        nxt = tile_proj(j + 1) if j + 1 < NT else None
        if j + 2 < NT:
            load_h(j + 2)
        if OVERLAP_PROJ:
            run_tile(j, nxt)
        else:
            run_tile(j, nxt if (FIN_OVERLAP or DRAIN_OVERLAP) else None)
            if nxt:
                for _ in nxt:
                    pass
        if after_tile is not None:
            after_tile(j)
    P.wait_all("sp", ["yTa0", "yTa1", "yTb0", "yTb1"])


def build_mix(T, **kw):
    nc = bass.Bass("TRN2", target_bir_lowering=False)
    hT = nc.dram_tensor("hT", [1024, T], BF16, kind="ExternalInput").ap()
    wcore = nc.dram_tensor("wcore", [1024, NCOL], F32, kind="ExternalInput").ap()
    vecs = nc.dram_tensor("vecs", [128, 11], F32, kind="ExternalInput").ap()
    up = nc.dram_tensor("up", [128, 128], F32, kind="ExternalInput").ap()
    fbf = nc.dram_tensor("fbf", [2, 1], F32, kind="ExternalInput").ap()
    consts = nc.dram_tensor("consts", [128, NCONST], F32, kind="ExternalInput").ap()
    yT = nc.dram_tensor("yT", [256, T], BF16, kind="ExternalOutput").ap()
    P = Prog(nc)
    emit_mix(P, nc, T, hT, wcore, vecs, up, fbf, consts, yT, **kw)
    print("mix ops", P.n_ops, "waits", P.n_waits)
    P.close()
    return nc

import numpy as np

LN_EPS = 1e-5
ALPHA = float(4 ** 0.25)
K_ID, K_ONE = 0, 1152


def emit_tok(P, nc, NTOK, consts, c_in, *, x_in=None, embg=None, embb=None,
             yT=None, xprev=None, wout=None, wada_g=None, bada_g=None, lng=None, lnb=None,
             wada_f=None, bada_f=None, x_out=None, hT_out=None, pfx="t",
             y_src=None, y_keys=(), h_dst=None, after_h=None):
    do_back = (yT is not None) or (y_src is not None)
    do_front = (hT_out is not None) or (h_dst is not None)
    if do_back and y_src is None:
        y_src = lambda s: yT.rearrange("(c p) t -> p c t", p=128)[:, :, s * 512:(s + 1) * 512]
    if do_front and h_dst is None:
        h_dst = lambda s: hT_out.rearrange("(c p) t -> p c t", p=128)[:, :, s * 512:(s + 1) * 512]
    NS = NTOK // 512
    V, A, G, PE = nc.vector, nc.scalar, nc.gpsimd, nc.tensor
    k = lambda s: pfx + s

    cst = P.sb(k("cst"), [128, 1280], F32)
    P.dma("sp", cst[:], consts[:, :], k("ld_cst"), writes=[k("cst")])
    ident = cst[:, K_ID:K_ID + 128]
    ones = cst[:, K_ONE:K_ONE + 128]
    c_sb = P.sb(k("c_sb"), [128, 8], F32)
    P.dma("sp", c_sb[:], c_in.rearrange("(c p) -> p c", p=128), k("ld_c"), writes=[k("c_sb")],
          allow_slow_non_contiguous=True)
    cbc = P.sb(k("cbc"), [128, 8, 128], F32)
    for c in range(8):
        P.op("dve", lambda: V.tensor_scalar(out=cbc[:, c, :], in0=ones, scalar1=c_sb[:, c:c + 1], scalar2=None,
                                            op0=ALU.mult), reads=[k("cst"), k("c_sb")], writes=[k("cbc")])
    pp = [P.ps(k(f"pp{i}"), [128, 512]) for i in range(4)]
    ppc = [0]

    def nextpp():
        i = ppc[0] % 4
        ppc[0] += 1
        return pp[i], k(f"pp{i}")

    wad = [P.sb(k(f"wad{i}"), [128, 8, 512], F32) for i in range(2)]
    wadc = [0]

    def mod_bc(wada, bada, N, name):
        mt = P.sb(k(name), [128, N], F32)
        P.dma("sp", mt[:], bada.partition_broadcast(128), k("ld_" + name), writes=[k(name)])
        wv = wada.rearrange("(c p) n -> p c n", p=128)
        for n0 in range(0, N, 512):
            b = wadc[0] % 2
            wadc[0] += 1
            P.dma("sp", wad[b][:], wv[:, :, n0:n0 + 512], k(f"ld_wad{b}"), writes=[k(f"wad{b}")])
            ps, pk = nextpp()
            for c in range(8):
                P.op("pe", lambda: PE.matmul(ps[:, :], lhsT=cbc[:, c, :], rhs=wad[b][:, c, :], start=(c == 0),
                                             stop=(c == 7)), reads=[k("cbc"), k(f"wad{b}")], writes=[pk])
            P.op("dve", lambda: V.tensor_tensor(out=mt[:, n0:n0 + 512], in0=ps[:, :], in1=mt[:, n0:n0 + 512],
                                                op=ALU.add), reads=[pk, k(name)], writes=[k(name)])
        return mt

    def bc_load(vec_ap, name):
        t = P.sb(k(name), [128, 1024], F32)
        P.dma("sp", t[:], vec_ap.partition_broadcast(128), k("ld_" + name), writes=[k(name)])
        return t

    if do_back:
        mg = mod_bc(wada_g, bada_g, 1024, "mg")
        P.op("dve", lambda: V.tensor_scalar(out=mg[:], in0=mg[:], scalar1=1.0, scalar2=None, op0=ALU.add),
             reads=[k("mg")], writes=[k("mg")])
        wo = P.sb(k("wo"), [128, 8, 1024], BF16)
        wov = wout.rearrange("(c p) n -> p c n", p=128)
        for pi, n0 in enumerate(range(0, 1024, 512)):
            b = wadc[0] % 2
            wadc[0] += 1
            P.dma("sp", wad[b][:], wov[:, :, n0:n0 + 512], k(f"ld_wad{b}"), writes=[k(f"wad{b}")])
            for c in range(8):
                e = "dve" if c % 2 == 0 else "pool"
                eng = V if c % 2 == 0 else G
                P.op(e, lambda: eng.tensor_tensor(out=wo[:, c, n0:n0 + 512], in0=wad[b][:, c, :],
                                                  in1=mg[:, n0:n0 + 512], op=ALU.mult),
                     reads=[k(f"wad{b}"), k("mg")], writes=[k("wo")])
        g_bc = bc_load(lng, "lng")
        b_bc = bc_load(lnb, "lnb")
    else:
        g_bc = bc_load(embg, "embg")
        b_bc = bc_load(embb, "embb")
    if do_front:
        mf = mod_bc(wada_f, bada_f, 2048, "mf")
        P.op("dve", lambda: V.tensor_scalar(out=mf[:, 1024:2048], in0=mf[:, 1024:2048], scalar1=1.0, scalar2=None,
                                            op0=ALU.add), reads=[k("mf")], writes=[k("mf")])
        fm = P.sb(k("fm"), [128, 16], F32)
        for q in range(4):
            ps, pk = nextpp()
            for u in range(4):
                cc = q * 4 + u
                P.op("pe", lambda: PE.transpose(ps[:, u * 128:(u + 1) * 128], mf[:, cc * 128:(cc + 1) * 128], ident),
                     reads=[k("mf"), k("cst")], writes=[pk])
            P.op("dve", lambda: V.tensor_copy(out=fm[:, q * 4:q * 4 + 4],
                                              in_=ps[:, :].rearrange("p (c t) -> p c t", t=128)[:, :, 0]),
                 reads=[pk], writes=[k("fm")])

    xt = [P.sb(k(f"xt{i}"), [128, 4, 1024], F32) for i in range(2)]
    yt = [P.sb(k(f"yt{i}"), [128, 8, 512], BF16) for i in range(2)] if do_back else None
    ht = [P.sb(k(f"ht{i}"), [128, 8, 512], BF16) for i in range(2)] if do_front else None
    xsrc = xprev if do_back else x_in

    def load(s):
        b = s % 2
        P.dma("sp", xt[b][:], xsrc[s * 512:(s + 1) * 512, :].rearrange("(u p) d -> p u d", p=128),
              k(f"ld_x{b}"), writes=[k(f"xt{b}")] + [k(f"xt{b}_{u}") for u in range(4)])
        if do_back:
            P.dma("sp", yt[b][:], y_src(s), k(f"ld_y{b}"), reads=y_keys, writes=[k(f"yt{b}")])

    stats4 = [P.sb(k(f"stats4_{i}"), [128, 4, 2, 6], F32) for i in range(2)]
    mv4 = [P.sb(k(f"mv4_{i}"), [128, 4, 2], F32) for i in range(2)]
    rs4 = [P.sb(k(f"rs4_{i}"), [128, 4], F32) for i in range(2)]
    nb4 = [P.sb(k(f"nb4_{i}"), [128, 4], F32) for i in range(2)]
    load(0)
    for s in range(NS):
        b = s % 2
        if s + 1 < NS:
            load(s + 1)
        xk = k(f"xt{b}")
        xku = [k(f"xt{b}_{u}") for u in range(4)]
        for u in range(4):
            xs = xt[b][:, u, :]
            if do_back:
                for n in range(2):
                    ps, pk = nextpp()
                    for c in range(8):
                        P.op("pe", lambda: PE.matmul(ps[:, :], lhsT=yt[b][:, c, u * 128:(u + 1) * 128],
                                                     rhs=wo[:, c, n * 512:(n + 1) * 512], start=(c == 0),
                                                     stop=(c == 7)), reads=[k(f"yt{b}"), k("wo")], writes=[pk])
                    P.op("dve", lambda: V.scalar_tensor_tensor(out=xs[:, n * 512:(n + 1) * 512],
                                                               in0=xs[:, n * 512:(n + 1) * 512], scalar=ALPHA,
                                                               in1=ps[:, :], op0=ALU.mult, op1=ALU.add),
                         reads=[xk, xku[u], pk], writes=[xku[u]])
            for n in range(2):
                P.op("dve", lambda: V.bn_stats(out=stats4[b][:, u, n, :], in_=xs[:, n * 512:(n + 1) * 512]),
                     reads=[xk, xku[u]], writes=[k(f"st4_{b}_{u}")])
            P.op("dve", lambda: V.bn_aggr(out=mv4[b][:, u, :], in_=stats4[b][:, u, :, :].rearrange("p a b -> p (a b)")),
                 reads=[k(f"st4_{b}_{u}")], writes=[k(f"mv4_{b}")])
        P.op("dve", lambda: V.tensor_scalar(out=rs4[b][:], in0=mv4[b][:, :, 1], scalar1=LN_EPS, scalar2=None,
                                            op0=ALU.add), reads=[k(f"mv4_{b}")], writes=[k(f"rs4_{b}")])
        P.op("act", lambda: A.activation(out=rs4[b][:], in_=rs4[b][:], func=AF.Sqrt), reads=[k(f"rs4_{b}")],
             writes=[k(f"rs4_{b}")])
        P.op("dve", lambda: V.reciprocal(out=rs4[b][:], in_=rs4[b][:]), reads=[k(f"rs4_{b}")], writes=[k(f"rs4_{b}")])
        P.op("dve", lambda: V.scalar_tensor_tensor(out=nb4[b][:], in0=mv4[b][:, :, 0], scalar=-1.0, in1=rs4[b][:],
                                                   op0=ALU.mult, op1=ALU.mult),
             reads=[k(f"mv4_{b}"), k(f"rs4_{b}")], writes=[k(f"nb4_{b}")])
        for u in range(4):
            xs = xt[b][:, u, :]
            P.op("act", lambda: A.activation(out=xs, in_=xs, func=AF.Identity, scale=rs4[b][:, u:u + 1],
                                             bias=nb4[b][:, u:u + 1]),
                 reads=[xk, xku[u], k(f"rs4_{b}"), k(f"nb4_{b}")], writes=[xku[u]])
            P.op("dve", lambda: V.tensor_tensor(out=xs, in0=xs, in1=g_bc[:], op=ALU.mult),
                 reads=[xku[u], k("lng"), k("embg")], writes=[xku[u]])
            P.op("pool", lambda: G.tensor_tensor(out=xs, in0=xs, in1=b_bc[:], op=ALU.add),
                 reads=[xku[u], k("lnb"), k("embb")], writes=[xku[u]])
        if x_out is not None:
            P.dma("sp", x_out[s * 512:(s + 1) * 512, :].rearrange("(u p) d -> p u d", p=128), xt[b][:],
                  k(f"st_x{b}"), reads=[xk] + xku, writes=[k(f"xo{b}")])
        if do_front:
            hk = k(f"ht{b}")
            for c in range(8):
                ps, pk = nextpp()
                for u in range(4):
                    P.op("pe", lambda: PE.transpose(ps[:, u * 128:(u + 1) * 128], xt[b][:, u, c * 128:(c + 1) * 128],
                                                    ident), reads=[xk, xku[u], k("cst")], writes=[pk])
                P.op("act", lambda: A.activation(out=ht[b][:, c, :], in_=ps[:, :], func=AF.Identity,
                                                 scale=fm[:, 8 + c:9 + c], bias=fm[:, c:c + 1]),
                     reads=[pk, k("fm")], writes=[hk])
            P.dma("sp", h_dst(s), ht[b][:], k(f"st_h{b}"), reads=[hk], writes=[k(f"ho{b}"), f"htile{s}"])
            if after_h is not None:
                after_h(s)
    P.wait_all("sp", [k("xo0"), k("xo1"), k("ho0"), k("ho1")])


def build_tok(NTOK, mode):
    nc = bass.Bass("TRN2", target_bir_lowering=False)
    dt = lambda n, s, d=F32, kind="ExternalInput": nc.dram_tensor(n, s, d, kind=kind).ap()
    consts = dt("consts", [128, 1280])
    c_in = dt("c", [1024])
    kw = {}
    if mode == "pre":
        kw.update(x_in=dt("x", [NTOK, 1024]), embg=dt("embg", [1024]), embb=dt("embb", [1024]))
    else:
        kw.update(yT=dt("yT", [1024, NTOK], BF16), xprev=dt("xprev", [NTOK, 1024]), wout=dt("wout", [1024, 1024]),
                  wada_g=dt("wada_g", [1024, 1024]), bada_g=dt("bada_g", [1024]), lng=dt("lng", [1024]),
                  lnb=dt("lnb", [1024]))
    if mode != "post":
        kw.update(wada_f=dt("wada_f", [1024, 2048]), bada_f=dt("bada_f", [2048]),
                  hT_out=dt("hT", [1024, NTOK], BF16, kind="ExternalOutput"))
    kw.update(x_out=dt("xo", [NTOK, 1024], kind="ExternalOutput"))
    P = Prog(nc)
    emit_tok(P, nc, NTOK, consts, c_in, **kw)
    print("tok", mode, "ops", P.n_ops, "waits", P.n_waits)
    P.close()
    return nc

import numpy as np

D = 1024
RWW = 512
RW_R0, RW_K0, RW_V0, RW_WD0, RW_AD0, RW_END = 0, 512, 1024, 1536, 1600, 1664
FX_Q0, FX_K0, FX_V0, FX_F0, FX_END = 1664, 2176, 2688, 3200, 3208
GATE0 = 3208


def core_cols(g):
    ch = np.arange(128 * g, 128 * g + 128)
    l64 = np.arange(64)
    return np.concatenate([
        RW_R0 + ch, RW_K0 + ch, RW_V0 + ch, RW_WD0 + l64, RW_AD0 + l64, GATE0 + ch,
        FX_Q0 + ch, FX_K0 + ch,
        GATE0 + 512 + 128 * g + l64, GATE0 + 512 + 128 * g + 64 + l64,
        FX_V0 + ch, FX_F0 + np.array([2 * g, 2 * g + 1])])


def pack_core(inp, l, g):
    ch = np.arange(128 * g, 128 * g + 128)
    l64 = np.arange(64)
    wcore = np.ascontiguousarray(inp["w_in"][l][:, core_cols(g)])
    mix = inp["rwkv_mix"][l]
    vecs = np.stack([
        mix[RW_R0 + ch], mix[RW_K0 + ch], mix[RW_V0 + ch],
        np.concatenate([mix[RW_WD0 + l64], mix[RW_AD0 + l64]]),
        inp["w0"][l][ch], inp["a0"][l][ch], inp["k_k"][l][ch], inp["k_a"][l][ch],
        inp["r_k"][l][ch], inp["gn_g"][l][ch], inp["gn_b"][l][ch]], axis=1).astype(np.float32)
    up = np.concatenate([inp["w_up"][l][:, ch], inp["a_up"][l][:, ch]], axis=0).astype(np.float32)
    fbf = inp["fox_bf"][l][[2 * g, 2 * g + 1]][:, None].astype(np.float32)
    return dict(wcore=wcore, vecs=np.ascontiguousarray(vecs), up=np.ascontiguousarray(up),
                fbf=np.ascontiguousarray(fbf))


T_SEQ = 16384
NTOK = 4096
RG = [[0, 1, 2, 3], [4, 5, 6, 7]]
_NC_CACHE = {}


def build_fused(T=T_SEQ, NT=NTOK):
    nc = bass.Bass("TRN2", target_bir_lowering=False)
    dt = lambda n, s, d=F32, kind="ExternalInput": nc.dram_tensor(n, s, d, kind=kind).ap()
    NS = NT // 512
    NK = T // 2048
    consts = dt("consts", [128, 1280])
    c_in = dt("c", [1024])
    x_in = dt("x", [NT, 1024])
    embg = dt("embg", [1024])
    embb = dt("embb", [1024])
    L = []
    for l in range(2):
        L.append(dict(
            wada_f=dt(f"wada_f{l}", [1024, 2048]), bada_f=dt(f"bada_f{l}", [2048]),
            wada_g=dt(f"wada_g{l}", [1024, 1024]), bada_g=dt(f"bada_g{l}", [1024]),
            wout=dt(f"wout{l}", [1024, 1024]), lng=dt(f"lng{l}", [1024]), lnb=dt(f"lnb{l}", [1024]),
            wcore=dt(f"wcore{l}", [1024, NCOL]), vecs=dt(f"vecs{l}", [128, 11]), up=dt(f"up{l}", [128, 128]),
            fbf=dt(f"fbf{l}", [2, 1])))
    xo = dt("xo", [NT, 1024], kind="ExternalOutput")
    xd = [nc.dram_tensor(f"xd{l}", [NT, 1024], F32).ap() for l in range(2)]
    hloc = [nc.dram_tensor(f"hloc{l}", [NS, 1024, 512], BF16).ap() for l in range(2)]
    hg = [nc.dram_tensor(f"hg{l}", [NS, 4, 1024, 512], BF16).ap() for l in range(2)]
    yc = [nc.dram_tensor(f"yc{l}", [NK, 256, 2048], BF16).ap() for l in range(2)]
    yg = [nc.dram_tensor(f"yg{l}", [NK, 4, 256, 2048], BF16).ap() for l in range(2)]
    q = nc.partition_id() % 4
    P = Prog(nc)

    def front_hooks(l):
        h_dst = lambda s: hloc[l][s].rearrange("(c p) t -> p c t", p=128)
        after_h = lambda s: P.cc("AllGather", hloc[l][s].opt(), hg[l][s].opt(), RG,
                                 reads=[f"htile{s}"], writes=[f"hg{s}"])
        return dict(h_dst=h_dst, after_h=after_h, wada_f=L[l]["wada_f"], bada_f=L[l]["bada_f"])

    P.begin_scope()
    emit_tok(P, nc, NT, consts, c_in, x_in=x_in, embg=embg, embb=embb, x_out=xd[0], pfx="a", **front_hooks(0))
    P.end_scope()
    for l in range(2):
        P.begin_scope()

        def after_tile(j, l=l):
            if j % 4 == 3:
                kk = j // 4
                keys = []
                for jj in range(j - 3, j + 1):
                    keys += [f"ytile{jj}_a", f"ytile{jj}_b0", f"ytile{jj}_b1"]
                P.cc("AllGather", yc[l][kk].opt(), yg[l][kk].opt(), RG, reads=keys, writes=[f"yg{kk}"])

        emit_mix(P, nc, T, None, L[l]["wcore"], L[l]["vecs"], L[l]["up"], L[l]["fbf"], consts, None,
                 h_src=lambda j, l=l: hg[l][j % NS, j // NS].rearrange("(c p) t -> p c t", p=128),
                 y_dst=lambda r0, r1, j, l=l: yc[l][j // 4, r0:r1, (j % 4) * 512:(j % 4 + 1) * 512],
                 after_tile=after_tile, h_keys=lambda j: [f"hg{j % NS}"])
        P.end_scope()
        P.begin_scope()

        def y_src(s, l=l):
            v = yg[l][bass.ds(2 * q + s // 4, 1)]
            return v.rearrange("o r (h p) t -> p (o r h) t", p=128)[:, :, (s % 4) * 512:(s % 4 + 1) * 512]

        kw = dict(y_src=y_src, y_keys=[f"yg{k_}" for k_ in range(NK)], xprev=xd[l], wout=L[l]["wout"], wada_g=L[l]["wada_g"], bada_g=L[l]["bada_g"],
                  lng=L[l]["lng"], lnb=L[l]["lnb"], pfx=f"b{l}")
        if l == 0:
            kw.update(front_hooks(1))
            kw.update(x_out=xd[1])
        else:
            kw.update(x_out=xo)
        emit_tok(P, nc, NT, consts, c_in, **kw)
        P.end_scope()
    print("fused ops", P.n_ops, "waits", P.n_waits)
    P.close()
    return nc


def wout_perm():
    idx = []
    for r in range(4):
        idx += list(range(128 * r, 128 * r + 128))
        idx += list(range(512 + 128 * r, 512 + 128 * r + 128))
    return np.array(idx)


def kernel(**inp):
    inp = {k: np.ascontiguousarray(np.asarray(v)) for k, v in inp.items()}
    consts = make_consts()
    ca = np.ascontiguousarray
    if "nc" not in _NC_CACHE:
        _NC_CACHE["nc"] = build_fused()
    nc = _NC_CACHE["nc"]
    perm = wout_perm()
    maps = []
    for core in range(8):
        b, g = core // 4, core % 4
        m = dict(consts=consts, c=ca(inp["c"][b]), x=ca(inp["x"][b][g * NTOK:(g + 1) * NTOK]),
                 embg=inp["emb_ln_g"], embb=inp["emb_ln_b"])
        for l in range(2):
            pk = pack_core(inp, l, g)
            m.update({f"wada_f{l}": ca(inp["w_ada"][l][:, 0:2048]), f"bada_f{l}": ca(inp["b_ada"][l][0:2048]),
                      f"wada_g{l}": ca(inp["w_ada"][l][:, 2048:3072]), f"bada_g{l}": ca(inp["b_ada"][l][2048:3072]),
                      f"wout{l}": ca(inp["w_out"][l][perm]), f"lng{l}": inp["ln_g"][l], f"lnb{l}": inp["ln_b"][l],
                      f"wcore{l}": pk["wcore"], f"vecs{l}": pk["vecs"], f"up{l}": pk["up"], f"fbf{l}": pk["fbf"]})
        maps.append(m)
    res = run_bass_kernel_spmd(nc, maps, core_ids=list(range(8))).results
    out = np.stack([np.concatenate([res[b * 4 + g]["xo"] for g in range(4)], axis=0) for b in range(2)], axis=0)
    return np.asarray(out, dtype=np.float32)
```

```python
import contextlib
import numpy as np
import concourse.bass as bass
import concourse.mybir as mybir
from concourse.bass_utils import run_bass_kernel_spmd

F32 = mybir.dt.float32
BF16 = mybir.dt.bfloat16
AF = mybir.ActivationFunctionType
ALU = mybir.AluOpType
AX = mybir.AxisListType

SEM_EPOCH = 20000


class Prog:
    def __init__(self, nc):
        self.nc = nc
        self.es = contextlib.ExitStack()
        self.eng = {"pe": nc.tensor, "act": nc.scalar, "dve": nc.vector,
                    "pool": nc.gpsimd, "sp": nc.sync}
        self.cnt = {e: 0 for e in self.eng}
        self.epoch = {e: 0 for e in self.eng}
        self.esem = {}
        for e in self.eng:
            self.esem[e] = self._newsem(f"s_{e}_0")
        self.seen = {e: {} for e in self.eng}
        self.sems = {}
        self.dcnt = {}
        self.lastw = {}
        self.reads = {}
        self.n_ops = 0
        self.n_waits = 0
        self.tes = self.es
        self.scope_id = 0
        self.ccsem = self._newsem("s_cc")
        self.ccn = 0
        self.kalias = {}

    def _newsem(self, name):
        return self.es.enter_context(self.nc.semaphore(name))

    def sb(self, name, shape, dt):
        return self.tes.enter_context(self.nc.sbuf_tensor(f"s{self.scope_id}_{name}", list(shape), dt))

    def ps(self, name, shape, dt=F32):
        return self.tes.enter_context(self.nc.psum_tensor(f"s{self.scope_id}_{name}", list(shape), dt))

    def begin_scope(self):
        self.scope_id += 1
        self.tes = contextlib.ExitStack()

    def end_scope(self):
        self.barrier()
        self.tes.close()
        self.tes = self.es

    def barrier(self):
        toks = []
        for f in self.eng:
            if self.cnt[f] > 0:
                toks.append((self.esem[f], self.cnt[f], f))
        for name, sem in self.sems.items():
            if self.dcnt[name] > 0:
                toks.append((sem, self.dcnt[name], "dma"))
        for e in self.eng:
            need = {id(sm): (sm, v) for (sm, v, o) in toks if o != e}
            self._emit_waits(e, need)
        keep = {k: t for k, t in self.lastw.items() if t[2] == "cc"}
        self.lastw.clear()
        self.reads.clear()
        self.lastw.update(keep)

    def cc(self, kind, in_ap, out_ap, rg, reads=(), writes=()):
        need = self._need("pool", reads, writes)
        self._emit_waits("pool", need)
        ins = self.nc.gpsimd.collective_compute(kind, ALU.bypass, replica_groups=rg, ins=[in_ap], outs=[out_ap])
        self.ccn += 1
        ins.then_inc(self.ccsem, 1)
        tok = (self.ccsem, self.ccn, "cc")
        self._commit("cc", tok, reads, writes)
        self.n_ops += 1
        return tok

    def close(self):
        if self.ccn > 0:
            self._emit_waits("pool", {id(self.ccsem): (self.ccsem, self.ccn)})
        self.es.close()

    def _need(self, e, reads, writes):
        need = {}

        def add(tok, kind):
            if tok is None:
                return
            sem, val, owner = tok
            if owner == e:
                if e in ("pe", "sp"):
                    return
                if kind == "war":
                    return
            k = id(sem)
            if k not in need or need[k][1] < val:
                need[k] = (sem, val)

        for r in reads:
            add(self.lastw.get(r), "raw")
        for w in writes:
            add(self.lastw.get(w), "waw")
            for tok in self.reads.get(w, {}).values():
                add(tok, "war")
        return need

    def _emit_waits(self, e, need):
        eng = self.eng[e]
        seen = self.seen[e]
        for k, (sem, val) in need.items():
            if seen.get(k, 0) >= val:
                continue
            eng.wait_ge(sem, val)
            seen[k] = val
            self.n_waits += 1

    def _commit(self, e, tok, reads, writes):
        for w in writes:
            self.lastw[w] = tok
            self.reads[w] = {}
        for r in reads:
            self.reads.setdefault(r, {})[(e, id(tok[0]))] = tok

    @staticmethod
    def _is_psum(k):
        return isinstance(k, str) and (k.startswith("ps") or "pp" in k)

    def alias(self, a, b):
        self.kalias[a] = b

    def _ka(self, keys):
        if not self.kalias:
            return keys
        return [self.kalias.get(k, k) for k in keys]

    def op(self, e, fn, reads=(), writes=()):
        reads, writes = self._ka(reads), self._ka(writes)
        px = [k for k in reads if self._is_psum(k)]
        if px:
            reads = [k for k in reads if not self._is_psum(k)]
            writes = list(writes) + px
        need = self._need(e, reads, writes)
        self._emit_waits(e, need)
        if self.cnt[e] >= SEM_EPOCH:
            self.epoch[e] += 1
            self.esem[e] = self._newsem(f"s_{e}_{self.epoch[e]}")
            self.cnt[e] = 0
        ins = fn()
        self.cnt[e] += 1
        ins.then_inc(self.esem[e], 1)
        tok = (self.esem[e], self.cnt[e], e)
        self._commit(e, tok, reads, writes)
        self.n_ops += 1
        return tok

    def dma(self, q, out, in_, dsem, reads=(), writes=(), **kw):
        reads, writes = self._ka(reads), self._ka(writes)
        need = self._need(q, reads, writes)
        self._emit_waits(q, need)
        if dsem not in self.sems:
            self.sems[dsem] = self._newsem("d_" + dsem)
            self.dcnt[dsem] = 0
        sem = self.sems[dsem]
        ins = self.eng[q].dma_start(out=out, in_=in_, **kw)
        self.dcnt[dsem] += 16
        ins.then_inc(sem, 16)
        tok = (sem, self.dcnt[dsem], "dma")
        self._commit("dma", tok, reads, writes)
        self.n_ops += 1
        return tok

    def wait_all(self, e, keys):
        need = {}
        for k in keys:
            tok = self.lastw.get(k)
            if tok is None:
                continue
            kk = id(tok[0])
            if kk not in need or need[kk][1] < tok[1]:
                need[kk] = (tok[0], tok[1])
        self._emit_waits(e, need)

import numpy as np

NCOL = 1154
C_R, C_K, C_V, C_WA, C_GA, C_FQ, C_FK = 0, 128, 256, 384, 512, 640, 768
C_GB0, C_GB1, C_FV, C_F = 896, 960, 1024, 1152
K_ID, K_BO, K_SU, K_UI, K_SL, K_RS, K_ONE = 0, 128, 256, 384, 512, 640, 1152
NCONST = 1280
GN_EPS = 64e-5
RW_DT = BF16
RW_ROUNDS = 36.0
OVERLAP_PROJ = False
NEU_BANKS = 3
FIN_OVERLAP = False
DRAIN_OVERLAP = False
DRAIN_MIN = 100


def make_consts():
    c = np.zeros((128, NCONST), np.float32)
    i = np.arange(128)
    c[:, K_ID:K_ID + 128] = np.eye(128)
    c[:, K_BO:K_BO + 128] = (i[:, None] // 64 == i[None, :] // 64)
    c[:, K_SU:K_SU + 128] = (i[:, None] < i[None, :])
    c[:, K_UI:K_UI + 128] = (i[:, None] <= i[None, :])
    c[:, K_SL:K_SL + 128] = (i[:, None] > i[None, :])
    rs = np.ones(512, np.float32)
    rs[::128] = 0
    c[:, K_RS:K_RS + 512] = rs[None, :]
    c[:, K_ONE:K_ONE + 128] = 1.0
    return c


def interleave(gens, weights):
    gens = list(gens)
    acc = [0.0] * len(gens)
    alive = [True] * len(gens)
    while any(alive):
        for i, g in enumerate(gens):
            if not alive[i]:
                continue
            acc[i] += weights[i]
            while acc[i] >= 1.0 and alive[i]:
                acc[i] -= 1.0
                try:
                    next(g)
                except StopIteration:
                    alive[i] = False


def emit_mix(P, nc, T, hT, wcore, vecs, up, fbf, consts, yT, do_rwkv=True, do_fox=True,
             h_src=None, y_dst=None, after_tile=None, h_keys=None):
    NT = T // 512
    NB = T // 128
    V = nc.vector
    A = nc.scalar
    G = nc.gpsimd
    PE = nc.tensor

    cst = P.sb("cst", [128, NCONST], F32)
    vec = P.sb("vec", [128, 20], F32)
    fbh = P.sb("fbh", [2, 1], F32)
    upt = P.sb("upt", [128, 128], F32)
    fbt = P.sb("fbt", [2, 1], F32)
    P.dma("sp", cst[:], consts[:, :], "ld_cst", writes=["cst"])
    P.dma("sp", vec[:, 0:11], vecs[:, :], "ld_vec", writes=["vec"])
    P.dma("sp", upt[:], up[:, :], "ld_upt", writes=["upt"])
    P.dma("sp", fbt[:], fbf[:, :], "ld_fbt", writes=["fbt"])
    ident = cst[:, K_ID:K_ID + 128]
    bones = cst[:, K_BO:K_BO + 128]
    mask2 = cst[:, K_SU:K_SU + 256]
    msl = cst[:, K_SL:K_SL + 128]
    rsm = cst[:, K_RS:K_RS + 512]
    ones = cst[:, K_ONE:K_ONE + 128]
    P.op("dve", lambda: V.tensor_scalar(out=vec[:, 11:15], in0=vec[:, 0:4], scalar1=-1.0, scalar2=1.0,
                                        op0=ALU.mult, op1=ALU.add), reads=["vec"], writes=["vec"])
    P.op("dve", lambda: V.tensor_scalar(out=vec[:, 15:16], in0=vec[:, 7:8], scalar1=-1.0, scalar2=1.0,
                                        op0=ALU.mult, op1=ALU.add), reads=["vec"], writes=["vec"])
    P.op("dve", lambda: V.tensor_scalar(out=vec[:, 16:18], in0=vec[:, 4:6], scalar1=0.5, scalar2=None,
                                        op0=ALU.mult), reads=["vec"], writes=["vec"])
    P.op("dve", lambda: V.tensor_scalar(out=fbh[:], in0=fbt[:], scalar1=0.5, scalar2=None, op0=ALU.mult),
         reads=["fbt"], writes=["fbh"])
    uib = P.sb("uib", [128, 128], BF16)
    idb = P.sb("idb", [128, 128], BF16)
    P.op("dve", lambda: V.tensor_copy(out=idb[:], in_=cst[:, K_ID:K_ID + 128]), reads=["cst"], writes=["idb"])
    P.op("dve", lambda: V.tensor_copy(out=uib[:], in_=cst[:, K_UI:K_UI + 128]), reads=["cst"], writes=["uib"])

    wsb = P.sb("wsb", [128, 8, NCOL], BF16)
    wst = [P.sb(f"wst{i}", [128, 8, 64], F32) for i in range(2)]
    wv = wcore.rearrange("(c p) n -> p c n", p=128)
    pieces = [(s, min(64, NCOL - s)) for s in range(0, NCOL, 64)]
    for pi, (s, n) in enumerate(pieces):
        b = pi % 2
        P.dma("sp", wst[b][:, :, 0:n], wv[:, :, s:s + n], f"ld_w{b}", writes=[f"wst{b}"])
        eng = "act" if pi % 2 == 0 else "dve"
        if eng == "act":
            P.op("act", lambda: A.copy(out=wsb[:, :, s:s + n], in_=wst[b][:, :, 0:n]),
                 reads=[f"wst{b}"], writes=["wsb"])
        else:
            P.op("dve", lambda: V.tensor_copy(out=wsb[:, :, s:s + n], in_=wst[b][:, :, 0:n]),
                 reads=[f"wst{b}"], writes=["wsb"])

    hTt = [P.sb(f"hT{i}", [128, 8, 512], BF16) for i in range(2)]
    if h_src is None:
        hv = hT.rearrange("(c p) t -> p c t", p=128)
        h_src = lambda j: hv[:, :, j * 512:(j + 1) * 512]
    if y_dst is None:
        y_dst = lambda r0, r1, j: yT[r0:r1, j * 512:(j + 1) * 512]
    NSB = 3 if OVERLAP_PROJ else (6 - NEU_BANKS)
    NPT = 4
    LOOK = 2 if OVERLAP_PROJ else (5 - NEU_BANKS)
    psS = [P.ps(f"psS{i}", [128, 512]) for i in range(NSB)]
    psO = P.ps("psO", [128, 512])
    psA = [P.ps(f"psA{i}", [128, 512]) for i in range(NEU_BANKS)]
    ppb = P.ps("ppb", [128, 512]) if OVERLAP_PROJ else None
    psY = P.ps("psY", [128, 512])
    ppc = [0]

    def nextpp():
        if OVERLAP_PROJ:
            return ppb, "ppb"
        i = ppc[0] % 2
        ppc[0] += 1
        return psA[i], f"psA{i}"

    KT = P.sb("KT", [128, T], BF16)
    Vaug = P.sb("Vaug", [128, NB, 2, 65], BF16)
    ckres = P.sb("ckres", [128, NB, 2], F32)
    P.op("pool", lambda: G.memset(Vaug[:, :, :, 64:65], 1.0), writes=["Vaug_ones"])
    QT = [[P.sb(f"QT{i}_{h}", [128, 512], BF16) for h in range(2)] for i in range(2)]
    for i in range(2):
        for h in range(2):
            P.op("pool", lambda: G.memset(QT[i][h][:], 0.0), writes=[f"QT{i}"])
    sgb = [[P.sb(f"sgb{i}_{h}", [64, 512], F32) for h in range(2)] for i in range(2)]
    rrow = P.sb("rrow", [128, 512], F32)
    lft = P.sb("lft", [2, 512], F32)
    lf = lft[:, :]
    onesrow = rrow[0:2, :]
    cum = [P.sb(f"cum{i}", [2, 512], F32) for i in range(2)]
    P.op("dve", lambda: V.memset(cum[1][:], 0.0), writes=["cum1"])
    i2 = P.sb("i2", [2, 2], F32)
    basebc = [P.sb(f"basebc{i}", [128, 2], F32) for i in range(2)]
    biasj = [P.sb(f"biasj{i}", [128, NB, 2], F32) for i in range(2)]
    PT = [P.sb(f"PT{i}", [128, 512], BF16) for i in range(NPT)]
    t1 = P.sb("t1", [64, 512], F32)
    osb = P.sb("osb", [65, 512], F32)
    ybo = [P.sb(f"ybo{i}", [64, 512], BF16) for i in range(2)]

    raw = {g: P.sb(f"raw_{g}", [128, 513], F32) for g in "rkvw"}
    for g in "rkvw":
        P.op("pool", lambda: G.memset(raw[g][:, 0:1], 0.0), writes=[f"raw_{g}c"])
    tmp = P.sb("tmp", [128, 512], F32)
    sh = {g: P.sb(f"sh_{g}", [128, 512], F32) for g in "rkvw"}
    a_t = P.sb("a_t", [128, 512], F32)
    ld = P.sb("ld", [128, 512], F32)
    cl = P.sb("cl", [128, 512], F32)
    einc = P.sb("einc", [128, 512], F32)
    eneg = P.sb("eneg", [128, 512], F32)
    kk = P.sb("kk", [128, 512], F32)
    kmod = tmp
    P.alias("kmod", "tmp")
    w1 = P.sb("w1", [128, 512], F32)
    w2 = P.sb("w2", [128, 512], F32)
    KRz = [P.sb(f"KRz{i}", [128, 4, 2, 128], RW_DT) for i in range(2)]
    for i in range(2):
        P.op("pool", lambda: G.memset(KRz[i][:], 0.0), writes=["KR"])
    Bt = P.sb("Bt", [128, 512], RW_DT)
    Kt = P.sb("Kt", [128, 512], RW_DT)
    bonus = ld
    P.alias("bonus", "ld")
    sgaL = [P.sb(f"sga{i}", [128, 512], F32) for i in range(2)]
    Vtok = P.sb("Vtok", [128, 4, 128], RW_DT)
    Btok = P.sb("Btok", [128, 4, 128], RW_DT)
    Ktok = P.sb("Ktok", [128, 4, 128], RW_DT)
    ST = P.sb("ST", [128, 64], F32)
    STw = P.sb("STw", [128, 64], F32)
    STb = P.sb("STb", [128, 128], RW_DT)
    P.op("dve", lambda: V.memset(ST[:], 0.0), writes=["ST0", "ST1"])
    P.op("dve", lambda: V.memset(STb[:], 0.0), writes=["STb0", "STb1"])
    AT = [[P.sb(f"AT{i}_{k}", [128, 256], RW_DT) for k in range(4)] for i in range(2)]
    AK = [[P.sb(f"AK{i}_{k}", [128, 256], RW_DT) for k in range(4)] for i in range(2)]
    XL = [[P.sb(f"XL{i}_{k}", [128, 384], RW_DT) for k in range(4)] for i in range(2)]
    Tm = [[XL[i][k][:, 256:384] for k in range(4)] for i in range(2)]
    TmF = Tm
    Zs = P.sb("Zs", [128, 128], RW_DT)
    Us = P.sb("Us", [128, 128], RW_DT)
    P.op("pool", lambda: G.memset(Us[:], 0.0), writes=["Us0", "Us1"])
    P.op("pool", lambda: G.memset(Zs[:], 0.0), writes=["Zs0", "Zs1"])
    ysb = cl
    for k_ in ("ysb", "ysb0", "ysb1"):
        P.alias(k_, "cl")
    yao = [P.sb(f"yao{i}", [128, 512], BF16) for i in range(2)]

    def proj_group(j, col, M, rhs_tile, rhs_key):
        ps, pk = nextpp()
        for c in range(8):
            P.op("pe", lambda: PE.matmul(ps[0:M, :], lhsT=wsb[:, c, col:col + M], rhs=rhs_tile[:, c, :],
                                         start=(c == 0), stop=(c == 7)),
                 reads=["wsb", rhs_key], writes=[pk])
        return ps, pk

    def load_h(j):
        b = j % 2
        P.dma("sp", hTt[b][:], h_src(j), f"ld_h{b}", reads=(h_keys(j) if h_keys else ()), writes=[f"hT{b}"])

    def tile_proj(j):
        b = j % 2
        ht, hk = hTt[b], f"hT{b}"
        sga = sgaL[b]
        sgak = f"sga{b}"
        if do_rwkv:
            for gi, (g, col) in enumerate((("r", C_R), ("k", C_K), ("v", C_V), ("w", C_WA))):
                ps, pk = proj_group(j, col, 128, ht, hk)
                rw = raw[g]
                if j > 0:
                    P.op("act", lambda: A.copy(out=rw[:, 0:1], in_=rw[:, 512:513]), reads=[f"raw_{g}"],
                         writes=[f"raw_{g}c"])
                P.op("dve", lambda: V.tensor_copy(out=rw[:, 1:513], in_=ps[:, :]), reads=[pk], writes=[f"raw_{g}"])
                P.op("act", lambda: A.activation(out=tmp[:], in_=ps[:, :], func=AF.Identity,
                                                 scale=vec[:, 11 + gi:12 + gi]),
                     reads=[pk, "vec"], writes=["tmp"])
                P.op("dve", lambda: V.scalar_tensor_tensor(out=sh[g][:], in0=rw[:, 0:512], scalar=vec[:, gi:gi + 1],
                                                           in1=tmp[:], op0=ALU.mult, op1=ALU.add),
                     reads=[f"raw_{g}", f"raw_{g}c", "tmp", "vec"], writes=[f"sh_{g}"])
                yield
            ps, pk = proj_group(j, C_GA, 128, ht, hk)
            P.op("act", lambda: A.activation(out=sga[:], in_=ps[:, :], func=AF.Tanh, scale=0.5), reads=[pk],
                 writes=[sgak])
            P.op("dve", lambda: V.scalar_tensor_tensor(out=sga[:], in0=sga[:], scalar=1.0, in1=ps[:, :],
                                                       op0=ALU.add, op1=ALU.mult), reads=[pk, "sga"], writes=[sgak])
        if do_fox:
            ps, pk = proj_group(j, C_FQ, 128, ht, hk)
            for h in range(2):
                hp_ = slice(64 * h, 64 * h + 64)
                P.op("act", lambda: A.activation(out=QT[b][h][hp_, :], in_=ps[hp_, :], func=AF.Copy, scale=0.125),
                     reads=[pk], writes=[f"QT{b}"])
            yield
            ps, pk = proj_group(j, C_FK, 128, ht, hk)
            P.op("dve", lambda: V.tensor_copy(out=KT[:, j * 512:(j + 1) * 512], in_=ps[:, :]),
                 reads=[pk], writes=[f"KT{j}"])
            yield
            for h, col in ((0, C_GB0), (1, C_GB1)):
                ps, pk = proj_group(j, col, 64, ht, hk)
                P.op("act", lambda: A.activation(out=sgb[b][h][:], in_=ps[0:64, :], func=AF.Tanh, scale=0.5),
                     reads=[pk], writes=[f"sgb{b}_{h}"])
                P.op("dve", lambda: V.scalar_tensor_tensor(out=sgb[b][h][:], in0=sgb[b][h][:], scalar=1.0,
                                                           in1=ps[0:64, :], op0=ALU.add, op1=ALU.mult),
                     reads=[pk, f"sgb{b}_{h}"], writes=[f"sgb{b}_{h}"])
                yield
            ps, pk = nextpp()
            for q in range(4):
                for c in range(8):
                    P.op("pe", lambda: PE.matmul(ps[:, q * 128:(q + 1) * 128], lhsT=ht[:, c, q * 128:(q + 1) * 128],
                                                 rhs=wsb[:, c, C_FV:C_FV + 128], start=(c == 0), stop=(c == 7)),
                         reads=["wsb", hk], writes=[pk])
            P.op("dve", lambda: V.tensor_copy(
                out=Vaug[:, 4 * j:4 * j + 4, :, 0:64],
                in_=ps[:, :].rearrange("p (q h d) -> p q h d", q=4, h=2)),
                reads=[pk], writes=[f"V{j}"])
            yield
            ps, pk = proj_group(j, C_F, 2, ht, hk)
            P.op("act", lambda: A.activation(out=lf, in_=ps[0:2, :], func=AF.Tanh, bias=fbh[:, 0:1], scale=0.5),
                 reads=[pk, "fbh"], writes=["lf"])
            P.op("dve", lambda: V.tensor_scalar(out=lf, in0=lf, scalar1=0.5, scalar2=0.5, op0=ALU.mult, op1=ALU.add),
                 reads=["lf"], writes=["lf"])
            P.op("act", lambda: A.activation(out=lf, in_=lf, func=AF.Ln), reads=["lf"], writes=["lf"])
            cprev, cb = cum[1 - b], cum[b]
            P.op("dve", lambda: V.tensor_tensor_scan(out=cb[:], data0=onesrow, data1=lf,
                                                     initial=cprev[:, 511:512], op0=ALU.mult, op1=ALU.add),
                 reads=["lf", f"cum{1-b}", "onesrow"], writes=[f"cum{b}"])
            yield
            ps, pk = nextpp()
            for q in range(4):
                P.op("pe", lambda: PE.matmul(ps[:, 2 * q:2 * q + 2], lhsT=cb[0:2, q * 128:(q + 1) * 128],
                                             rhs=cst[0:2, K_ID:K_ID + 2], start=True, stop=True),
                     reads=[f"cum{b}", "cst"], writes=[pk])
            P.op("dve", lambda: V.tensor_scalar(out=i2[:], in0=cst[0:2, K_ID:K_ID + 2], scalar1=cb[:, 255:256],
                                                scalar2=None, op0=ALU.mult),
                 reads=[f"cum{b}", "cst"], writes=["i2"])
            P.op("pe", lambda: PE.matmul(ps[:, 8:10], lhsT=ones[0:2, :], rhs=i2[:, :], start=True, stop=True),
                 reads=["i2", "cst"], writes=[pk])
            P.op("dve", lambda: V.tensor_copy(out=ckres[:, 4 * j:4 * j + 4, :],
                                              in_=ps[:, 0:8].rearrange("p (q h) -> p q h", q=4)),
                 reads=[pk], writes=[f"ck{j}"])
            P.op("dve", lambda: V.tensor_copy(out=basebc[b][:], in_=ps[:, 8:10]), reads=[pk], writes=[f"basebc{b}"])
        yield

    P.op("dve", lambda: V.memset(onesrow, 1.0), writes=["onesrow"])

    def attn_tile(j):
        b = j % 2
        nb = 4 * j + 4
        steps = [(h, kb) for h in range(2) for kb in range(nb)]
        for h in range(2):
            bj = biasj[h]
            P.op("dve", lambda: V.tensor_scalar(out=bj[:, 0:nb, h], in0=ckres[:, 0:nb, h], scalar1=-1.0,
                                                scalar2=basebc[b][:, h:h + 1], op0=ALU.mult, op1=ALU.add),
                 reads=[f"ck{jj}" for jj in range(j + 1)] + [f"basebc{b}"], writes=[f"biasj{h}"])
        yield

        def q0_of(kb):
            m = kb - 4 * j
            return 128 * m if m > 0 else 0

        def emit_S(i):
            h, kb = steps[i]
            hp = slice(64 * h, 64 * h + 64)
            q0 = q0_of(kb)
            si = i % NSB
            P.op("pe", lambda: PE.matmul(psS[si][:, q0:512], lhsT=KT[:, kb * 128:(kb + 1) * 128],
                                         rhs=QT[b][h][:, q0:512], start=True, stop=True),
                 reads=[f"KT{kb // 4}", f"QT{b}"], writes=[f"psS{si}"])

        for i0 in range(min(LOOK, len(steps))):
            emit_S(i0)
        for i, (h, kb) in enumerate(steps):
            if i + LOOK < len(steps):
                emit_S(i + LOOK)
            bj = biasj[h]
            m = kb - 4 * j
            q0 = q0_of(kb)
            si = i % NSB
            pi = i % NPT
            P.op("act", lambda: A.activation(out=PT[pi][:, q0:512], in_=psS[si][:, q0:512], func=AF.Exp,
                                             bias=bj[:, kb, h:h + 1], scale=1.0),
                 reads=[f"psS{si}", f"biasj{h}"], writes=[f"PT{pi}"])
            if m >= 0:
                P.op("pool", lambda: G.tensor_tensor(out=PT[pi][:, q0:q0 + 128], in0=PT[pi][:, q0:q0 + 128],
                                                     in1=uib[:], op=ALU.mult),
                     reads=[f"PT{pi}", "uib"], writes=[f"PT{pi}"])
            P.op("pe", lambda: PE.matmul(psO[0:65, q0:512], lhsT=Vaug[:, kb, h, :], rhs=PT[pi][:, q0:512],
                                         start=(kb == 0), stop=(kb == nb - 1)),
                 reads=[f"PT{pi}", f"V{kb // 4}", "Vaug_ones"], writes=["psO"])
            if kb == nb - 1:
                P.op("dve", lambda: V.tensor_copy(out=osb[0:65, :], in_=psO[0:65, :]), reads=["psO"], writes=["osb"])
                P.op("dve", lambda: V.reciprocal(out=rrow[64:65, :], in_=osb[64:65, :]), reads=["osb"],
                     writes=["rrow"])
                pq, pqk = psS[si], f"psS{si}"
                P.op("pe", lambda: PE.matmul(pq[0:64, :], lhsT=ones[64:65, 0:64], rhs=rrow[64:65, :],
                                             start=True, stop=True),
                     reads=["rrow", "cst"], writes=[pqk])
                P.op("dve", lambda: V.scalar_tensor_tensor(out=t1[:], in0=pq[0:64, :], scalar=0.5, in1=sgb[b][h][:],
                                                           op0=ALU.mult, op1=ALU.mult),
                     reads=[pqk, f"sgb{b}_{h}"], writes=["t1"])
                P.op("pool", lambda: G.tensor_tensor(out=ybo[h][:], in0=osb[0:64, :], in1=t1[:], op=ALU.mult),
                     reads=["osb", "t1"], writes=[f"ybo{h}"])
                P.dma("sp", y_dst(128 + 64 * h, 192 + 64 * h, j), ybo[h][:], f"st_yb{h}",
                      reads=[f"ybo{h}"], writes=[f"yTb{h}", f"ytile{j}_b{h}"])
            yield

    def SLK(h, i0=0, i1=4):
        return [f"psA{h}"]

    psC = psY
    CK = ["psC"]

    progress = [0, 0]

    def rwkv_prep(j):
        for h_ in range(2):
            for c_ in range(4):
                ndone[h_][c_] = False
        pa, pak = psA[0], SLK(0)
        pb, pbk = psA[1], SLK(1)
        th = tmp
        P.op("act", lambda: A.activation(out=th[0:64, :], in_=sh["w"][0:64, :], func=AF.Tanh),
             reads=["sh_w"], writes=["tmp"])
        P.op("pe", lambda: PE.matmul(pa[:, :], lhsT=upt[0:64, :], rhs=th[0:64, :], start=True, stop=True),
             reads=["upt", "tmp"], writes=pak)
        P.op("pe", lambda: PE.matmul(pb[:, :], lhsT=upt[64:128, :], rhs=sh["w"][64:128, :], start=True, stop=True),
             reads=["upt", "sh_w"], writes=pbk)
        P.op("act", lambda: A.activation(out=ld[:], in_=pa[:, :], func=AF.Tanh, bias=vec[:, 16:17], scale=0.5),
             reads=pak + ["vec"], writes=["ld"])
        P.op("act", lambda: A.activation(out=a_t[:], in_=pb[:, :], func=AF.Tanh, bias=vec[:, 17:18], scale=0.5),
             reads=pbk + ["vec"], writes=["a_t"])
        yield
        P.op("dve", lambda: V.tensor_scalar(out=ld[:], in0=ld[:], scalar1=1.0, scalar2=-0.5 * float(np.exp(-0.5)),
                                            op0=ALU.add, op1=ALU.mult), reads=["ld"], writes=["ld"])
        P.op("pool", lambda: G.tensor_scalar(out=a_t[:], in0=a_t[:], scalar1=0.5, scalar2=0.5, op0=ALU.mult,
                                             op1=ALU.add), reads=["a_t"], writes=["a_t"])
        P.op("dve", lambda: V.tensor_tensor_scan(out=cl[:], data0=rsm, data1=ld[:], initial=0.0,
                                                 op0=ALU.mult, op1=ALU.add),
             reads=["ld", "cst"], writes=["cl"])
        P.op("act", lambda: A.activation(out=einc[:], in_=cl[:], func=AF.Exp), reads=["cl"], writes=["einc"])
        P.op("act", lambda: A.activation(out=eneg[:], in_=cl[:], func=AF.Exp, scale=-1.0),
             reads=["cl"], writes=["eneg"])
        P.op("dve", lambda: V.tensor_tensor(out=w1[:], in0=cl[:], in1=ld[:], op=ALU.subtract),
             reads=["cl", "ld"], writes=["w1"])
        P.op("act", lambda: A.activation(out=w1[:], in_=w1[:], func=AF.Exp), reads=["w1"], writes=["w1"])
        yield
        P.op("dve", lambda: V.tensor_scalar(out=kk[:], in0=sh["k"][:], scalar1=vec[:, 6:7], scalar2=None,
                                            op0=ALU.mult), reads=["sh_k", "vec"], writes=["kk"])
        P.op("dve", lambda: V.tensor_tensor(out=w2[:], in0=kk[:], in1=kk[:], op=ALU.mult),
             reads=["kk"], writes=["w2"])
        P.op("pe", lambda: PE.matmul(pa[:, :], lhsT=bones, rhs=w2[:], start=True, stop=True),
             reads=["cst", "w2"], writes=pak)
        P.op("dve", lambda: V.tensor_scalar(out=w2[:], in0=pa[:, :], scalar1=1e-24, scalar2=None, op0=ALU.max),
             reads=pak, writes=["w2"])
        P.op("act", lambda: A.activation(out=w2[:], in_=w2[:], func=AF.Ln), reads=["w2"], writes=["w2"])
        P.op("act", lambda: A.activation(out=w2[:], in_=w2[:], func=AF.Exp, scale=-0.5), reads=["w2"], writes=["w2"])
        P.op("dve", lambda: V.tensor_tensor(out=kk[:], in0=kk[:], in1=w2[:], op=ALU.mult),
             reads=["kk", "w2"], writes=["kk"])
        yield
        P.op("dve", lambda: V.tensor_scalar(out=w2[:], in0=a_t[:], scalar1=vec[:, 7:8], scalar2=vec[:, 15:16],
                                            op0=ALU.mult, op1=ALU.add), reads=["a_t", "vec"], writes=["w2"])
        P.op("dve", lambda: V.tensor_tensor(out=kmod[:], in0=sh["k"][:], in1=w2[:], op=ALU.mult),
             reads=["sh_k", "w2"], writes=["kmod"])
        P.op("dve", lambda: V.tensor_tensor(out=w2[:], in0=kk[:], in1=a_t[:], op=ALU.mult),
             reads=["kk", "a_t"], writes=["w2"])
        P.op("dve", lambda: V.tensor_tensor(out=Bt[:], in0=w2[:], in1=eneg[:], op=ALU.mult),
             reads=["w2", "eneg"], writes=["Bt"])
        P.op("pool", lambda: G.tensor_tensor(out=Kt[:], in0=kmod[:], in1=eneg[:], op=ALU.mult),
             reads=["kmod", "eneg"], writes=["Kt"])
        for hh in range(2):
            hq = slice(64 * hh, 64 * hh + 64)
            P.op("dve", lambda: V.tensor_tensor(out=KRz[hh][hq, :, 0, :],
                                                in0=kk[hq, :].rearrange("p (c t) -> p c t", c=4),
                                                in1=w1[hq, :].rearrange("p (c t) -> p c t", c=4), op=ALU.mult),
                 reads=["kk", "w1"], writes=["KR"])
            P.op("pool", lambda: G.tensor_tensor(out=KRz[hh][hq, :, 1, :],
                                                 in0=sh["r"][hq, :].rearrange("p (c t) -> p c t", c=4),
                                                 in1=einc[hq, :].rearrange("p (c t) -> p c t", c=4), op=ALU.mult),
                 reads=["sh_r", "einc"], writes=["KR"])
        yield
        P.op("dve", lambda: V.scalar_tensor_tensor(out=w2[:], in0=sh["r"][:], scalar=vec[:, 8:9], in1=kmod[:],
                                                   op0=ALU.mult, op1=ALU.mult),
             reads=["sh_r", "vec", "kmod"], writes=["w2"])
        P.op("pe", lambda: PE.matmul(pb[:, :], lhsT=bones, rhs=w2[:], start=True, stop=True),
             reads=["cst", "w2"], writes=pbk)
        P.op("dve", lambda: V.tensor_tensor(out=bonus[:], in0=pb[:, :], in1=sh["v"][:], op=ALU.mult),
             reads=pbk + ["sh_v"], writes=["bonus"])
        yield
        for (src, skey, dst, dkey, pq, pqk) in ((sh["v"], "sh_v", Vtok, "Vtok", pa, pak),
                                                (Bt, "Bt", Btok, "Btok", pb, pbk),
                                                (Kt, "Kt", Ktok, "Ktok", pa, pak)):
            isb = (src.dtype == BF16)
            pqv = pq[:, :].bitcast(BF16) if isb else pq[:, :]
            for c in range(4):
                P.op("pe", lambda: PE.transpose(pqv[:, c * 128:(c + 1) * 128], src[:, c * 128:(c + 1) * 128],
                                                idb[:] if isb else ident),
                     reads=[skey, "cst", "idb"], writes=pqk)
            P.op("dve", lambda: V.tensor_copy(out=dst[:].rearrange("p c t -> p (c t)"), in_=pqv[:, 0:512]),
                 reads=pqk, writes=[dkey])
            yield

    ndone = [[False] * 4, [False] * 4]

    def neumann(j, h, c):
        bi = h if NEU_BANKS == 2 else (2 * c + h) % NEU_BANKS
        pa = psA[bi]
        pk = SLK(bi)
        cs = slice(c * 128, (c + 1) * 128)
        at, atk = AT[h][c], f"AT{h}_{c}"
        ak, akk = AK[h][c], f"AK{h}_{c}"
        X, Xk = XL[h][c], f"TmF{h}_{c}"
        Lc, Ltc, Tc = X[:, 0:128], X[:, 128:256], X[:, 256:384]
        krc = KRz[h][:, c, :, :].rearrange("p a t -> p (a t)")
        P.op("pe", lambda: PE.matmul(pa[:, 0:256], lhsT=Bt[:, cs], rhs=krc, start=True, stop=True),
             reads=["Bt", "KR"], writes=pk)
        P.op("pe", lambda: PE.matmul(pa[:, 256:384], lhsT=KRz[h][:, c, 0, :], rhs=Bt[:, cs], start=True, stop=True),
             reads=["Bt", "KR"], writes=pk)
        P.op("dve", lambda: V.tensor_tensor(out=at[:], in0=pa[:, 0:256], in1=mask2, op=ALU.mult),
             reads=pk + ["cst"], writes=[atk])
        P.op("dve", lambda: V.tensor_tensor(out=Lc, in0=pa[:, 256:384], in1=msl, op=ALU.mult),
             reads=pk + ["cst"], writes=[Xk])
        P.op("pool", lambda: G.tensor_tensor(out=Tc, in0=ident, in1=at[:, 0:128], op=ALU.subtract),
             reads=[atk, "cst", Xk], writes=[Xk])
        yield
        P.op("pe", lambda: PE.matmul(pa[:, 0:128], lhsT=at[:, 0:128], rhs=Lc, start=True, stop=True),
             reads=[atk, Xk], writes=pk)
        P.op("pe", lambda: PE.matmul(pa[:, 128:256], lhsT=Lc, rhs=at[:, 0:128], start=True, stop=True),
             reads=[atk, Xk], writes=pk)
        P.op("pe", lambda: PE.matmul(pa[:, 256:512], lhsT=Kt[:, cs], rhs=krc, start=True, stop=True),
             reads=["Kt", "KR"], writes=pk)
        P.op("dve", lambda: V.tensor_copy(out=X[:, 0:256], in_=pa[:, 0:256]), reads=pk + [Xk], writes=[Xk])
        P.op("dve", lambda: V.tensor_tensor(out=ak[:], in0=pa[:, 256:512], in1=mask2, op=ALU.mult),
             reads=pk + ["cst"], writes=[akk])
        yield
        for k in range(1, 7):
            P.op("pe", lambda: PE.matmul(pa[:, 256:384], lhsT=Lc, rhs=Tc, start=True, stop=False),
                 reads=[Xk], writes=pk)
            P.op("pe", lambda: PE.matmul(pa[:, 256:384], lhsT=idb[:], rhs=Tc, start=False, stop=True),
                 reads=[Xk, "idb"], writes=pk)
            if k < 6:
                P.op("pe", lambda: PE.matmul(pa[:, 0:128], lhsT=Ltc, rhs=Lc, start=True, stop=True),
                     reads=[Xk], writes=pk)
            if k < 5:
                P.op("pe", lambda: PE.matmul(pa[:, 128:256], lhsT=Lc, rhs=Ltc, start=True, stop=True),
                     reads=[Xk], writes=pk)
            lo = 0 if k < 6 else 256
            P.op("dve", lambda: V.tensor_copy(out=X[:, lo:384], in_=pa[:, lo:384]), reads=pk + [Xk], writes=[Xk])
            yield
        ndone[h][c] = True

    def chain(j, h):
        pa = psC
        hp = slice(64 * h, 64 * h + 64)
        hc = slice(64 * h, 64 * h + 64)
        o0 = 256 * h
        stk, stbk, stwk, zk, uk = f"ST{h}", f"STb{h}", f"STw{h}", f"Zs{h}", f"Us{h}"
        for c in range(4):
            while not ndone[h][c]:
                yield
            par = c
            cs = slice(c * 128, (c + 1) * 128)
            wc = einc[hp, c * 128 + 127:c * 128 + 128]
            P.op("pe", lambda: PE.matmul(pa[:, o0:o0 + 64], lhsT=KRz[h][:, c, 0, :], rhs=STb[:, 0:64], start=True, stop=False),
                 reads=["KR", stbk], writes=CK)
            P.op("pe", lambda: PE.matmul(pa[:, o0:o0 + 64], lhsT=AK[h][par][:, 0:128], rhs=Vtok[:, c, hc],
                                         start=False, stop=True),
                 reads=[f"AK{h}_{par}", "Vtok"], writes=CK)
            P.op("dve", lambda: V.tensor_copy(out=Zs[:, hc], in_=pa[:, o0:o0 + 64]), reads=CK, writes=[zk])
            P.op("pool", lambda: G.tensor_scalar(out=STw[hp, :], in0=ST[hp, :], scalar1=wc, scalar2=None,
                                                 op0=ALU.mult), reads=[stk, "einc"], writes=[stwk])
            yield
            P.op("pe", lambda: PE.matmul(pa[:, o0 + 64:o0 + 128], lhsT=TmF[h][par], rhs=Zs[:, hc], start=True, stop=True),
                 reads=[f"TmF{h}_{par}", zk], writes=CK)
            P.op("dve", lambda: V.tensor_scalar(out=Us[:, hc], in0=pa[:, o0 + 64:o0 + 128], scalar1=-1.0,
                                                scalar2=None, op0=ALU.mult), reads=CK, writes=[uk])
            yield
            yo = pa[:, o0 + 128:o0 + 256]
            P.op("pe", lambda: PE.matmul(yo, lhsT=STb[:, :], rhs=KRz[h][:, c, 1, :], start=True, stop=False),
                 reads=[stbk, "KR"], writes=CK)
            P.op("pe", lambda: PE.matmul(yo, lhsT=Us[:, :], rhs=AT[h][par][:, 128:256], start=False, stop=False),
                 reads=[uk, f"AT{h}_{par}"], writes=CK)
            P.op("pe", lambda: PE.matmul(yo, lhsT=Vtok[:, c, :], rhs=AK[h][par][:, 128:256], start=False,
                                         stop=True), reads=["Vtok", f"AK{h}_{par}"], writes=CK)
            P.op("dve", lambda: V.tensor_copy(out=ysb[hp, cs], in_=pa[hp, o0 + 128:o0 + 256]), reads=CK,
                 writes=[f"ysb{h}"])
            P.op("pe", lambda: PE.matmul(pa[:, o0:o0 + 64], lhsT=Btok[:, c, :], rhs=Us[:, hc], start=True, stop=False),
                 reads=["Btok", uk], writes=CK)
            P.op("pe", lambda: PE.matmul(pa[:, o0:o0 + 64], lhsT=Ktok[:, c, :], rhs=Vtok[:, c, hc], start=False, stop=True),
                 reads=["Ktok", "Vtok"], writes=CK)
            P.op("dve", lambda: V.scalar_tensor_tensor(out=ST[hp, :], in0=pa[hp, o0:o0 + 64], scalar=wc, in1=STw[hp, :],
                                                       op0=ALU.mult, op1=ALU.add),
                 reads=CK + ["einc", stwk], writes=[stk])
            P.op("pool", lambda: G.tensor_copy(out=STb[hp, 0:64], in_=ST[hp, :]), reads=[stk], writes=[stbk])
            P.op("dve", lambda: V.tensor_copy(out=STb[hp, 64:128], in_=ST[hp, :]), reads=[stk], writes=[stbk])
            yield

    def rwkv_fin(j):
        b = j % 2
        pa, pak = psA[0], SLK(0)
        pb, pbk = psA[1], SLK(1)
        P.op("pool", lambda: G.tensor_tensor(out=w2[:], in0=ysb[:], in1=ysb[:], op=ALU.mult),
             reads=["ysb0", "ysb1", "ysb"], writes=["w2"])
        yield
        P.op("pe", lambda: PE.matmul(pa[:, :], lhsT=bones, rhs=ysb[:], start=True, stop=True),
             reads=["cst", "ysb0", "ysb1", "ysb"], writes=pak)
        P.op("pe", lambda: PE.matmul(pb[:, :], lhsT=bones, rhs=w2[:], start=True, stop=True),
             reads=["cst", "w2"], writes=pbk)
        P.op("act", lambda: A.activation(out=w1[:], in_=pa[:, :], func=AF.Square, scale=1.0 / 64), reads=pak,
             writes=["w1"])
        P.op("dve", lambda: V.scalar_tensor_tensor(out=ysb[:], in0=pa[:, :], scalar=-1.0 / 64, in1=ysb[:],
                                                   op0=ALU.mult, op1=ALU.add), reads=pak + ["ysb", "ysb0", "ysb1"],
             writes=["ysb", "ysb0", "ysb1"])
        P.op("dve", lambda: V.scalar_tensor_tensor(out=w1[:], in0=pb[:, :], scalar=1.0 / 64, in1=w1[:],
                                                   op0=ALU.mult, op1=ALU.subtract), reads=pbk + ["w1"], writes=["w1"])
        yield
        P.op("dve", lambda: V.tensor_scalar(out=w1[:], in0=w1[:], scalar1=GN_EPS, scalar2=None, op0=ALU.add),
             reads=["w1"], writes=["w1"])
        P.op("act", lambda: A.activation(out=w1[:], in_=w1[:], func=AF.Ln), reads=["w1"], writes=["w1"])
        P.op("act", lambda: A.activation(out=w1[:], in_=w1[:], func=AF.Exp, scale=-0.5), reads=["w1"], writes=["w1"])
        yield
        P.op("pool", lambda: G.tensor_tensor(out=ysb[:], in0=ysb[:], in1=w1[:], op=ALU.mult),
             reads=["ysb", "w1"], writes=["ysb"])
        P.op("dve", lambda: V.tensor_scalar(out=ysb[:], in0=ysb[:], scalar1=vec[:, 9:10], scalar2=vec[:, 10:11],
                                            op0=ALU.mult, op1=ALU.add), reads=["ysb", "vec"], writes=["ysb"])
        yield
        P.op("pool", lambda: G.tensor_tensor(out=ysb[:], in0=ysb[:], in1=bonus[:], op=ALU.add),
             reads=["ysb", "bonus"], writes=["ysb"])
        P.op("dve", lambda: V.scalar_tensor_tensor(out=yao[b][:], in0=ysb[:], scalar=0.5, in1=sgaL[b][:], op0=ALU.mult,
                                                   op1=ALU.mult), reads=["ysb", f"sga{b}"], writes=[f"yao{b}"])
        P.dma("sp", y_dst(0, 128, j), yao[b][:], f"st_ya{b}", reads=[f"yao{b}"], writes=[f"yTa{b}", f"ytile{j}_a"])
        yield

    def run_stage(prims, bg, ratio):
        alive = list(prims)
        acc = 0.0
        while alive:
            for g in list(alive):
                try:
                    next(g)
                except StopIteration:
                    alive.remove(g)
            if bg[0] is not None:
                acc += ratio
                while acc >= 1.0 and bg[0] is not None:
                    acc -= 1.0
                    try:
                        next(bg[0])
                    except StopIteration:
                        bg[0] = None

    def run_tile(j, nxt):
        bg = [attn_tile(j) if do_fox else None]
        n_attn = 2 * (4 * j + 4) + 1
        drain = bool(nxt) and DRAIN_OVERLAP and n_attn >= DRAIN_MIN
        r = n_attn / (RW_ROUNDS + (14.0 if drain else 0.0))
        if do_rwkv:
            run_stage([rwkv_prep(j)], bg, r)
            run_stage([neumann(j, h_, c_) for c_ in range(4) for h_ in range(2)] + [chain(j, 0), chain(j, 1)] + ([nxt] if (nxt and OVERLAP_PROJ) else []), bg, r)
            run_stage([rwkv_fin(j)] + ([nxt] if (nxt and FIN_OVERLAP) else []), bg, r)
            if drain:
                run_stage([nxt], bg, r)
        elif nxt:
            run_stage([nxt], bg, r)
        while bg[0] is not None:
            try:
                next(bg[0])
            except StopIteration:
                bg[0] = None

    load_h(0)
    if NT > 1:
        load_h(1)
    for _ in tile_proj(0):
        pass
    for j in range(NT):
        nxt = tile_proj(j + 1) if j + 1 < NT else None
        if j + 2 < NT:
            load_h(j + 2)
        if OVERLAP_PROJ:
            run_tile(j, nxt)
        else:
            run_tile(j, nxt if (FIN_OVERLAP or DRAIN_OVERLAP) else None)
            if nxt:
                for _ in nxt:
                    pass
        if after_tile is not None:
            after_tile(j)
    P.wait_all("sp", ["yTa0", "yTa1", "yTb0", "yTb1"])


def build_mix(T, **kw):
    nc = bass.Bass("TRN2", target_bir_lowering=False)
    hT = nc.dram_tensor("hT", [1024, T], BF16, kind="ExternalInput").ap()
    wcore = nc.dram_tensor("wcore", [1024, NCOL], F32, kind="ExternalInput").ap()
    vecs = nc.dram_tensor("vecs", [128, 11], F32, kind="ExternalInput").ap()
    up = nc.dram_tensor("up", [128, 128], F32, kind="ExternalInput").ap()
    fbf = nc.dram_tensor("fbf", [2, 1], F32, kind="ExternalInput").ap()
    consts = nc.dram_tensor("consts", [128, NCONST], F32, kind="ExternalInput").ap()
    yT = nc.dram_tensor("yT", [256, T], BF16, kind="ExternalOutput").ap()
    P = Prog(nc)
    emit_mix(P, nc, T, hT, wcore, vecs, up, fbf, consts, yT, **kw)
    print("mix ops", P.n_ops, "waits", P.n_waits)
    P.close()
    return nc

import numpy as np

LN_EPS = 1e-5
ALPHA = float(4 ** 0.25)
K_ID, K_ONE = 0, 1152


def emit_tok(P, nc, NTOK, consts, c_in, *, x_in=None, embg=None, embb=None,
             yT=None, xprev=None, wout=None, wada_g=None, bada_g=None, lng=None, lnb=None,
             wada_f=None, bada_f=None, x_out=None, hT_out=None, pfx="t",
             y_src=None, y_keys=(), h_dst=None, after_h=None):
    do_back = (yT is not None) or (y_src is not None)
    do_front = (hT_out is not None) or (h_dst is not None)
    if do_back and y_src is None:
        y_src = lambda s: yT.rearrange("(c p) t -> p c t", p=128)[:, :, s * 512:(s + 1) * 512]
    if do_front and h_dst is None:
        h_dst = lambda s: hT_out.rearrange("(c p) t -> p c t", p=128)[:, :, s * 512:(s + 1) * 512]
    NS = NTOK // 512
    V, A, G, PE = nc.vector, nc.scalar, nc.gpsimd, nc.tensor
    k = lambda s: pfx + s

    cst = P.sb(k("cst"), [128, 1280], F32)
    P.dma("sp", cst[:], consts[:, :], k("ld_cst"), writes=[k("cst")])
    ident = cst[:, K_ID:K_ID + 128]
    ones = cst[:, K_ONE:K_ONE + 128]
    c_sb = P.sb(k("c_sb"), [128, 8], F32)
    P.dma("sp", c_sb[:], c_in.rearrange("(c p) -> p c", p=128), k("ld_c"), writes=[k("c_sb")],
          allow_slow_non_contiguous=True)
    cbc = P.sb(k("cbc"), [128, 8, 128], F32)
    for c in range(8):
        P.op("dve", lambda: V.tensor_scalar(out=cbc[:, c, :], in0=ones, scalar1=c_sb[:, c:c + 1], scalar2=None,
                                            op0=ALU.mult), reads=[k("cst"), k("c_sb")], writes=[k("cbc")])
    pp = [P.ps(k(f"pp{i}"), [128, 512]) for i in range(4)]
    ppc = [0]

    def nextpp():
        i = ppc[0] % 4
        ppc[0] += 1
        return pp[i], k(f"pp{i}")

    wad = [P.sb(k(f"wad{i}"), [128, 8, 512], F32) for i in range(2)]
    wadc = [0]

    def mod_bc(wada, bada, N, name):
        mt = P.sb(k(name), [128, N], F32)
        P.dma("sp", mt[:], bada.partition_broadcast(128), k("ld_" + name), writes=[k(name)])
        wv = wada.rearrange("(c p) n -> p c n", p=128)
        for n0 in range(0, N, 512):
            b = wadc[0] % 2
            wadc[0] += 1
            P.dma("sp", wad[b][:], wv[:, :, n0:n0 + 512], k(f"ld_wad{b}"), writes=[k(f"wad{b}")])
            ps, pk = nextpp()
            for c in range(8):
                P.op("pe", lambda: PE.matmul(ps[:, :], lhsT=cbc[:, c, :], rhs=wad[b][:, c, :], start=(c == 0),
                                             stop=(c == 7)), reads=[k("cbc"), k(f"wad{b}")], writes=[pk])
            P.op("dve", lambda: V.tensor_tensor(out=mt[:, n0:n0 + 512], in0=ps[:, :], in1=mt[:, n0:n0 + 512],
                                                op=ALU.add), reads=[pk, k(name)], writes=[k(name)])
        return mt

    def bc_load(vec_ap, name):
        t = P.sb(k(name), [128, 1024], F32)
        P.dma("sp", t[:], vec_ap.partition_broadcast(128), k("ld_" + name), writes=[k(name)])
        return t

    if do_back:
        mg = mod_bc(wada_g, bada_g, 1024, "mg")
        P.op("dve", lambda: V.tensor_scalar(out=mg[:], in0=mg[:], scalar1=1.0, scalar2=None, op0=ALU.add),
             reads=[k("mg")], writes=[k("mg")])
        wo = P.sb(k("wo"), [128, 8, 1024], BF16)
        wov = wout.rearrange("(c p) n -> p c n", p=128)
        for pi, n0 in enumerate(range(0, 1024, 512)):
            b = wadc[0] % 2
            wadc[0] += 1
            P.dma("sp", wad[b][:], wov[:, :, n0:n0 + 512], k(f"ld_wad{b}"), writes=[k(f"wad{b}")])
            for c in range(8):
                e = "dve" if c % 2 == 0 else "pool"
                eng = V if c % 2 == 0 else G
                P.op(e, lambda: eng.tensor_tensor(out=wo[:, c, n0:n0 + 512], in0=wad[b][:, c, :],
                                                  in1=mg[:, n0:n0 + 512], op=ALU.mult),
                     reads=[k(f"wad{b}"), k("mg")], writes=[k("wo")])
        g_bc = bc_load(lng, "lng")
        b_bc = bc_load(lnb, "lnb")
    else:
        g_bc = bc_load(embg, "embg")
        b_bc = bc_load(embb, "embb")
    if do_front:
        mf = mod_bc(wada_f, bada_f, 2048, "mf")
        P.op("dve", lambda: V.tensor_scalar(out=mf[:, 1024:2048], in0=mf[:, 1024:2048], scalar1=1.0, scalar2=None,
                                            op0=ALU.add), reads=[k("mf")], writes=[k("mf")])
        fm = P.sb(k("fm"), [128, 16], F32)
        for q in range(4):
            ps, pk = nextpp()
            for u in range(4):
                cc = q * 4 + u
                P.op("pe", lambda: PE.transpose(ps[:, u * 128:(u + 1) * 128], mf[:, cc * 128:(cc + 1) * 128], ident),
                     reads=[k("mf"), k("cst")], writes=[pk])
            P.op("dve", lambda: V.tensor_copy(out=fm[:, q * 4:q * 4 + 4],
                                              in_=ps[:, :].rearrange("p (c t) -> p c t", t=128)[:, :, 0]),
                 reads=[pk], writes=[k("fm")])

    NXB = 3
    xt = [P.sb(k(f"xt{i}"), [128, 4, 1024], F32) for i in range(NXB)]
    yt = [P.sb(k(f"yt{i}"), [128, 8, 512], BF16) for i in range(NXB)] if do_back else None
    ht = [P.sb(k(f"ht{i}"), [128, 8, 512], BF16) for i in range(2)] if do_front else None
    xsrc = xprev if do_back else x_in

    def load(s):
        b = s % NXB
        P.dma("sp", xt[b][:], xsrc[s * 512:(s + 1) * 512, :].rearrange("(u p) d -> p u d", p=128),
              k(f"ld_x{b}"), writes=[k(f"xt{b}")] + [k(f"xt{b}_{u}") for u in range(4)])
        if do_back:
            P.dma("sp", yt[b][:], y_src(s), k(f"ld_y{b}"), reads=y_keys, writes=[k(f"yt{b}")])

    stats4 = [P.sb(k(f"stats4_{i}"), [128, 4, 2, 6], F32) for i in range(2)]
    mv4 = [P.sb(k(f"mv4_{i}"), [128, 4, 2], F32) for i in range(2)]
    rs4 = [P.sb(k(f"rs4_{i}"), [128, 4], F32) for i in range(2)]
    nb4 = [P.sb(k(f"nb4_{i}"), [128, 4], F32) for i in range(2)]
    load(0)
    if NS > 1:
        load(1)
    for s in range(NS):
        b = s % 2
        bx = s % NXB
        if s + 2 < NS:
            load(s + 2)
        xk = k(f"xt{bx}")
        xku = [k(f"xt{bx}_{u}") for u in range(4)]
        for u in range(4):
            xs = xt[bx][:, u, :]
            if do_back:
                for n in range(2):
                    ps, pk = nextpp()
                    for c in range(8):
                        P.op("pe", lambda: PE.matmul(ps[:, :], lhsT=yt[bx][:, c, u * 128:(u + 1) * 128],
                                                     rhs=wo[:, c, n * 512:(n + 1) * 512], start=(c == 0),
                                                     stop=(c == 7)), reads=[k(f"yt{bx}"), k("wo")], writes=[pk])
                    P.op("dve", lambda: V.scalar_tensor_tensor(out=xs[:, n * 512:(n + 1) * 512],
                                                               in0=xs[:, n * 512:(n + 1) * 512], scalar=ALPHA,
                                                               in1=ps[:, :], op0=ALU.mult, op1=ALU.add),
                         reads=[xk, xku[u], pk], writes=[xku[u]])
            for n in range(2):
                P.op("dve", lambda: V.bn_stats(out=stats4[b][:, u, n, :], in_=xs[:, n * 512:(n + 1) * 512]),
                     reads=[xk, xku[u]], writes=[k(f"st4_{b}_{u}")])
            P.op("dve", lambda: V.bn_aggr(out=mv4[b][:, u, :], in_=stats4[b][:, u, :, :].rearrange("p a b -> p (a b)")),
                 reads=[k(f"st4_{b}_{u}")], writes=[k(f"mv4_{b}")])
        P.op("dve", lambda: V.tensor_scalar(out=rs4[b][:], in0=mv4[b][:, :, 1], scalar1=LN_EPS, scalar2=None,
                                            op0=ALU.add), reads=[k(f"mv4_{b}")], writes=[k(f"rs4_{b}")])
        P.op("act", lambda: A.activation(out=rs4[b][:], in_=rs4[b][:], func=AF.Sqrt), reads=[k(f"rs4_{b}")],
             writes=[k(f"rs4_{b}")])
        P.op("dve", lambda: V.reciprocal(out=rs4[b][:], in_=rs4[b][:]), reads=[k(f"rs4_{b}")], writes=[k(f"rs4_{b}")])
        P.op("dve", lambda: V.scalar_tensor_tensor(out=nb4[b][:], in0=mv4[b][:, :, 0], scalar=-1.0, in1=rs4[b][:],
                                                   op0=ALU.mult, op1=ALU.mult),
             reads=[k(f"mv4_{b}"), k(f"rs4_{b}")], writes=[k(f"nb4_{b}")])
        for u in range(4):
            xs = xt[bx][:, u, :]
            P.op("act", lambda: A.activation(out=xs, in_=xs, func=AF.Identity, scale=rs4[b][:, u:u + 1],
                                             bias=nb4[b][:, u:u + 1]),
                 reads=[xk, xku[u], k(f"rs4_{b}"), k(f"nb4_{b}")], writes=[xku[u]])
            P.op("dve", lambda: V.tensor_tensor(out=xs, in0=xs, in1=g_bc[:], op=ALU.mult),
                 reads=[xku[u], k("lng"), k("embg")], writes=[xku[u]])
            P.op("pool", lambda: G.tensor_tensor(out=xs, in0=xs, in1=b_bc[:], op=ALU.add),
                 reads=[xku[u], k("lnb"), k("embb")], writes=[xku[u]])
        if x_out is not None:
            P.dma("sp", x_out[s * 512:(s + 1) * 512, :].rearrange("(u p) d -> p u d", p=128), xt[bx][:],
                  k(f"st_x{bx}"), reads=[xk] + xku, writes=[k(f"xo{bx}")])
        if do_front:
            hk = k(f"ht{b}")
            for c in range(8):
                ps, pk = nextpp()
                for u in range(4):
                    P.op("pe", lambda: PE.transpose(ps[:, u * 128:(u + 1) * 128], xt[bx][:, u, c * 128:(c + 1) * 128],
                                                    ident), reads=[xk, xku[u], k("cst")], writes=[pk])
                P.op("act", lambda: A.activation(out=ht[b][:, c, :], in_=ps[:, :], func=AF.Identity,
                                                 scale=fm[:, 8 + c:9 + c], bias=fm[:, c:c + 1]),
                     reads=[pk, k("fm")], writes=[hk])
            P.dma("sp", h_dst(s), ht[b][:], k(f"st_h{b}"), reads=[hk], writes=[k(f"ho{b}"), f"htile{s}"])
            if after_h is not None:
                after_h(s)
    P.wait_all("sp", [k("xo0"), k("xo1"), k("xo2"), k("ho0"), k("ho1")])


def build_tok(NTOK, mode):
    nc = bass.Bass("TRN2", target_bir_lowering=False)
    dt = lambda n, s, d=F32, kind="ExternalInput": nc.dram_tensor(n, s, d, kind=kind).ap()
    consts = dt("consts", [128, 1280])
    c_in = dt("c", [1024])
    kw = {}
    if mode == "pre":
        kw.update(x_in=dt("x", [NTOK, 1024]), embg=dt("embg", [1024]), embb=dt("embb", [1024]))
    else:
        kw.update(yT=dt("yT", [1024, NTOK], BF16), xprev=dt("xprev", [NTOK, 1024]), wout=dt("wout", [1024, 1024]),
                  wada_g=dt("wada_g", [1024, 1024]), bada_g=dt("bada_g", [1024]), lng=dt("lng", [1024]),
                  lnb=dt("lnb", [1024]))
    if mode != "post":
        kw.update(wada_f=dt("wada_f", [1024, 2048]), bada_f=dt("bada_f", [2048]),
                  hT_out=dt("hT", [1024, NTOK], BF16, kind="ExternalOutput"))
    kw.update(x_out=dt("xo", [NTOK, 1024], kind="ExternalOutput"))
    P = Prog(nc)
    emit_tok(P, nc, NTOK, consts, c_in, **kw)
    print("tok", mode, "ops", P.n_ops, "waits", P.n_waits)
    P.close()
    return nc

import numpy as np

D = 1024
RWW = 512
RW_R0, RW_K0, RW_V0, RW_WD0, RW_AD0, RW_END = 0, 512, 1024, 1536, 1600, 1664
FX_Q0, FX_K0, FX_V0, FX_F0, FX_END = 1664, 2176, 2688, 3200, 3208
GATE0 = 3208


def core_cols(g):
    ch = np.arange(128 * g, 128 * g + 128)
    l64 = np.arange(64)
    return np.concatenate([
        RW_R0 + ch, RW_K0 + ch, RW_V0 + ch, RW_WD0 + l64, RW_AD0 + l64, GATE0 + ch,
        FX_Q0 + ch, FX_K0 + ch,
        GATE0 + 512 + 128 * g + l64, GATE0 + 512 + 128 * g + 64 + l64,
        FX_V0 + ch, FX_F0 + np.array([2 * g, 2 * g + 1])])


def pack_core(inp, l, g):
    ch = np.arange(128 * g, 128 * g + 128)
    l64 = np.arange(64)
    wcore = np.ascontiguousarray(inp["w_in"][l][:, core_cols(g)])
    mix = inp["rwkv_mix"][l]
    vecs = np.stack([
        mix[RW_R0 + ch], mix[RW_K0 + ch], mix[RW_V0 + ch],
        np.concatenate([mix[RW_WD0 + l64], mix[RW_AD0 + l64]]),
        inp["w0"][l][ch], inp["a0"][l][ch], inp["k_k"][l][ch], inp["k_a"][l][ch],
        inp["r_k"][l][ch], inp["gn_g"][l][ch], inp["gn_b"][l][ch]], axis=1).astype(np.float32)
    up = np.concatenate([inp["w_up"][l][:, ch], inp["a_up"][l][:, ch]], axis=0).astype(np.float32)
    fbf = inp["fox_bf"][l][[2 * g, 2 * g + 1]][:, None].astype(np.float32)
    return dict(wcore=wcore, vecs=np.ascontiguousarray(vecs), up=np.ascontiguousarray(up),
                fbf=np.ascontiguousarray(fbf))


T_SEQ = 16384
NTOK = 4096
RG = [[0, 1, 2, 3], [4, 5, 6, 7]]
_NC_CACHE = {}


def build_fused(T=T_SEQ, NT=NTOK):
    nc = bass.Bass("TRN2", target_bir_lowering=False)
    dt = lambda n, s, d=F32, kind="ExternalInput": nc.dram_tensor(n, s, d, kind=kind).ap()
    NS = NT // 512
    NK = T // 2048
    consts = dt("consts", [128, 1280])
    c_in = dt("c", [1024])
    x_in = dt("x", [NT, 1024])
    embg = dt("embg", [1024])
    embb = dt("embb", [1024])
    L = []
    for l in range(2):
        L.append(dict(
            wada_f=dt(f"wada_f{l}", [1024, 2048]), bada_f=dt(f"bada_f{l}", [2048]),
            wada_g=dt(f"wada_g{l}", [1024, 1024]), bada_g=dt(f"bada_g{l}", [1024]),
            wout=dt(f"wout{l}", [1024, 1024]), lng=dt(f"lng{l}", [1024]), lnb=dt(f"lnb{l}", [1024]),
            wcore=dt(f"wcore{l}", [1024, NCOL]), vecs=dt(f"vecs{l}", [128, 11]), up=dt(f"up{l}", [128, 128]),
            fbf=dt(f"fbf{l}", [2, 1])))
    xo = dt("xo", [NT, 1024], kind="ExternalOutput")
    xd = [nc.dram_tensor(f"xd{l}", [NT, 1024], F32).ap() for l in range(2)]
    hloc = [nc.dram_tensor(f"hloc{l}", [NS, 1024, 512], BF16).ap() for l in range(2)]
    hg = [nc.dram_tensor(f"hg{l}", [NS, 4, 1024, 512], BF16).ap() for l in range(2)]
    yc = [nc.dram_tensor(f"yc{l}", [NK, 256, 2048], BF16).ap() for l in range(2)]
    yg = [nc.dram_tensor(f"yg{l}", [NK, 4, 256, 2048], BF16).ap() for l in range(2)]
    q = nc.partition_id() % 4
    P = Prog(nc)

    def front_hooks(l):
        h_dst = lambda s: hloc[l][s].rearrange("(c p) t -> p c t", p=128)
        after_h = lambda s: P.cc("AllGather", hloc[l][s].opt(), hg[l][s].opt(), RG,
                                 reads=[f"htile{s}"], writes=[f"hg{s}"])
        return dict(h_dst=h_dst, after_h=after_h, wada_f=L[l]["wada_f"], bada_f=L[l]["bada_f"])

    P.begin_scope()
    emit_tok(P, nc, NT, consts, c_in, x_in=x_in, embg=embg, embb=embb, x_out=xd[0], pfx="a", **front_hooks(0))
    P.end_scope()
    for l in range(2):
        P.begin_scope()

        def after_tile(j, l=l):
            if j % 4 == 3:
                kk = j // 4
                keys = []
                for jj in range(j - 3, j + 1):
                    keys += [f"ytile{jj}_a", f"ytile{jj}_b0", f"ytile{jj}_b1"]
                P.cc("AllGather", yc[l][kk].opt(), yg[l][kk].opt(), RG, reads=keys, writes=[f"yg{kk}"])

        emit_mix(P, nc, T, None, L[l]["wcore"], L[l]["vecs"], L[l]["up"], L[l]["fbf"], consts, None,
                 h_src=lambda j, l=l: hg[l][j % NS, j // NS].rearrange("(c p) t -> p c t", p=128),
                 y_dst=lambda r0, r1, j, l=l: yc[l][j // 4, r0:r1, (j % 4) * 512:(j % 4 + 1) * 512],
                 after_tile=after_tile, h_keys=lambda j: [f"hg{j % NS}"])
        P.end_scope()
        P.begin_scope()

        def y_src(s, l=l):
            v = yg[l][bass.ds(2 * q + s // 4, 1)]
            return v.rearrange("o r (h p) t -> p (o r h) t", p=128)[:, :, (s % 4) * 512:(s % 4 + 1) * 512]

        kw = dict(y_src=y_src, y_keys=[f"yg{k_}" for k_ in range(NK)], xprev=xd[l], wout=L[l]["wout"], wada_g=L[l]["wada_g"], bada_g=L[l]["bada_g"],
                  lng=L[l]["lng"], lnb=L[l]["lnb"], pfx=f"b{l}")
        if l == 0:
            kw.update(front_hooks(1))
            kw.update(x_out=xd[1])
        else:
            kw.update(x_out=xo)
        emit_tok(P, nc, NT, consts, c_in, **kw)
        P.end_scope()
    print("fused ops", P.n_ops, "waits", P.n_waits)
    P.close()
    return nc


def wout_perm():
    idx = []
    for r in range(4):
        idx += list(range(128 * r, 128 * r + 128))
        idx += list(range(512 + 128 * r, 512 + 128 * r + 128))
    return np.array(idx)


def kernel(**inp):
    inp = {k: np.ascontiguousarray(np.asarray(v)) for k, v in inp.items()}
    consts = make_consts()
    ca = np.ascontiguousarray
    if "nc" not in _NC_CACHE:
        _NC_CACHE["nc"] = build_fused()
    nc = _NC_CACHE["nc"]
    perm = wout_perm()
    maps = []
    for core in range(8):
        b, g = core // 4, core % 4
        m = dict(consts=consts, c=ca(inp["c"][b]), x=ca(inp["x"][b][g * NTOK:(g + 1) * NTOK]),
                 embg=inp["emb_ln_g"], embb=inp["emb_ln_b"])
        for l in range(2):
            pk = pack_core(inp, l, g)
            m.update({f"wada_f{l}": ca(inp["w_ada"][l][:, 0:2048]), f"bada_f{l}": ca(inp["b_ada"][l][0:2048]),
                      f"wada_g{l}": ca(inp["w_ada"][l][:, 2048:3072]), f"bada_g{l}": ca(inp["b_ada"][l][2048:3072]),
                      f"wout{l}": ca(inp["w_out"][l][perm]), f"lng{l}": inp["ln_g"][l], f"lnb{l}": inp["ln_b"][l],
                      f"wcore{l}": pk["wcore"], f"vecs{l}": pk["vecs"], f"up{l}": pk["up"], f"fbf{l}": pk["fbf"]})
        maps.append(m)
    res = run_bass_kernel_spmd(nc, maps, core_ids=list(range(8))).results
    out = np.stack([np.concatenate([res[b * 4 + g]["xo"] for g in range(4)], axis=0) for b in range(2)], axis=0)
    return np.asarray(out, dtype=np.float32)
```
